# Optimizing a Trainium2 kernel written in Bass

```python
import math
import jax, jax.numpy as jnp
from jax import lax
import numpy as np

D_MODEL = 1024
BATCH = 4
SEQ = 8192
DEPTH = 4

HEAD_DIM = 64
NSA_HEADS = 8
NSA_GROUPS = 2
NSA_HPG = NSA_HEADS // NSA_GROUPS
CMP_LEN = 32
CMP_STRIDE = 16
CMP_HIDDEN = 256
SEL_BLK = 64
N_SEL = 16
WINDOW = 512
DSA_HEADS = 8
DSA_KV_RANK = 128
IDX_HEADS = 8
IDX_DIM = 64
DSA_TOPK = 256
DIFF_HEADS = 8
DIFF_DIM = 64
D_FF = 4 * D_MODEL
NUM_BUCKETS = 32
MAX_DISTANCE = 128
REL_HEADS = NSA_HEADS + DSA_HEADS
QBLK = 128
N_EVEN = (DEPTH + 1) // 2
N_ODD = DEPTH // 2
ALPHA = (2 * DEPTH) ** 0.25
BETA = (8 * DEPTH) ** -0.25
NEG_INF = -1e30
BIG = 1e9
EPS = 1e-5
NSA_Q = NSA_HEADS * HEAD_DIM
NSA_KV = 2 * NSA_GROUPS * HEAD_DIM
NSA_GATE = NSA_HEADS * 3
DSA_Q = DSA_HEADS * HEAD_DIM
EVEN_SIZES = (NSA_Q, NSA_KV, NSA_KV, NSA_KV, NSA_GATE, DSA_Q, DSA_KV_RANK, IDX_HEADS * IDX_DIM, IDX_DIM, IDX_HEADS)
EVEN_IN = sum(EVEN_SIZES)
EVEN_OUT = NSA_Q + DSA_HEADS * HEAD_DIM
ODD_IN = 3 * DIFF_HEADS * 2 * DIFF_DIM
ODD_OUT = DIFF_HEADS * 2 * DIFF_DIM

kernel_name = "hybrid_nsa_dsa_diff_deepnorm_trunk"


def _split(x, sizes):
    offs = np.cumsum(sizes)[:-1].tolist()
    return jnp.split(x, offs, axis=-1)


def layer_norm(x, g, b):
    xf = x.astype(jnp.float32)
    mu = jnp.mean(xf, axis=-1, keepdims=True)
    var = jnp.mean(jnp.square(xf - mu), axis=-1, keepdims=True)
    return ((xf - mu) * lax.rsqrt(var + EPS) * g + b).astype(x.dtype)


def rms_norm(x, g):
    xf = x.astype(jnp.float32)
    return (xf * lax.rsqrt(jnp.mean(jnp.square(xf), axis=-1, keepdims=True) + EPS) * g).astype(x.dtype)


def masked_softmax(logits, mask):
    z = jnp.where(mask, logits.astype(jnp.float32), NEG_INF)
    z = z - jnp.max(z, axis=-1, keepdims=True)
    e = jnp.where(mask, jnp.exp(z), 0.0)
    return e / jnp.maximum(jnp.sum(e, axis=-1, keepdims=True), 1e-30)


def t5_bucket(dist):
    n = jnp.maximum(dist, 0)
    exact = NUM_BUCKETS // 2
    nf = jnp.maximum(n, 1).astype(jnp.float32)
    large = exact + (jnp.log(nf / exact) / math.log(MAX_DISTANCE / exact) * (NUM_BUCKETS - exact)).astype(jnp.int32)
    return jnp.where(n < exact, n, jnp.minimum(large, NUM_BUCKETS - 1))


def sweep_query_blocks(fn, B, S):
    out = lax.map(fn, jnp.arange(S // QBLK))
    return jnp.moveaxis(out, 0, 1).reshape(B, S, -1)


def nsa_compress(k_raw, pe, w1, w2):
    B, S, G, Dh = k_raw.shape
    nc = (S - CMP_LEN) // CMP_STRIDE + 1
    idx = np.arange(nc)[:, None] * CMP_STRIDE + np.arange(CMP_LEN)[None, :]
    blk = k_raw[:, idx] + pe[:, None, :]
    blk = blk.transpose(0, 1, 3, 2, 4).reshape(B, nc, G, CMP_LEN * Dh)
    return jax.nn.gelu(blk @ w1) @ w2


def nsa_attention(q, kvc, kvs, kvw, gate_logits, pe_k, pe_v, w1_k, w2_k, w1_v, w2_v, bias_tab):
    B, S = q.shape[:2]
    G, HPG, Dh = NSA_GROUPS, NSA_HPG, HEAD_DIM
    q = q.reshape(B, S, G, HPG, Dh)
    kc_raw, vc_raw = [a.reshape(B, S, G, Dh) for a in jnp.split(kvc, 2, axis=-1)]
    kc = nsa_compress(kc_raw, pe_k, w1_k, w2_k)
    vc = nsa_compress(vc_raw, pe_v, w1_v, w2_v)
    ks, vs = [a.reshape(B, S, G, Dh).transpose(0, 2, 1, 3) for a in jnp.split(kvs, 2, axis=-1)]
    kw, vw = [jnp.pad(a.reshape(B, S, G, Dh), ((0, 0), (WINDOW, 0), (0, 0), (0, 0))) for a in jnp.split(kvw, 2, axis=-1)]
    gates = jax.nn.sigmoid(gate_logits.astype(jnp.float32)).astype(q.dtype).reshape(B, S, G, HPG, 3)
    nc = kc.shape[1]
    nb = S // SEL_BLK
    n_sel = min(N_SEL, nb)
    cs = np.arange(nc) * CMP_STRIDE
    ss = np.arange(nb) * SEL_BLK
    overlap = np.clip(np.minimum(cs[:, None] + CMP_LEN, ss[None, :] + SEL_BLK) - np.maximum(cs[:, None], ss[None, :]), 0, None)
    imp = jnp.asarray(overlap / CMP_STRIDE, dtype=jnp.float32)
    cmp_end = jnp.asarray(cs + CMP_LEN - 1, dtype=jnp.int32)
    tab = bias_tab.reshape(NUM_BUCKETS, G, HPG)
    scale = Dh ** -0.5
    bi = jnp.arange(B)[:, None, None, None]
    gi = jnp.arange(G)[None, :, None, None]
    jblk = jnp.arange(nb)

    def block(qb):
        q0 = qb * QBLK
        t = q0 + jnp.arange(QBLK)
        qq = lax.dynamic_slice_in_dim(q, q0, QBLK, axis=1)
        gg = lax.dynamic_slice_in_dim(gates, q0, QBLK, axis=1)
        dc = t[:, None] - cmp_end[None, :]
        bias_c = tab[t5_bucket(dc)].transpose(2, 3, 0, 1)
        lc = jnp.einsum('bqghd,bngd->bghqn', qq, kc) * scale + bias_c
        pc = masked_softmax(lc, dc >= 0)
        oc = jnp.einsum('bghqn,bngd->bqghd', pc.astype(vc.dtype), vc)
        score = jnp.einsum('bghqn,nj->bgqj', pc, imp)
        cb = (t // SEL_BLK)[:, None]
        allowed = jblk[None, :] * SEL_BLK <= t[:, None]
        forced = (jblk[None, :] == 0) | (jblk[None, :] == cb) | (jblk[None, :] == cb - 1)
        score = jnp.where(forced, BIG, jnp.where(allowed, score, -BIG))
        _, sel = lax.top_k(score, n_sel)
        tok = (sel[..., None] * SEL_BLK + jnp.arange(SEL_BLK)).reshape(B, G, QBLK, n_sel * SEL_BLK)
        k_sel = ks[bi, gi, tok]
        v_sel = vs[bi, gi, tok]
        dist = t[None, None, :, None] - tok
        bias_s = tab[t5_bucket(dist), gi].transpose(0, 1, 4, 2, 3)
        ls = jnp.einsum('bqghd,bgqtd->bghqt', qq, k_sel) * scale + bias_s
        ps = masked_softmax(ls, (dist >= 0)[:, :, None])
        o_s = jnp.einsum('bghqt,bgqtd->bqghd', ps.astype(v_sel.dtype), v_sel)
        kwb = lax.dynamic_slice_in_dim(kw, q0, WINDOW + QBLK, axis=1)
        vwb = lax.dynamic_slice_in_dim(vw, q0, WINDOW + QBLK, axis=1)
        s = q0 - WINDOW + jnp.arange(WINDOW + QBLK)
        dw = t[:, None] - s[None, :]
        mask_w = (dw >= 0) & (dw < WINDOW) & (s[None, :] >= 0)
        bias_w = tab[t5_bucket(dw)].transpose(2, 3, 0, 1)
        lw = jnp.einsum('bqghd,bsgd->bghqs', qq, kwb) * scale + bias_w
        pw = masked_softmax(lw, mask_w)
        o_w = jnp.einsum('bghqs,bsgd->bqghd', pw.astype(vwb.dtype), vwb)
        out = gg[..., 0:1] * oc + gg[..., 1:2] * o_s + gg[..., 2:3] * o_w
        return out.reshape(B, QBLK, G * HPG * Dh)

    return sweep_query_blocks(block, B, S)


def dsa_attention(q, kv_lat, iq, ik, iw, kv_norm, w_uk, w_uv, bias_tab):
    B, S = q.shape[:2]
    ckv = rms_norm(kv_lat, kv_norm)
    q_lat = jnp.einsum('bshd,rhd->bshr', q, w_uk)
    iq = iq.reshape(B, S, IDX_HEADS, IDX_DIM)
    iw = iw * (IDX_HEADS ** -0.5 * IDX_DIM ** -0.5)
    topk = min(DSA_TOPK, S // 4)
    scale = HEAD_DIM ** -0.5
    bi = jnp.arange(B)[:, None, None]
    spos = jnp.arange(S)

    def block(qb):
        q0 = qb * QBLK
        t = q0 + jnp.arange(QBLK)
        iqb = lax.dynamic_slice_in_dim(iq, q0, QBLK, axis=1)
        iwb = lax.dynamic_slice_in_dim(iw, q0, QBLK, axis=1)
        qlb = lax.dynamic_slice_in_dim(q_lat, q0, QBLK, axis=1)
        rel = jax.nn.relu(jnp.einsum('bqjd,bsd->bqjs', iqb, ik))
        isc = jnp.einsum('bqj,bqjs->bqs', iwb, rel).astype(jnp.float32)
        isc = jnp.where(spos[None, None, :] <= t[None, :, None], isc, NEG_INF)
        _, idx = lax.top_k(isc, topk)
        c_sel = ckv[bi, idx]
        dist = t[None, :, None] - idx
        bias = bias_tab[t5_bucket(dist)].transpose(0, 3, 1, 2)
        logits = jnp.einsum('bqhr,bqkr->bhqk', qlb, c_sel) * scale + bias
        p = masked_softmax(logits, (dist >= 0)[:, None])
        o_lat = jnp.einsum('bhqk,bqkr->bqhr', p.astype(c_sel.dtype), c_sel)
        o = jnp.einsum('bqhr,rhd->bqhd', o_lat, w_uv)
        return o.reshape(B, QBLK, DSA_HEADS * HEAD_DIM)

    return sweep_query_blocks(block, B, S)


def even_mixer(h, w_in, w_out, pe_k, pe_v, w1_k, w2_k, w1_v, w2_v, kv_norm, w_uk, w_uv, rel_bias):
    B, S, _ = h.shape
    nsa_q, kvc, kvs, kvw, gate, dsa_q, dsa_kv, idx_q, idx_k, idx_w = _split(h @ w_in, EVEN_SIZES)
    o_a = nsa_attention(nsa_q, kvc, kvs, kvw, gate, pe_k, pe_v, w1_k, w2_k, w1_v, w2_v, rel_bias[:, :NSA_HEADS])
    o_b = dsa_attention(dsa_q.reshape(B, S, DSA_HEADS, HEAD_DIM), dsa_kv, idx_q, idx_k, idx_w, kv_norm, w_uk, w_uv, rel_bias[:, NSA_HEADS:])
    return jnp.concatenate([o_a, o_b], axis=-1) @ w_out


def diff_mixer(h, w_in, w_out, lam, subln, rel_bias, lambda_init):
    B, S, _ = h.shape
    q, k, v = jnp.split(h @ w_in, 3, axis=-1)
    q = q.reshape(B, S, DIFF_HEADS, 2, DIFF_DIM)
    k = k.reshape(B, S, DIFF_HEADS, 2, DIFF_DIM)
    v = v.reshape(B, S, DIFF_HEADS, 2 * DIFF_DIM)
    lamf = lam.astype(jnp.float32)
    lam_full = jnp.exp(jnp.sum(lamf[0] * lamf[1])) - jnp.exp(jnp.sum(lamf[2] * lamf[3])) + lambda_init
    tab = rel_bias.reshape(NUM_BUCKETS, DIFF_HEADS, 2)
    scale = DIFF_DIM ** -0.5
    spos = jnp.arange(S)

    def block(qb):
        q0 = qb * QBLK
        t = q0 + jnp.arange(QBLK)
        qq = lax.dynamic_slice_in_dim(q, q0, QBLK, axis=1)
        dist = t[:, None] - spos[None, :]
        bias = tab[t5_bucket(dist)].transpose(2, 3, 0, 1)
        logits = jnp.einsum('bqhmd,bshmd->bhmqs', qq, k) * scale + bias
        p = masked_softmax(logits, dist >= 0)
        pd = p[:, :, 0] - lam_full * p[:, :, 1]
        o = jnp.einsum('bhqs,bshe->bqhe', pd.astype(v.dtype), v)
        o = rms_norm(o, subln) * (1.0 - lambda_init)
        return o.reshape(B, QBLK, DIFF_HEADS * 2 * DIFF_DIM)

    return sweep_query_blocks(block, B, S) @ w_out


def sq_relu_mlp(h, w1, w2):
    return jnp.square(jax.nn.relu(h @ w1)) @ w2


def modulate(x, shift, scale):
    return x * (1.0 + scale[:, None, :]) + shift[:, None, :]


def setup_inputs(seed: int = 0) -> dict:
    key = jax.random.key(seed)
    ks = jax.random.split(key, 26)
    f32 = jnp.float32

    def nrm(k, shape, fan_in, s=1.0):
        return jax.random.normal(k, shape, f32) * (s * fan_in ** -0.5)

    D = D_MODEL
    return {
        "x": jax.random.normal(ks[0], (BATCH, SEQ, D), f32),
        "c": jax.random.normal(ks[1], (BATCH, D), f32),
        "rel_bias": 0.5 * jax.random.normal(ks[2], (NUM_BUCKETS, REL_HEADS), f32),
        "ada_w": nrm(ks[3], (DEPTH, D, 6 * D), D, 0.2),
        "ada_b": 0.01 * jax.random.normal(ks[4], (DEPTH, 6 * D), f32),
        "ln_g": 1.0 + 0.02 * jax.random.normal(ks[5], (DEPTH, 2, D), f32),
        "ln_b": 0.02 * jax.random.normal(ks[6], (DEPTH, 2, D), f32),
        "ev_w_in": nrm(ks[7], (N_EVEN, D, EVEN_IN), D),
        "ev_w_out": nrm(ks[8], (N_EVEN, EVEN_OUT, D), EVEN_OUT, BETA),
        "nsa_pe_k": 0.5 * jax.random.normal(ks[9], (N_EVEN, CMP_LEN, HEAD_DIM), f32),
        "nsa_pe_v": 0.5 * jax.random.normal(ks[10], (N_EVEN, CMP_LEN, HEAD_DIM), f32),
        "nsa_w1_k": nrm(ks[11], (N_EVEN, CMP_LEN * HEAD_DIM, CMP_HIDDEN), CMP_LEN * HEAD_DIM),
        "nsa_w2_k": nrm(ks[12], (N_EVEN, CMP_HIDDEN, HEAD_DIM), CMP_HIDDEN),
        "nsa_w1_v": nrm(ks[13], (N_EVEN, CMP_LEN * HEAD_DIM, CMP_HIDDEN), CMP_LEN * HEAD_DIM),
        "nsa_w2_v": nrm(ks[14], (N_EVEN, CMP_HIDDEN, HEAD_DIM), CMP_HIDDEN),
        "dsa_kv_norm": 1.0 + 0.02 * jax.random.normal(ks[15], (N_EVEN, DSA_KV_RANK), f32),
        "dsa_w_uk": nrm(ks[16], (N_EVEN, DSA_KV_RANK, DSA_HEADS, HEAD_DIM), DSA_KV_RANK),
        "dsa_w_uv": nrm(ks[17], (N_EVEN, DSA_KV_RANK, DSA_HEADS, HEAD_DIM), DSA_KV_RANK),
        "od_w_in": nrm(ks[18], (N_ODD, D, ODD_IN), D),
        "od_w_out": nrm(ks[19], (N_ODD, ODD_OUT, D), ODD_OUT, BETA),
        "diff_lam": 0.1 * jax.random.normal(ks[20], (N_ODD, 4, DIFF_DIM), f32),
        "diff_subln": 1.0 + 0.02 * jax.random.normal(ks[21], (N_ODD, 2 * DIFF_DIM), f32),
        "mlp_w1": nrm(ks[22], (DEPTH, D, D_FF), D),
        "mlp_w2": nrm(ks[23], (DEPTH, D_FF, D), D_FF, BETA),
    }


def reference(x, c, rel_bias, ada_w, ada_b, ln_g, ln_b, ev_w_in, ev_w_out, nsa_pe_k, nsa_pe_v, nsa_w1_k, nsa_w2_k, nsa_w1_v, nsa_w2_v, dsa_kv_norm, dsa_w_uk, dsa_w_uv, od_w_in, od_w_out, diff_lam, diff_subln, mlp_w1, mlp_w2):
    c_act = jax.nn.silu(c)
    for l in range(DEPTH):
        ada = c_act @ ada_w[l] + ada_b[l]
        sh1, sc1, g1, sh2, sc2, g2 = jnp.split(ada, 6, axis=-1)
        h = modulate(x, sh1, sc1)
        i = l // 2
        if l % 2 == 0:
            y = even_mixer(h, ev_w_in[i], ev_w_out[i], nsa_pe_k[i], nsa_pe_v[i], nsa_w1_k[i], nsa_w2_k[i], nsa_w1_v[i], nsa_w2_v[i], dsa_kv_norm[i], dsa_w_uk[i], dsa_w_uv[i], rel_bias)
        else:
            lambda_init = 0.8 - 0.6 * math.exp(-0.3 * l)
            y = diff_mixer(h, od_w_in[i], od_w_out[i], diff_lam[i], diff_subln[i], rel_bias, lambda_init)
        x = layer_norm(ALPHA * x + (1.0 + g1[:, None, :]) * y, ln_g[l, 0], ln_b[l, 0])
        h = modulate(x, sh2, sc2)
        y = sq_relu_mlp(h, mlp_w1[l], mlp_w2[l])
        x = layer_norm(ALPHA * x + (1.0 + g2[:, None, :]) * y, ln_g[l, 1], ln_b[l, 1])
    return x
```

```python
import os
import numpy as np
import ml_dtypes
from contextlib import ExitStack
import concourse.bass as bass
import concourse.mybir as mybir
from concourse.bass_utils import run_bass_kernel_spmd

F32 = mybir.dt.float32
BF16 = mybir.dt.bfloat16
AF = mybir.ActivationFunctionType
ALU = mybir.AluOpType
AX = mybir.AxisListType
NPBF = ml_dtypes.bfloat16


class Buf:
    __slots__ = ("name", "t", "w", "rs", "sem", "cnt", "skind")

    def __init__(self, name, t):
        self.name = name
        self.t = t
        self.w = None
        self.rs = []
        self.sem = None
        self.cnt = 0
        self.skind = None

    def __getitem__(self, idx):
        return self.t[idx]

    def view(self, name, ap):
        return Buf(name, ap)


class Prog:
    ENGS = ("pe", "act", "dve", "pool", "sp")

    def __init__(self, nc, stack):
        self.nc = nc
        self.stack = stack
        self.eng = {"pe": nc.tensor, "act": nc.scalar, "dve": nc.vector,
                    "pool": nc.gpsimd, "sp": nc.sync}
        self.esem = {e: stack.enter_context(nc.semaphore("es_" + e)) for e in self.ENGS}
        self.tick = {e: 0 for e in self.ENGS}
        self.waited = {e: {} for e in self.ENGS}
        self.free_sems = {"hw": [], "sw": []}
        self.all = []
        self.cur = None
        self.nbuf = 0
        self.n_inst = 0
        self.bar_t = stack.enter_context(nc.sbuf_tensor("bar_t", [128, 8], F32))

    def sb(self, name, shape, dt, stack=None):
        self.nbuf += 1
        t = (stack or self.stack).enter_context(
            self.nc.sbuf_tensor("%s_%d" % (name, self.nbuf), list(shape), dt))
        nbytes = int(np.prod(shape[1:])) * (4 if dt == F32 else 2)
        pad = (-nbytes) % 64
        if pad:
            (stack or self.stack).enter_context(
                self.nc.sbuf_tensor("pad_%d" % self.nbuf, [128, pad // 2], BF16))
        b = Buf(name, t)
        self.all.append(b)
        if stack is not None and self.cur is not None:
            self.cur.append(b)
        return b

    def ps(self, name, shape, dt=F32, stack=None):
        self.nbuf += 1
        t = (stack or self.stack).enter_context(
            self.nc.psum_tensor("%s_%d" % (name, self.nbuf), list(shape), dt))
        b = Buf(name, t)
        self.all.append(b)
        return b

    def mk(self, name, ap):
        b = Buf(name, ap)
        self.all.append(b)
        return b

    def begin_stage(self):
        self.cur = []

    def end_stage(self):
        self.barrier()
        for b in self.cur:
            if b.sem is not None:
                self.free_sems[b.skind].append((b.sem, b.cnt))
                b.sem = None
        ids = set(id(b) for b in self.cur)
        self.all = [b for b in self.all if id(b) not in ids]
        self.cur = None

    def _dsem(self, b, kind):
        assert b.skind in (None, kind), "buffer %s used by both HW and SW DMA queues" % b.name
        b.skind = kind
        if b.sem is None:
            if self.free_sems[kind]:
                b.sem, b.cnt = self.free_sems[kind].pop()
                if os.environ.get("KDBG"):
                    print("REUSE", b.sem, b.cnt, "->", b.name)
            else:
                b.sem = self.stack.enter_context(self.nc.semaphore("ds_%s_%d" % (b.name, self.nbuf)))
                self.nbuf += 1
                b.cnt = 0
        return b.sem


    def _wait(self, e, dep):
        if dep is None:
            return
        if dep[0] == "c":
            _, pe_, tk = dep
            key = ("c", pe_)
            sem = self.esem[pe_]
            val = tk
        else:
            _, sem, val = dep
            key = ("d", id(sem))
        w = self.waited[e]
        if w.get(key, -1) >= val:
            return
        w[key] = val
        self.eng[e].wait_ge(sem, val)
        self.n_inst += 1

    def _deps(self, e, reads, writes):
        deps = []
        for b in reads:
            if b.w is not None:
                if not (b.w[0] == "c" and b.w[1] == e and e == "pe"):
                    deps.append(b.w)
        for b in writes:
            if b.w is not None and not (b.w[0] == "c" and b.w[1] == e and e == "pe"):
                deps.append(b.w)
            for r in b.rs:
                if not (r[0] == "c" and r[1] == e and e == "pe"):
                    deps.append(r)
        for d in self._compact(deps):
            self._wait(e, d)

    def op(self, e, fn, reads=(), writes=()):
        self._deps(e, reads, writes)
        ins = fn(self.eng[e])
        self.tick[e] += 1
        ins.then_inc(self.esem[e], 1)
        self.n_inst += 1
        me = ("c", e, self.tick[e])
        for b in reads:
            b.rs.append(me)
            if len(b.rs) > 24:
                b.rs = self._compact(b.rs)
        for b in writes:
            b.w = me
            b.rs = []
        return ins

    def _compact(self, rs):
        best = {}
        out = []
        for r in rs:
            if r[0] == "c":
                if r[1] not in best or best[r[1]][2] < r[2]:
                    best[r[1]] = r
            else:
                k = ("d", id(r[1]))
                if k not in best or best[k][2] < r[2]:
                    best[k] = r
        return list(best.values())

    def dma(self, q, out, in_, sbuf, is_load, extra_reads=(), **kw):
        if is_load:
            self._deps(q, extra_reads, (sbuf,))
        else:
            self._deps(q, (sbuf,) + tuple(extra_reads), ())
        sem = self._dsem(sbuf, "sw" if q == "pool" else "hw")
        ins = self.eng[q].dma_start(out=out, in_=in_, **kw)
        sbuf.cnt += 16
        ins.then_inc(sem, 16)
        self.n_inst += 1
        me = ("d", sem, sbuf.cnt)
        if is_load:
            sbuf.w = me
            sbuf.rs = []
        else:
            sbuf.rs.append(me)
        return ins

    def barrier(self):
        for e in self.ENGS:
            if e != "pool" and self.tick[e] > 0:
                self._wait("pool", ("c", e, self.tick[e]))
        for b in self.all:
            if b.sem is not None and b.cnt > 0:
                self._wait("pool", ("d", b.sem, b.cnt))
        if self.tick["pool"] > 0:
            self._wait("pool", ("c", "pool", self.tick["pool"]))
        ins = self.nc.gpsimd.memset(self.bar_t[:], 0.0)
        self.tick["pool"] += 1
        ins.then_inc(self.esem["pool"], 1)
        for e in self.ENGS:
            if e != "pool":
                self._wait(e, ("c", "pool", self.tick["pool"]))
        for b in self.all:
            b.w = None
            b.rs = []


import math

D = 1024
DFF = 4096
NEG = -30000.0
ALPHA = 8.0 ** 0.25
EPS = 1e-5
FM_ROWS = 2816
R_QN, R_KVC, R_KS, R_KW, R_IQ, R_IK, R_CKV, R_QLAT = 0, 512, 768, 896, 1024, 1536, 1664, 1792


def t5_bucket_np(dist):
    n = np.maximum(dist, 0)
    nf = np.maximum(n, 1).astype(np.float32)
    large = 16 + (np.log(nf / np.float32(16)) / np.float32(math.log(8.0)) * np.float32(16)).astype(np.int32)
    return np.where(n < 16, n, np.minimum(large, 31))


def make_consts(S):
    NT = S // 128
    NB = S // 64
    NC = S // 16 - 1
    NCB = (NC + 127) // 128
    c = {}
    c["ident_bf"] = np.eye(128, dtype=np.float32).astype(NPBF)
    c["ident_f"] = np.eye(128, dtype=np.float32)
    L = 383
    dist = np.arange(L) - 127
    bk = t5_bucket_np(dist)
    OH = np.zeros((33, L), np.float32)
    for j in range(L):
        if dist[j] >= 0:
            OH[bk[j], j] += 1.0
            OH[31, j] -= 1.0
        else:
            OH[32, j] = NEG
    c["ohf"] = OH
    c["ohr"] = np.ascontiguousarray(OH[:, ::-1])
    sl = np.arange(128)[:, None]
    ql = np.arange(128)[None, :]
    c["tw4"] = np.where(ql < sl, 0.0, NEG).astype(np.float32).astype(NPBF)
    c["tri"] = np.where(np.arange(128)[None, :] <= np.arange(128)[:, None], 0.0, -1e30).astype(np.float32)
    E = np.zeros((128, NT * 128), np.float32)
    for kt in range(NT):
        for s in range(128):
            E[2 * kt + s // 64, kt * 128 + s] = 1.0
    c["esel"] = E.astype(NPBF)
    cc = np.arange(256)[None, :]
    qq = np.arange(128)[:, None]
    cbl = (qq >= 64).astype(np.int64)
    c["sela"] = (cc <= 128 + cbl).astype(np.float32)
    c["selb"] = np.where(cc == 128 + cbl, 2e9, np.where(cc == 127 + cbl, 1e9,
                         np.where(cc > 128 + cbl, -1e9, 0.0))).astype(np.float32)
    cs = np.arange(NC) * 16
    ss = np.arange(NB) * 64
    ov = np.clip(np.minimum(cs[:, None] + 32, ss[None, :] + 64) - np.maximum(cs[:, None], ss[None, :]), 0, None)
    imp = np.zeros((NCB * 128, NB), np.float32)
    imp[:NC] = ov / 16.0
    c["imp"] = np.ascontiguousarray(imp.reshape(NCB, 128, NB).transpose(1, 0, 2)).reshape(128, NCB * NB)
    return c


W_NAMES = ["rel_bias", "ada_w", "ada_b", "ln_g", "ln_b", "ev_w_in", "ev_w_out", "nsa_pe_k", "nsa_pe_v",
           "nsa_w1_k", "nsa_w2_k", "nsa_w1_v", "nsa_w2_v", "dsa_kv_norm", "dsa_w_uk", "dsa_w_uv",
           "od_w_in", "od_w_out", "diff_lam", "diff_subln", "mlp_w1", "mlp_w2"]


class KB:
    def __init__(self, S, layers, shapes, dbg=()):
        self.S = S
        self.NT = S // 128
        self.NB = S // 64
        self.NC = S // 16 - 1
        self.NCB = (self.NC + 127) // 128
        self.layers = layers
        self.dbg = dbg
        nc = bass.Bass("TRN2", target_bir_lowering=False)
        self.nc = nc
        d = {}
        d["x"] = nc.dram_tensor("x", [S, D], F32, kind="ExternalInput").ap()
        d["c"] = nc.dram_tensor("c", [1, D], F32, kind="ExternalInput").ap()
        for n in W_NAMES:
            d[n] = nc.dram_tensor(n, list(shapes[n]), F32, kind="ExternalInput").ap()
        cs = make_consts(S)
        for n, v in cs.items():
            d[n] = nc.dram_tensor(n, list(v.shape), BF16 if v.dtype == NPBF else F32, kind="ExternalInput").ap()
        self.consts = cs
        d["y"] = nc.dram_tensor("y", [S, D], F32, kind="ExternalOutput").ap()
        self.d = d
        I = lambda n, sh, dt: nc.dram_tensor(n, sh, dt, kind="Internal")
        self.FMt = I("FM", [FM_ROWS, S], BF16)
        self.FM = self.FMt.ap()
        self.TM = I("TM", [S, 1024], BF16).ap()
        self.OS = I("OS", [S, 1024], BF16).ap()
        self.XS = [I("XS0", [S, D], F32).ap(), I("XS1", [S, D], F32).ap()]
        self.X1 = I("X1", [S, D], F32).ap()
        self.H2T = I("H2T", [D, S], BF16).ap()
        self.ADA = I("ADA", [128, 6 * D], F32).ap()
        self.GATE = I("GATE", [S, 24], F32).ap()
        self.IW = I("IW", [S, 8], F32).ap()
        NCB = self.NCB
        self.KC = [I("KC%d" % g, [64, NCB * 128], BF16).ap() for g in range(2)]
        self.VC = [I("VC%d" % g, [NCB * 128, 64], BF16).ap() for g in range(2)]
        self.VRt = I("VR", [16, 383], F32)
        self.VFt = I("VF", [16, 383], F32)
        self.dbg_out = {}
        for n, sh, dt in dbg:
            self.dbg_out[n] = nc.dram_tensor(n, sh, dt, kind="ExternalOutput").ap()

    def build(self):
        with ExitStack() as st:
            p = Prog(self.nc, st)
            self.p = p
            self.psA = [p.ps("psA", [128, 512]) for _ in range(2)]
            self.psOt = [p.ps("psO", [128, 512]) for _ in range(4)]
            self.psO = self.psOt
            self.psB = [p.ps("psB", [128, 512]) for _ in range(2)]
            self.iA = 0
            self.iB = 0
            self.iO = 0
            self.iE = 0
            self.setup()
            xin = self.d["x"]
            for li, l in enumerate(self.layers):
                xout = self.d["y"] if li == len(self.layers) - 1 else self.XS[li % 2]
                sk = os.environ.get("KSKIP", "").split(",")
                self.stage_ada(l)
                self.stage_pre(l, xin)
                if l % 2 == 0:
                    if "cmp" not in sk:
                        self.stage_cmp(l)
                    if "nsa" not in sk:
                        self.stage_nsa(l)
                    if "dsa" not in sk:
                        self.stage_dsa(l)
                else:
                    self.stage_diff(l)
                self.stage_post1(l, xin)
                self.stage_post2(l, xout)
                xin = xout
            p.barrier()
        return self.nc

    def nextA(self):
        self.iA += 1
        return self.psA[self.iA % 2]

    def nextB(self):
        self.iB += 1
        return self.psB[self.iB % 2]

    def mm(self, ob, out, lhsT, rhs, reads, start, stop):
        self.p.op("pe", lambda e: e.matmul(out, lhsT=lhsT, rhs=rhs, start=start, stop=stop), reads, [ob])

    def tr(self, ob, out, in_, reads, f32=False):
        idn = self.ident_f if f32 else self.ident_bf
        self.p.op("pe", lambda e: e.transpose(out=out, in_=in_, identity=idn[:]), list(reads) + [idn], [ob])

    def evac(self, out, in_, ob, ib, scale=None, eng=None):
        self.iE += 1
        e = eng or ("act" if self.iE % 2 else "dve")
        if e == "act":
            if scale is None:
                self.p.op("act", lambda a: a.copy(out=out, in_=in_), [ib], [ob])
            else:
                self.p.op("act", lambda a: a.mul(out=out, in_=in_, mul=float(scale)), [ib], [ob])
        else:
            if scale is None:
                self.p.op(e, lambda a: a.tensor_copy(out=out, in_=in_), [ib], [ob])
            else:
                self.p.op(e, lambda a: a.tensor_scalar(out=out, in0=in_, scalar1=float(scale), scalar2=None,
                                                       op0=ALU.mult), [ib], [ob])

    def load(self, buf, out, in_, q="sp", **kw):
        self.p.dma(q, out, in_, buf, True, **kw)

    def store(self, buf, out, in_, q="sp", **kw):
        self.p.dma(q, out, in_, buf, False, **kw)

    def rstd_from(self, dst, src_ap, srcbuf, scale, stackbufs):
        p = self.p
        p.op("dve", lambda e: e.tensor_scalar(out=dst[:, 0:1], in0=src_ap, scalar1=float(scale), scalar2=EPS,
                                              op0=ALU.mult, op1=ALU.add), [srcbuf], [dst])
        p.op("act", lambda e: e.sqrt(out=dst[:, 0:1], in_=dst[:, 0:1]), [dst], [dst])
        p.op("dve", lambda e: e.reciprocal(out=dst[:, 0:1], in_=dst[:, 0:1]), [dst], [dst])

    def setup(self):
        p, d = self.p, self.d
        g = lambda n, sh, dt: p.sb(n, sh, dt)
        self.ident_bf = g("identbf", [128, 128], BF16)
        self.ident_f = g("identf", [128, 128], F32)
        self.tw4 = g("tw4", [128, 128], BF16)
        self.tri = g("tri", [128, 128], F32)
        self.sela = g("sela", [128, 256], F32)
        self.selb = g("selb", [128, 256], F32)
        self.imp = g("imp", [128, self.NCB * self.NB], F32)
        for b, n in ((self.ident_bf, "ident_bf"), (self.ident_f, "ident_f"), (self.tw4, "tw4"), (self.tri, "tri"),
                     (self.sela, "sela"), (self.selb, "selb"), (self.imp, "imp")):
            self.load(b, b[:], d[n][:, :])
        self.ones_bf = g("onesbf", [128, 128], BF16)
        p.op("dve", lambda e: e.memset(self.ones_bf[:], 1.0), [], [self.ones_bf])
        self.Tt = g("Tt", [128, 16, 2, 128], BF16)
        self.CBt = g("CBt", [128, 8, 16], BF16)
        self.cact = g("cact", [128, 8, 128], BF16)
        self.gate_all = g("gateall", [128, self.NT, 24], F32)
        self.iw_all = g("iwall", [128, self.NT, 8], F32)
        p.begin_stage()
        with ExitStack() as st:
            rel33 = p.sb("rel33", [33, 16], F32, st)
            ohf = p.sb("ohf", [33, 383], F32, st)
            ohr = p.sb("ohr", [33, 383], F32, st)
            vs = p.sb("vs", [16, 383], F32, st)
            vs2 = p.sb("vs2", [16, 383], F32, st)
            p.op("dve", lambda e: e.memset(rel33[:], 1.0), [], [rel33])
            self.load(rel33, rel33[0:32, :], d["rel_bias"][:, :])
            self.load(ohf, ohf[:], d["ohf"][:, :])
            self.load(ohr, ohr[:], d["ohr"][:, :])
            for oh, v, dst in ((ohr, vs, self.VRt), (ohf, vs2, self.VFt)):
                ps = self.nextA()
                self.mm(ps, ps[0:16, 0:383], rel33[:, :], oh[:, :], [rel33, oh], True, True)
                self.evac(v[:], ps[0:16, 0:383], v, ps, eng="dve")
                self.store(v, dst.ap()[:, :], v[:])
            ct = p.sb("ct", [128, 8], F32, st)
            self.load(ct, ct[:], d["c"].rearrange("o (k p) -> p (o k)", p=128), allow_slow_non_contiguous=True)
            p.op("act", lambda e: e.activation(out=ct[:], in_=ct[:], func=AF.Silu), [ct], [ct])
            for kc in range(8):
                p.op("dve", lambda e: e.tensor_scalar(out=self.cact[:, kc, :], in0=self.ones_bf[:],
                                                      scalar1=ct[:, kc:kc + 1], scalar2=None, op0=ALU.mult),
                     [self.ones_bf, ct], [self.cact])
            p.barrier()
            tst = [p.sb("tst", [128, 128], F32, st) for _ in range(2)]
            for h in range(16):
                for dl in range(2):
                    t = tst[(h * 2 + dl) % 2]
                    src = bass.AP(self.VRt, h * 383 + 255 - 128 * dl, [[1, 128], [-1, 128]])
                    self.load(t, t[:], src, allow_slow_non_contiguous=True)
                    p.op("dve", lambda e: e.tensor_copy(out=self.Tt[:, h, dl, :], in_=t[:]), [t], [self.Tt])
            cst = [p.sb("cst", [128, 15], F32, st) for _ in range(2)]
            for h in range(8):
                t = cst[h % 2]
                src = bass.AP(self.VFt, h * 383 + 224, [[1, 128], [-16, 15]])
                self.load(t, t[:], src, allow_slow_non_contiguous=True)
                p.op("dve", lambda e: e.tensor_copy(out=self.CBt[:, h, 0:15], in_=t[:]), [t], [self.CBt])
            p.end_stage()

    def stage_ada(self, l):
        p, d = self.p, self.d
        p.begin_stage()
        with ExitStack() as st:
            wch = [p.sb("adaw", [128, 8, 512], BF16, st) for _ in range(2)]
            bch = [p.sb("adab", [1, 512], BF16, st) for _ in range(2)]
            ob = [p.sb("adao", [128, 512], F32, st) for _ in range(2)]
            for ch in range(12):
                w, b, o = wch[ch % 2], bch[ch % 2], ob[ch % 2]
                self.load(w, w[:], d["ada_w"][l, :, ch * 512:(ch + 1) * 512].rearrange("(k p) n -> p k n", p=128), q="pool")
                self.load(b, b[:], d["ada_b"][l:l + 1, ch * 512:(ch + 1) * 512], q="pool")
                ps = self.nextA()
                for kc in range(8):
                    self.mm(ps, ps[:, :], self.cact[:, kc, :], w[:, kc, :], [self.cact, w], kc == 0, False)
                self.mm(ps, ps[:, :], self.ones_bf[0:1, :], b[0:1, :], [self.ones_bf, b], False, True)
                add1 = 1.0 if (ch // 2) in (1, 2, 4, 5) else 0.0
                p.op("dve", lambda e: e.tensor_scalar(out=o[:], in0=ps[:, :], scalar1=add1, scalar2=None, op0=ALU.add),
                     [ps], [o])
                self.store(o, self.ADA[:, ch * 512:(ch + 1) * 512], o[:])
            p.end_stage()

    def modtile(self, st, name, col0):
        t = self.p.sb(name, [128, D], F32, st)
        self.load(t, t[:], self.ADA[:, col0:col0 + D])
        return t

    def bctile(self, st, name, src_row_ap, n=D, q="sp"):
        t = self.p.sb(name, [128, n], F32, st)
        self.load(t, t[:], src_row_ap.partition_broadcast(128), q=q)
        return t

    def stage_pre(self, l, xin):
        p, d, S = self.p, self.d, self.S
        even = l % 2 == 0
        i = l // 2
        NW = 2528 if even else 3072
        w_in = d["ev_w_in"] if even else d["od_w_in"]
        p.begin_stage()
        with ExitStack() as st:
            W = p.sb("W", [128, 8, NW], BF16, st)
            for kc in range(8):
                self.load(W, W[:, kc, :], w_in[i, kc * 128:(kc + 1) * 128, :], q="pool")
            A1 = self.modtile(st, "A1", 1 * D)
            B1 = self.modtile(st, "B1", 0)
            xts = [p.sb("xt", [128, D], F32, st) for _ in range(2)]
            hbs = [p.sb("hb", [128, D], BF16, st) for _ in range(2)]
            hT4s = [p.sb("hT4", [128, 8, 512], BF16, st) for _ in range(2)]
            stg = [p.sb("stg", [128, 512], BF16, st) for _ in range(4)]
            istg = 0
            if even:
                wuk = p.sb("wuk", [128, 512], BF16, st)
                self.load(wuk, wuk[:], d["dsa_w_uk"][i].rearrange("r h d -> r (h d)"), q="pool")
                wukT = p.sb("wukT", [128, 4, 128], BF16, st)
                for pr in range(4):
                    pb = self.nextB()
                    pbv = pb.t[:].bitcast(BF16)
                    self.tr(pb, pbv[:, 0:128], wuk[:, pr * 128:(pr + 1) * 128], [wuk])
                    self.evac(wukT[:, pr, :], pbv[:, 0:128], wukT, pb)
                kvn = self.bctile(st, "kvn", d["dsa_kv_norm"][i:i + 1, :], 128)
                dqs = [p.sb("dq", [128, 512], BF16, st) for _ in range(2)]
                tmst = [p.sb("tmst", [128, 384], BF16, st) for _ in range(2)]
                gst = [p.sb("gst", [128, 24], F32, st) for _ in range(2)]
                iwst = [p.sb("iwst", [128, 8], F32, st) for _ in range(2)]
                ckst = [p.sb("ckst", [128, 512], BF16, st) for _ in range(2)]
                sst = [p.sb("sst", [128, 2], F32, st) for _ in range(2)]
                junk = p.sb("junk", [128, 128], F32, st)
                traws = [p.sb("traw", [128, 512], F32, st) for _ in range(2)]
                if os.environ.get("KDBG"):
                    print("SBUF remaining after pre-even alloc:", self.nc.sbuf_bytes_remaining)
                fm = []
                for ch in range(4):
                    fm.append((ch * 128, 128, 0.125, R_QN + ch * 128))
                fm += [(512, 128, None, R_KVC), (640, 128, None, R_KVC + 128), (768, 128, None, R_KS),
                       (1024, 128, None, R_KW)]
                for ch in range(4):
                    fm.append((1944 + ch * 128, 128, None, R_IQ + ch * 128))
                fm.append((2456, 64, None, R_IK))
            else:
                vst = [p.sb("vst", [128, 1024], BF16, st) for _ in range(2)]
                fm = []
                for ch in range(8):
                    fm.append((ch * 128, 128, 0.125, ch * 128))
                for ch in range(8):
                    fm.append((1024 + ch * 128, 128, None, 1024 + ch * 128))
            for g4 in range(S // 512):
                hT4 = hT4s[g4 % 2]
                cols = slice(g4 * 512, (g4 + 1) * 512)
                for j in range(4):
                    tt = g4 * 4 + j
                    xt, hb = xts[tt % 2], hbs[tt % 2]
                    self.load(xt, xt[:], xin[tt * 128:(tt + 1) * 128, :])
                    p.op("pool", lambda e: e.tensor_tensor(out=xt[:], in0=xt[:], in1=A1[:], op=ALU.mult), [xt, A1], [xt])
                    p.op("dve", lambda e: e.tensor_tensor(out=hb[:], in0=xt[:], in1=B1[:], op=ALU.add), [xt, B1], [hb])
                    pb = self.nextB()
                    pbv = pb.t[:].bitcast(BF16)
                    for kc in range(8):
                        self.tr(pb, pbv[:, kc * 128:(kc + 1) * 128], hb[:, kc * 128:(kc + 1) * 128], [hb])
                    self.evac(hT4[:, :, j * 128:(j + 1) * 128], pbv.rearrange("p (k t) -> p k t", k=8), hT4, pb)
                for (c0, n, scale, r0) in fm:
                    ps = self.nextA()
                    for kc in range(8):
                        self.mm(ps, ps[0:n, :], W[:, kc, c0:c0 + n], hT4[:, kc, :], [W, hT4], kc == 0, kc == 7)
                    sg = stg[istg % 4]
                    istg += 1
                    self.evac(sg[0:n, :], ps[0:n, :], sg, ps, scale)
                    self.store(sg, self.FM[r0:r0 + n, cols], sg[0:n, :])
                psk = os.environ.get("KPRE", "").split(",")
                if even:
                    for pr in range(4 if "ql" not in psk else 0):
                        ps = self.nextA()
                        c0 = 1304 + pr * 128
                        for kc in range(8):
                            self.mm(ps, ps[:, :], W[:, kc, c0:c0 + 128], hT4[:, kc, :], [W, hT4], kc == 0, kc == 7)
                        dq = dqs[pr % 2]
                        self.evac(dq[:], ps[:, :], dq, ps)
                        for hh in range(2):
                            h = 2 * pr + hh
                            ps2 = self.nextA()
                            self.mm(ps2, ps2[:, :], wukT[hh * 64:(hh + 1) * 64, pr, :], dq[hh * 64:(hh + 1) * 64, :],
                                    [wukT, dq], True, True)
                            sg = stg[istg % 4]
                            istg += 1
                            self.evac(sg[:], ps2[:, :], sg, ps2, 0.125)
                            self.store(sg, self.FM[R_QLAT + h * 128:R_QLAT + (h + 1) * 128, cols], sg[:])
                    ck = ckst[g4 % 2]
                    for j in range(4 if "tm" not in psk else 0):
                        tt = g4 * 4 + j
                        rows = slice(tt * 128, (tt + 1) * 128)
                        ps = self.nextA()
                        ps_iw = self.nextA()
                        for kc in range(8):
                            self.mm(ps_iw, ps_iw[:, 0:136], hT4[:, kc, j * 128:(j + 1) * 128], W[:, kc, 2392:2528],
                                    [hT4, W], kc == 0, kc == 7)
                        for (c0, n, o0) in ((896, 128, 0), (1152, 152, 128), (1816, 128, 384)):
                            for kc in range(8):
                                self.mm(ps, ps[:, o0:o0 + n], hT4[:, kc, j * 128:(j + 1) * 128], W[:, kc, c0:c0 + n],
                                        [hT4, W], kc == 0, kc == 7)
                        tm, gs, iw, ss = tmst[tt % 2], gst[tt % 2], iwst[tt % 2], sst[tt % 2]
                        traw = traws[tt % 2]
                        p.op("dve", lambda e: e.tensor_copy(out=traw[:], in_=ps[:, :]), [ps], [traw])
                        self.evac(tm[:, 0:256], traw[:, 0:256], tm, traw)
                        p.op("dve", lambda e: e.tensor_tensor(out=junk[:], in0=traw[:, 384:512], in1=traw[:, 384:512], op=ALU.mult), [traw], [junk])
                        p.op("dve", lambda e: e.reduce_sum(out=ss[:, 0:1], in_=junk[:], axis=AX.X), [junk], [ss])
                        self.rstd2(ss[:, 0:1], ss, ss[:, 0:1], ss, 1.0 / 128)
                        p.op("dve", lambda e: e.scalar_tensor_tensor(out=tm[:, 256:384], in0=traw[:, 384:512],
                                                                     scalar=ss[:, 0:1], in1=kvn[:], op0=ALU.mult,
                                                                     op1=ALU.mult), [traw, ss, kvn], [tm])
                        p.op("act", lambda e: e.activation(out=self.gate_all[:, tt, :], in_=traw[:, 256:280], func=AF.Sigmoid), [traw], [self.gate_all])
                        p.op("dve", lambda e: e.tensor_copy(out=self.iw_all[:, tt, :], in_=ps_iw[:, 128:136]), [ps_iw], [self.iw_all])
                        self.store(tm, self.TM[rows, 0:384], tm[:])
                        pb = self.nextB()
                        pbv = pb.t[:].bitcast(BF16)
                        self.tr(pb, pbv[:, 0:128], tm[:, 256:384], [tm])
                        self.evac(ck[:, j * 128:(j + 1) * 128], pbv[:, 0:128], ck, pb)
                    self.store(ck, self.FM[R_CKV:R_CKV + 128, cols], ck[:])
                else:
                    for j in range(4):
                        tt = g4 * 4 + j
                        vs_ = vst[tt % 2]
                        for hf in range(2):
                            ps = self.nextA()
                            c0 = 2048 + hf * 512
                            for kc in range(8):
                                self.mm(ps, ps[:, :], hT4[:, kc, j * 128:(j + 1) * 128], W[:, kc, c0:c0 + 512],
                                        [hT4, W], kc == 0, kc == 7)
                            self.evac(vs_[:, hf * 512:(hf + 1) * 512], ps[:, :], vs_, ps)
                        self.store(vs_, self.TM[tt * 128:(tt + 1) * 128, :], vs_[:])
            p.end_stage()

    def attn(self, qts, lo, hi, q_ap, k_ap, v_ap, dv, special, chunk_extra, done, pts):
        p = self.p
        slots = {}
        for qt in qts:
            self.iO += 1
            slots[qt] = self.psO[self.iO % 4]
        kmin = min(lo(qt) for qt in qts)
        kmax = max(hi(qt) for qt in qts)
        q0 = qts[0]
        for kt in range(kmin, kmax + 1):
            act = [qt for qt in qts if lo(qt) <= kt <= hi(qt)]
            if not act:
                continue
            a, b = act[0] - q0, act[-1] - q0 + 1
            ps = self.nextA()
            extras = []
            if chunk_extra is not None:
                for (l_, r_, bufs) in chunk_extra(kt, act[0], act[-1] + 1):
                    extras.append((ps[:, a * 128:b * 128], l_, r_, bufs))
            for qt in act:
                for (l_, r_, bufs) in special(qt, kt):
                    j = qt - q0
                    extras.append((ps[:, j * 128:(j + 1) * 128], l_, r_, bufs))
            qa, qb_ = q_ap(act[0], act[-1] + 1)
            ka, kb_ = k_ap(kt)
            self.mm(ps, ps[:, a * 128:b * 128], ka, qa, kb_ + qb_, True, len(extras) == 0)
            for n_, (o_, l_, r_, bufs) in enumerate(extras):
                self.mm(ps, o_, l_, r_, bufs, False, n_ == len(extras) - 1)
            self.iPT = getattr(self, "iPT", 0) + 1
            pt = pts[self.iPT % len(pts)]
            p.op("act", lambda e: e.activation(out=pt[:, a * 128:b * 128], in_=ps[:, a * 128:b * 128], func=AF.Exp),
                 [ps], [pt])
            va, vb_ = v_ap(kt)
            for qt in act:
                j = qt - q0
                so = slots[qt]
                self.mm(so, so[:, 0:dv + 1], pt[:, j * 128:(j + 1) * 128], va, [pt] + vb_, kt == lo(qt), kt == hi(qt))
                if kt == hi(qt):
                    done(qt, so)

    def rstd2(self, dst_ap, dstbuf, src_ap, srcbuf, scale):
        p = self.p
        p.op("dve", lambda e: e.tensor_scalar(out=dst_ap, in0=src_ap, scalar1=float(scale), scalar2=EPS,
                                              op0=ALU.mult, op1=ALU.add), [srcbuf], [dstbuf])
        p.op("act", lambda e: e.sqrt(out=dst_ap, in_=dst_ap), [dstbuf], [dstbuf])
        p.op("dve", lambda e: e.reciprocal(out=dst_ap, in_=dst_ap), [dstbuf], [dstbuf])

    def load_split(self, buf, out3, in3, n, parts=4, q="sp"):
        step = (n + parts - 1) // parts
        for a in range(0, n, step):
            b = min(n, a + step)
            self.load(buf, out3[:, a:b], in3[:, a:b], q=q)

    def stage_diff(self, l):
        p, d, S, NT = self.p, self.d, self.S, self.NT
        i = l // 2
        lam_init = 0.8 - 0.6 * math.exp(-0.3 * l)
        p.begin_stage()
        with ExitStack() as st:
            lamt = self.bctile(st, "lamt", d["diff_lam"][i:i + 1].rearrange("o a b -> o (a b)"), 256)
            pr = p.sb("lpr", [128, 128], F32, st)
            sm = p.sb("lsm", [128, 4], F32, st)
            p.op("dve", lambda e: e.tensor_tensor(out=pr[:, 0:64], in0=lamt[:, 0:64], in1=lamt[:, 64:128], op=ALU.mult), [lamt], [pr])
            p.op("dve", lambda e: e.tensor_tensor(out=pr[:, 64:128], in0=lamt[:, 128:192], in1=lamt[:, 192:256], op=ALU.mult), [lamt], [pr])
            p.op("dve", lambda e: e.reduce_sum(out=sm[:, 0:1], in_=pr[:, 0:64], axis=AX.X), [pr], [sm])
            p.op("dve", lambda e: e.reduce_sum(out=sm[:, 1:2], in_=pr[:, 64:128], axis=AX.X), [pr], [sm])
            p.op("act", lambda e: e.activation(out=sm[:, 0:2], in_=sm[:, 0:2], func=AF.Exp), [sm], [sm])
            p.op("dve", lambda e: e.tensor_tensor(out=sm[:, 2:3], in0=sm[:, 1:2], in1=sm[:, 0:1], op=ALU.subtract), [sm], [sm])
            p.op("dve", lambda e: e.tensor_scalar(out=sm[:, 3:4], in0=sm[:, 2:3], scalar1=-lam_init, scalar2=None, op0=ALU.add), [sm], [sm])
            subw = self.bctile(st, "subw", d["diff_subln"][i:i + 1, :], 128)
            p.op("dve", lambda e: e.tensor_scalar(out=subw[:], in0=subw[:], scalar1=1.0 - lam_init, scalar2=None, op0=ALU.mult), [subw], [subw])
            Kh = [p.sb("Kh", [128, S], BF16, st) for _ in range(2)]
            Qh = [p.sb("Qh", [128, S], BF16, st) for _ in range(2)]
            Va = [p.sb("Va", [128, NT, 129], BF16, st) for _ in range(2)]
            for v in Va:
                p.op("pool", lambda e: e.memset(v[:, :, 128:129], 1.0), [], [v])
            pts = [p.sb("pt", [128, 512], BF16, st) for _ in range(3)]
            o0s = [p.sb("o0", [128, 4, 128], F32, st) for _ in range(2)]
            ods = [p.sb("od", [128, 128], F32, st) for _ in range(2)]
            osts = [p.sb("ost", [128, 128], BF16, st) for _ in range(3)]
            rrs = [p.sb("rr", [128, 4], F32, st) for _ in range(4)]
            junk = p.sb("junk", [128, 128], F32, st)
            cnt = [0]
            for h in range(8):
                K_, Q_, V_ = Kh[h % 2], Qh[h % 2], Va[h % 2]
                self.load(K_, K_[:], self.FM[1024 + h * 128:1024 + (h + 1) * 128, :])
                self.load(Q_, Q_[:], self.FM[h * 128:(h + 1) * 128, :])
                self.load_split(V_, V_[:, :, 0:128], self.TM[:, h * 128:(h + 1) * 128].rearrange("(t p) e -> p t e", p=128), NT)
                for c in range(NT // 4):
                    qts = list(range(c * 4, c * 4 + 4))
                    o0c = o0s[c % 2]
                    for m in range(2):
                        col = h * 2 + m

                        def special(qt, kt, col=col):
                            if kt == qt:
                                return [(self.ident_bf[:], self.Tt[:, col, 0, :], [self.ident_bf, self.Tt])]
                            if kt == qt - 1:
                                return [(self.ident_bf[:], self.Tt[:, col, 1, :], [self.ident_bf, self.Tt])]
                            return []

                        def done(qt, so, m=m, q0=qts[0], o0c=o0c, h=h):
                            j = qt - q0
                            cnt[0] += 1
                            r = rrs[cnt[0] % 4]
                            p.op("dve", lambda e: e.reciprocal(out=r[:, 0:1], in_=so[:, 128:129]), [so], [r])
                            if m == 0:
                                p.op("dve", lambda e: e.tensor_scalar(out=o0c[:, j, :], in0=so[:, 0:128], scalar1=r[:, 0:1],
                                                                      scalar2=None, op0=ALU.mult), [so, r], [o0c])
                                return
                            od, ot = ods[cnt[0] % 2], osts[cnt[0] % 3]
                            p.op("dve", lambda e: e.tensor_tensor(out=r[:, 1:2], in0=r[:, 0:1], in1=sm[:, 3:4], op=ALU.mult), [r, sm], [r])
                            p.op("dve", lambda e: e.scalar_tensor_tensor(out=od[:], in0=so[:, 0:128], scalar=r[:, 1:2],
                                                                         in1=o0c[:, j, :], op0=ALU.mult, op1=ALU.add),
                                 [so, r, o0c], [od])
                            p.op("act", lambda e: e.activation(out=junk[:], in_=od[:], func=AF.Square, accum_out=r[:, 2:3]),
                                 [od], [junk, r])
                            self.rstd2(r[:, 2:3], r, r[:, 2:3], r, 1.0 / 128)
                            p.op("dve", lambda e: e.scalar_tensor_tensor(out=ot[:], in0=od[:], scalar=r[:, 2:3], in1=subw[:],
                                                                         op0=ALU.mult, op1=ALU.mult), [od, r, subw], [ot])
                            self.store(ot, self.OS[qt * 128:(qt + 1) * 128, h * 128:(h + 1) * 128], ot[:])

                        self.attn(qts, lambda qt: 0, lambda qt: qt,
                                  lambda a, b, m=m, Q_=Q_: (Q_[m * 64:(m + 1) * 64, a * 128:b * 128], [Q_]),
                                  lambda kt, m=m, K_=K_: (K_[m * 64:(m + 1) * 64, kt * 128:(kt + 1) * 128], [K_]),
                                  lambda kt, V_=V_: (V_[:, kt, :], [V_]),
                                  128, special, None, done, pts)
            p.end_stage()

    def layernorm(self, ut, xo, LG, LB, stats, mv, rs):
        p = self.p
        p.op("dve", lambda e: e.bn_stats(out=stats[:, 0, :], in_=ut[:, 0:512]), [ut], [stats])
        p.op("dve", lambda e: e.bn_stats(out=stats[:, 1, :], in_=ut[:, 512:1024]), [ut], [stats])
        p.op("dve", lambda e: e.bn_aggr(out=mv[:], in_=stats[:].rearrange("p a b -> p (a b)")), [stats], [mv])
        self.rstd2(rs[:, 0:1], rs, mv[:, 1:2], mv, 1.0)
        p.op("dve", lambda e: e.tensor_scalar(out=ut[:], in0=ut[:], scalar1=mv[:, 0:1], scalar2=rs[:, 0:1],
                                              op0=ALU.subtract, op1=ALU.mult), [ut, mv, rs], [ut])
        p.op("pool", lambda e: e.tensor_tensor(out=ut[:], in0=ut[:], in1=LG[:], op=ALU.mult), [ut, LG], [ut])
        p.op("dve", lambda e: e.tensor_tensor(out=xo[:], in0=ut[:], in1=LB[:], op=ALU.add), [ut, LB], [xo])

    def stage_post1(self, l, xin):
        p, d, S, NT = self.p, self.d, self.S, self.NT
        i = l // 2
        w_out = d["ev_w_out"] if l % 2 == 0 else d["od_w_out"]
        p.begin_stage()
        with ExitStack() as st:
            Wo = p.sb("Wo", [128, 8, D], BF16, st)
            self.load(Wo, Wo[:], w_out[i].rearrange("(k p) n -> p k n", p=128), q="pool")
            G1 = self.modtile(st, "G1", 2 * D)
            SH2 = self.modtile(st, "SH2", 3 * D)
            SC2 = self.modtile(st, "SC2", 4 * D)
            LG = self.bctile(st, "LG", d["ln_g"][l, 0:1, :])
            LB = self.bctile(st, "LB", d["ln_b"][l, 0:1, :])
            ots = [p.sb("ot", [128, D], BF16, st) for _ in range(2)]
            oTs = [p.sb("oT", [128, 8, 128], BF16, st) for _ in range(2)]
            xts = [p.sb("xt", [128, D], F32, st) for _ in range(2)]
            uts = [p.sb("ut", [128, D], F32, st) for _ in range(2)]
            x1s = [p.sb("x1", [128, D], F32, st) for _ in range(2)]
            hbs = [p.sb("hb", [128, D], BF16, st) for _ in range(2)]
            hTs = [p.sb("hT", [128, 8, 128], BF16, st) for _ in range(2)]
            sts = [p.sb("stats", [128, 2, 6], F32, st) for _ in range(2)]
            mvs = [p.sb("mv", [128, 2], F32, st) for _ in range(2)]
            rss = [p.sb("rs", [128, 1], F32, st) for _ in range(2)]
            for tt in range(NT):
                k = tt % 2
                rows = slice(tt * 128, (tt + 1) * 128)
                ot, oT, xt, ut, x1, hb, hT = ots[k], oTs[k], xts[k], uts[k], x1s[k], hbs[k], hTs[k]
                self.load(ot, ot[:], self.OS[rows, :])
                self.load(xt, xt[:], xin[rows, :])
                pb = self.nextB()
                pbv = pb.t[:].bitcast(BF16)
                for kc in range(8):
                    self.tr(pb, pbv[:, kc * 128:(kc + 1) * 128], ot[:, kc * 128:(kc + 1) * 128], [ot])
                self.evac(oT[:], pbv.rearrange("p (k t) -> p k t", k=8), oT, pb)
                for hf in range(2):
                    ps = self.nextA()
                    for kc in range(8):
                        self.mm(ps, ps[:, :], oT[:, kc, :], Wo[:, kc, hf * 512:(hf + 1) * 512], [oT, Wo], kc == 0, kc == 7)
                    p.op("dve", lambda e: e.tensor_tensor(out=ut[:, hf * 512:(hf + 1) * 512], in0=ps[:, :],
                                                          in1=G1[:, hf * 512:(hf + 1) * 512], op=ALU.mult), [ps, G1], [ut])
                p.op("dve", lambda e: e.scalar_tensor_tensor(out=ut[:], in0=xt[:], scalar=ALPHA, in1=ut[:], op0=ALU.mult,
                                                              op1=ALU.add), [xt, ut], [ut])
                self.layernorm(ut, x1, LG, LB, sts[k], mvs[k], rss[k])
                self.store(x1, self.X1[rows, :], x1[:])
                p.op("pool", lambda e: e.tensor_tensor(out=ut[:], in0=x1[:], in1=SC2[:], op=ALU.mult), [x1, SC2], [ut])
                p.op("dve", lambda e: e.tensor_tensor(out=hb[:], in0=ut[:], in1=SH2[:], op=ALU.add), [ut, SH2], [hb])
                pb = self.nextB()
                pbv = pb.t[:].bitcast(BF16)
                for kc in range(8):
                    self.tr(pb, pbv[:, kc * 128:(kc + 1) * 128], hb[:, kc * 128:(kc + 1) * 128], [hb])
                self.evac(hT[:], pbv.rearrange("p (k t) -> p k t", k=8), hT, pb)
                self.store(hT, self.H2T[:, rows].rearrange("(k p) t -> p k t", p=128), hT[:])
            p.end_stage()

    def stage_post2(self, l, xout):
        p, d, S, NT = self.p, self.d, self.S, self.NT
        p.begin_stage()
        with ExitStack() as st:
            W1 = p.sb("W1", [128, 8, DFF], BF16, st)
            for kc in range(8):
                self.load(W1, W1[:, kc, :], d["mlp_w1"][l, kc * 128:(kc + 1) * 128, :], q="pool")
            W2 = p.sb("W2", [128, 32, D], BF16, st)
            w2v = d["mlp_w2"][l].rearrange("(f p) n -> p f n", p=128)
            for a in range(0, 32, 8):
                self.load(W2, W2[:, a:a + 8, :], w2v[:, a:a + 8, :], q="pool")
            G2 = self.modtile(st, "G2", 5 * D)
            LG = self.bctile(st, "LG", d["ln_g"][l, 1:2, :])
            LB = self.bctile(st, "LB", d["ln_b"][l, 1:2, :])
            h2s = [p.sb("h2", [128, 8, 256], BF16, st) for _ in range(2)]
            aT = p.sb("aT", [128, 32, 256], BF16, st)
            rls = [p.sb("rl", [128, 256], F32, st) for _ in range(2)]
            xts = [p.sb("xt", [128, D], F32, st) for _ in range(2)]
            uts = [p.sb("ut", [128, D], F32, st) for _ in range(2)]
            sts = [p.sb("stats", [128, 2, 6], F32, st) for _ in range(2)]
            mvs = [p.sb("mv", [128, 2], F32, st) for _ in range(2)]
            rss = [p.sb("rs", [128, 1], F32, st) for _ in range(2)]
            for g2 in range(S // 256):
                h2 = h2s[g2 % 2]
                self.load(h2, h2[:], self.H2T[:, g2 * 256:(g2 + 1) * 256].rearrange("(k p) t -> p k t", p=128))
                for fc in range(32):
                    ps = self.nextA()
                    for kc in range(8):
                        self.mm(ps, ps[:, 0:256], W1[:, kc, fc * 128:(fc + 1) * 128], h2[:, kc, :], [W1, h2], kc == 0, kc == 7)
                    rl = rls[fc % 2]
                    p.op("act", lambda e: e.activation(out=rl[:], in_=ps[:, 0:256], func=AF.Relu), [ps], [rl])
                    eng = "dve" if fc % 2 else "pool"
                    p.op(eng, lambda e: e.tensor_tensor(out=aT[:, fc, :], in0=rl[:], in1=rl[:], op=ALU.mult), [rl], [aT])
                for j in range(2):
                    tt = g2 * 2 + j
                    k = tt % 2
                    rows = slice(tt * 128, (tt + 1) * 128)
                    xt, ut = xts[k], uts[k]
                    self.load(xt, xt[:], self.X1[rows, :])
                    for hf in range(2):
                        ps = self.psOt[(tt * 2 + hf) % 4]
                        for fc in range(32):
                            self.mm(ps, ps[:, :], aT[:, fc, j * 128:(j + 1) * 128], W2[:, fc, hf * 512:(hf + 1) * 512],
                                    [aT, W2], fc == 0, fc == 31)
                        p.op("dve", lambda e: e.tensor_tensor(out=ut[:, hf * 512:(hf + 1) * 512], in0=ps[:, :],
                                                              in1=G2[:, hf * 512:(hf + 1) * 512], op=ALU.mult), [ps, G2], [ut])
                    p.op("dve", lambda e: e.scalar_tensor_tensor(out=ut[:], in0=xt[:], scalar=ALPHA, in1=ut[:], op0=ALU.mult,
                                                                  op1=ALU.add), [xt, ut], [ut])
                    self.layernorm(ut, xt, LG, LB, sts[k], mvs[k], rss[k])
                    self.store(xt, xout[rows, :], xt[:])
            p.end_stage()

    def stage_cmp(self, l):
        p, d, S, NT, NC, NCB = self.p, self.d, self.S, self.NT, self.NC, self.NCB
        i = l // 2
        p.begin_stage()
        with ExitStack() as st:
            krs = [p.sb("kr", [128, S], BF16, st) for _ in range(2)]
            hidT = p.sb("hidT", [128, 2, NCB * 128], BF16, st)
            xh = p.sb("xh", [128, 512], F32, st)
            t1 = p.sb("t1", [128, 512], F32, st)
            kcst = p.sb("kcst", [64, NCB * 128], BF16, st)
            vcst = p.sb("vcst", [128, NCB, 64], BF16, st)
            p.op("dve", lambda e: e.memset(kcst[:], 0.0), [], [kcst])
            p.op("dve", lambda e: e.memset(vcst[:], 0.0), [], [vcst])
            for kv in range(2):
                nm_ = "k" if kv == 0 else "v"
                w1 = p.sb("w1", [128, 16, 256], BF16, st)
                self.load(w1, w1[:], d["nsa_w1_" + nm_][i].rearrange("(c p) h -> p c h", p=128), q="pool")
                w2 = p.sb("w2", [128, 2, 64], BF16, st)
                self.load(w2, w2[:], d["nsa_w2_" + nm_][i].rearrange("(c p) e -> p c e", p=128), q="pool")
                pef = p.sb("pef", [128, 16], F32, st)
                for two in range(2):
                    self.load(pef, pef[two * 64:(two + 1) * 64, :],
                              d["nsa_pe_" + nm_][i].rearrange("(c two) e -> two e c", two=2)[two],
                              allow_slow_non_contiguous=True)
                peb = p.sb("peb", [128, 16, 32], BF16, st)
                for c in range(16):
                    p.op("dve", lambda e: e.tensor_scalar(out=peb[:, c, :], in0=self.ones_bf[:, 0:32], scalar1=pef[:, c:c + 1],
                                                          scalar2=None, op0=ALU.mult), [self.ones_bf, pef], [peb])
                biasT = p.sb("biasT", [128, 2], F32, st)
                pb = self.nextB()
                for hc in range(2):
                    for c in range(16):
                        self.mm(pb, pb[:, hc * 32:(hc + 1) * 32], w1[:, c, hc * 128:(hc + 1) * 128], peb[:, c, :], [w1, peb], c == 0, c == 15)
                for hc in range(2):
                    self.evac(biasT[:, hc:hc + 1], pb[:, hc * 32:hc * 32 + 1], biasT, pb, eng="dve")
                for g in range(2):
                    kr = krs[g]
                    r0 = R_KVC + kv * 128 + g * 64
                    self.load(kr, kr[0:64, :], self.FM[r0:r0 + 64, :])
                    self.load(kr, kr[64:128, 0:S - 1], self.FM[r0:r0 + 64, 1:S])
                    for hc in range(2):
                        ps = self.nextA()
                        for c in range(16):
                            self.mm(ps, ps[:, 0:NC], w1[:, c, hc * 128:(hc + 1) * 128],
                                    kr[:, 2 * c:2 * c + 16 * (NC - 1) + 1:16], [w1, kr], c == 0, c == 15)
                        p.op("act", lambda e: e.activation(out=xh[:, 0:NC], in_=ps[:, 0:NC], func=AF.Identity,
                                                           bias=biasT[:, hc:hc + 1]), [ps, biasT], [xh])
                        p.op("dve", lambda e: e.tensor_tensor(out=t1[:, 0:NC], in0=xh[:, 0:NC], in1=xh[:, 0:NC], op=ALU.mult), [xh], [t1])
                        p.op("dve", lambda e: e.tensor_scalar(out=t1[:, 0:NC], in0=t1[:, 0:NC], scalar1=0.044715, scalar2=1.0,
                                                              op0=ALU.mult, op1=ALU.add), [t1], [t1])
                        p.op("dve", lambda e: e.tensor_tensor(out=t1[:, 0:NC], in0=t1[:, 0:NC], in1=xh[:, 0:NC], op=ALU.mult), [t1, xh], [t1])
                        p.op("act", lambda e: e.activation(out=t1[:, 0:NC], in_=t1[:, 0:NC], func=AF.Tanh,
                                                           scale=0.7978845608028654), [t1], [t1])
                        p.op("dve", lambda e: e.tensor_scalar(out=t1[:, 0:NC], in0=t1[:, 0:NC], scalar1=0.5, scalar2=0.5,
                                                              op0=ALU.mult, op1=ALU.add), [t1], [t1])
                        p.op("dve", lambda e: e.tensor_tensor(out=hidT[:, hc, 0:NC], in0=t1[:, 0:NC], in1=xh[:, 0:NC], op=ALU.mult),
                             [t1, xh], [hidT])
                    if kv == 0:
                        ps = self.nextA()
                        for hc in range(2):
                            self.mm(ps, ps[0:64, 0:NC], w2[:, hc, :], hidT[:, hc, 0:NC], [w2, hidT], hc == 0, hc == 1)
                        self.evac(kcst[:, 0:NC], ps[0:64, 0:NC], kcst, ps)
                        self.store(kcst, self.KC[g], kcst[:])
                    else:
                        for nb in range(NCB):
                            n0 = nb * 128
                            nn = min(128, NC - n0)
                            ps = self.nextA()
                            for hc in range(2):
                                self.mm(ps, ps[0:nn, 0:64], hidT[:, hc, n0:n0 + nn], w2[:, hc, :], [hidT, w2], hc == 0, hc == 1)
                            self.evac(vcst[0:nn, nb, :], ps[0:nn, 0:64], vcst, ps)
                        self.store(vcst, self.VC[g].rearrange("(b p) e -> p b e", p=128), vcst[:])
            p.end_stage()

    def stage_nsa(self, l):
        p, d, S, NT, NB, NC, NCB = self.p, self.d, self.S, self.NT, self.NB, self.NC, self.NCB
        p.begin_stage()
        with ExitStack() as st:
            esel = p.sb("esel", [128, NT * 128], BF16, st)
            self.load(esel, esel[:], d["esel"][:, :])
            gate = self.gate_all
            kc2 = p.sb("kc2", [128, NCB * 128], BF16, st)
            vc = p.sb("vc", [128, NCB, 64], BF16, st)
            ks2 = p.sb("ks2", [128, S], BF16, st)
            kw2 = p.sb("kw2", [128, S], BF16, st)
            vsa = p.sb("vsa", [128, NT, 65], BF16, st)
            vwa = p.sb("vwa", [128, NT, 65], BF16, st)
            p.op("pool", lambda e: e.memset(vsa[:, :, 64:65], 1.0), [], [vsa])
            p.op("pool", lambda e: e.memset(vwa[:, :, 64:65], 1.0), [], [vwa])
            Qp = [p.sb("Qp", [128, S], BF16, st) for _ in range(2)]
            pts = [p.sb("pt", [128, 512], BF16, st) for _ in range(3)]
            accs = [p.sb("acc", [128, 4, 4, 64], F32, st) for _ in range(2)]
            nmTs = [p.sb("nmT", [128, 512], BF16, st) for _ in range(2)]
            Pbs = [p.sb("Pb", [128, NCB * 128], BF16, st) for _ in range(2)]
            Pg = p.sb("Pg", [128, NCB * 128], F32, st)
            PTs = [p.sb("PT", [128, NCB, 128], BF16, st) for _ in range(2)]
            PgT = p.sb("PgT", [128, NCB, 128], F32, st)
            sc2 = p.sb("sc2", [128, NB], F32, st)
            sc3 = p.sb("sc3", [128, NB], F32, st)
            m1 = p.sb("m1", [128, 8], F32, st)
            m2 = p.sb("m2", [128, 8], F32, st)
            nm = p.sb("nm", [128, 128], BF16, st)
            p.op("dve", lambda e: e.memset(nm[:], 0.0), [], [nm])
            rrs = [p.sb("rr", [128, 4], F32, st) for _ in range(4)]
            osts = [p.sb("ost", [128, 256], BF16, st) for _ in range(2)]
            cnt = [0]
            for g in range(2):
                for b in Pbs + [Pg]:
                    p.op("dve", lambda e: e.memset(b[:], 0.0), [], [b])
                for half in range(2):
                    hs = slice(half * 64, (half + 1) * 64)
                    self.load(kc2, kc2[hs, :], self.KC[g])
                    self.load(ks2, ks2[hs, :], self.FM[R_KS + g * 64:R_KS + (g + 1) * 64, :])
                    self.load(kw2, kw2[hs, :], self.FM[R_KW + g * 64:R_KW + (g + 1) * 64, :])
                self.load(vc, vc[:], self.VC[g].rearrange("(b p) e -> p b e", p=128))
                self.load_split(vsa, vsa[:, :, 0:64], self.TM[:, g * 64:(g + 1) * 64].rearrange("(t p) e -> p t e", p=128), NT)
                self.load_split(vwa, vwa[:, :, 0:64], self.TM[:, 128 + g * 64:128 + (g + 1) * 64].rearrange("(t p) e -> p t e", p=128), NT)
                for pp in range(2):
                    r0 = R_QN + (g * 4 + 2 * pp) * 64
                    self.load(Qp[pp], Qp[pp][:], self.FM[r0:r0 + 128, :])
                for c in range(NT // 4):
                    qts = list(range(c * 4, c * 4 + 4))
                    q0 = qts[0]
                    acc, nmT = accs[c % 2], nmTs[c % 2]
                    for j, qt in enumerate(qts):
                        lim = min(NC, 8 * qt + 7)
                        nblk = (lim + 127) // 128
                        n_lo = max(0, 8 * qt - 8)
                        n_hi = min(lim, 8 * qt + 7)
                        mm_lo = n_lo - (8 * qt - 8)
                        mm_hi = n_hi - (8 * qt - 8)
                        for hh in range(4):
                            h = g * 4 + hh
                            qs = slice((hh % 2) * 64, (hh % 2) * 64 + 64)
                            Q_ = Qp[hh // 2]
                            ps = self.nextA()
                            self.mm(ps, ps[:, 0:lim], Q_[qs, qt * 128:(qt + 1) * 128], kc2[qs, 0:lim], [Q_, kc2], True, False)
                            self.mm(ps, ps[:, n_lo:n_hi], self.ident_bf[:], self.CBt[:, h, mm_lo:mm_hi], [self.ident_bf, self.CBt], False, True)
                            cnt[0] += 1
                            r = rrs[cnt[0] % 4]
                            Pb = Pbs[cnt[0] % 2]
                            PT = PTs[cnt[0] % 2]
                            p.op("act", lambda e: e.activation(out=Pb[:, 0:lim], in_=ps[:, 0:lim], func=AF.Exp, accum_out=r[:, 0:1]),
                                 [ps], [Pb, r])
                            p.op("dve", lambda e: e.tensor_scalar(out=r[:, 0:1], in0=r[:, 0:1], scalar1=1e-30, scalar2=None, op0=ALU.max), [r], [r])
                            p.op("dve", lambda e: e.reciprocal(out=r[:, 0:1], in_=r[:, 0:1]), [r], [r])
                            if hh == 0:
                                p.op("dve", lambda e: e.tensor_scalar(out=Pg[:, 0:lim], in0=Pb[:, 0:lim], scalar1=r[:, 0:1], scalar2=None,
                                                                      op0=ALU.mult), [Pb, r], [Pg])
                            else:
                                p.op("dve", lambda e: e.scalar_tensor_tensor(out=Pg[:, 0:lim], in0=Pb[:, 0:lim], scalar=r[:, 0:1],
                                                                             in1=Pg[:, 0:lim], op0=ALU.mult, op1=ALU.add), [Pb, r, Pg], [Pg])
                            pb = self.nextB()
                            pbv = pb.t[:].bitcast(BF16)
                            for nb in range(nblk):
                                self.tr(pb, pbv[:, nb * 128:(nb + 1) * 128], Pb[:, nb * 128:(nb + 1) * 128], [Pb])
                            self.evac(PT[:, 0:nblk, :], pbv[:, 0:nblk * 128].rearrange("p (b t) -> p b t", b=nblk), PT, pb)
                            self.iO += 1
                            so = self.psO[self.iO % 4]
                            for nb in range(nblk):
                                self.mm(so, so[:, 0:64], PT[:, nb, :], vc[:, nb, :], [PT, vc], nb == 0, nb == nblk - 1)
                            p.op("dve", lambda e: e.tensor_tensor(out=r[:, 1:2], in0=r[:, 0:1], in1=gate[:, qt, h * 3:h * 3 + 1], op=ALU.mult),
                                 [r, gate], [r])
                            p.op("dve", lambda e: e.tensor_scalar(out=acc[:, hh, j, :], in0=so[:, 0:64], scalar1=r[:, 1:2], scalar2=None,
                                                                  op0=ALU.mult), [so, r], [acc])
                        pb = self.nextB()
                        for nb in range(nblk):
                            self.tr(pb, pb[:, nb * 128:(nb + 1) * 128], Pg[:, nb * 128:(nb + 1) * 128], [Pg], f32=True)
                        self.evac(PgT[:, 0:nblk, :], pb[:, 0:nblk * 128].rearrange("p (b t) -> p b t", b=nblk), PgT, pb)
                        ps = self.nextA()
                        for nb in range(nblk):
                            self.mm(ps, ps[:, 0:NB], PgT[:, nb, :], self.imp[:, nb * NB:(nb + 1) * NB], [PgT, self.imp], nb == 0, nb == nblk - 1)
                        o_ = 128 - 2 * qt
                        p.op("dve", lambda e: e.tensor_tensor(out=sc2[:], in0=ps[:, 0:NB], in1=self.sela[:, o_:o_ + NB], op=ALU.mult),
                             [ps, self.sela], [sc2])
                        p.op("dve", lambda e: e.tensor_tensor(out=sc2[:], in0=sc2[:], in1=self.selb[:, o_:o_ + NB], op=ALU.add),
                             [sc2, self.selb], [sc2])
                        p.op("dve", lambda e: e.memset(sc2[:, 0:1], 3e9), [], [sc2])
                        p.op("dve", lambda e: e.max(out=m1[:], in_=sc2[:]), [sc2], [m1])
                        p.op("dve", lambda e: e.match_replace(out=sc3[:], in_to_replace=m1[:], in_values=sc2[:], imm_value=-3e9),
                             [m1, sc2], [sc3])
                        p.op("dve", lambda e: e.max(out=m2[:], in_=sc3[:]), [sc3], [m2])
                        p.op("dve", lambda e: e.tensor_scalar(out=nm[:, 0:NB], in0=sc2[:], scalar1=m2[:, 7:8], scalar2=NEG,
                                                              op0=ALU.is_lt, op1=ALU.mult), [sc2, m2], [nm])
                        pb = self.nextB()
                        pbv = pb.t[:].bitcast(BF16)
                        self.tr(pb, pbv[:, 0:128], nm[:, :], [nm])
                        self.evac(nmT[:, j * 128:(j + 1) * 128], pbv[:, 0:128], nmT, pb)
                    for br in range(2):
                        for hh in range(4):
                            h = g * 4 + hh
                            qs = slice((hh % 2) * 64, (hh % 2) * 64 + 64)
                            Q_ = Qp[hh // 2]
                            K_ = ks2 if br == 0 else kw2
                            V_ = vsa if br == 0 else vwa

                            def special(qt, kt, h=h, br=br):
                                ib = self.ident_bf
                                if kt == qt:
                                    return [(ib[:], self.Tt[:, h, 0, :], [ib, self.Tt])]
                                if kt == qt - 1:
                                    return [(ib[:], self.Tt[:, h, 1, :], [ib, self.Tt])]
                                if br == 1 and kt == qt - 4:
                                    return [(ib[:], self.tw4[:], [ib, self.tw4])]
                                return []

                            def cextra(kt, qa, qb, nmT=nmT, q0=q0):
                                return [(esel[:, kt * 128:(kt + 1) * 128], nmT[:, (qa - q0) * 128:(qb - q0) * 128], [esel, nmT])]

                            def done(qt, so, h=h, hh=hh, br=br, acc=acc, q0=q0):
                                j = qt - q0
                                cnt[0] += 1
                                r = rrs[cnt[0] % 4]
                                p.op("dve", lambda e: e.reciprocal(out=r[:, 0:1], in_=so[:, 64:65]), [so], [r])
                                p.op("dve", lambda e: e.tensor_tensor(out=r[:, 1:2], in0=r[:, 0:1], in1=gate[:, qt, h * 3 + 1 + br:h * 3 + 2 + br],
                                                                      op=ALU.mult), [r, gate], [r])
                                p.op("dve", lambda e: e.scalar_tensor_tensor(out=acc[:, hh, j, :], in0=so[:, 0:64], scalar=r[:, 1:2],
                                                                             in1=acc[:, hh, j, :], op0=ALU.mult, op1=ALU.add),
                                     [so, r, acc], [acc])

                            lo = (lambda qt: 0) if br == 0 else (lambda qt: max(0, qt - 4))
                            self.attn(qts, lo, lambda qt: qt,
                                      lambda a, b, Q_=Q_, qs=qs: (Q_[qs, a * 128:b * 128], [Q_]),
                                      lambda kt, K_=K_, qs=qs: (K_[qs, kt * 128:(kt + 1) * 128], [K_]),
                                      lambda kt, V_=V_: (V_[:, kt, :], [V_]),
                                      64, special, cextra if br == 0 else None, done, pts)
                    for j, qt in enumerate(qts):
                        ot = osts[qt % 2]
                        p.op("pool", lambda e: e.tensor_copy(out=ot[:].rearrange("p (h e) -> p h e", h=4), in_=acc[:, :, j, :]), [acc], [ot])
                        self.store(ot, self.OS[qt * 128:(qt + 1) * 128, g * 256:(g + 1) * 256], ot[:])
            p.end_stage()

    def stage_dsa(self, l):
        p, d, S, NT = self.p, self.d, self.S, self.NT
        i = l // 2
        NIT = 18
        p.begin_stage()
        with ExitStack() as st:
            ckvT = p.sb("ckvT", [128, S], BF16, st)
            self.load(ckvT, ckvT[:], self.FM[R_CKV:R_CKV + 128, :])
            ckva = p.sb("ckva", [128, NT, 129], BF16, st)
            p.op("pool", lambda e: e.memset(ckva[:, :, 128:129], 1.0), [], [ckva])
            self.load_split(ckva, ckva[:, :, 0:128], self.TM[:, 256:384].rearrange("(t p) e -> p t e", p=128), NT)
            ik2 = p.sb("ik2", [128, S], BF16, st)
            for half in range(2):
                self.load(ik2, ik2[half * 64:(half + 1) * 64, :], self.FM[R_IK:R_IK + 64, :])
            wuv = p.sb("wuv", [128, 512], BF16, st)
            self.load(wuv, wuv[:], d["dsa_w_uv"][i].rearrange("r h e -> r (h e)"), q="pool")
            iw = self.iw_all
            isc = p.sb("isc", [128, S], F32, st)
            nm = p.sb("nm", [128, S], BF16, st)
            nmT = p.sb("nmT", [128, NT, 256], BF16, st)
            iqs = [p.sb("iq", [128, 4, 256], BF16, st) for _ in range(2)]
            qls = [p.sb("ql", [128, 8, 256], BF16, st) for _ in range(2)]
            rls = [p.sb("rl", [128, 512], F32, st) for _ in range(2)]
            pts = [p.sb("pt", [128, 256], BF16, st) for _ in range(3)]
            sml = p.sb("sml", [128, 8], F32, st)
            thrc = p.sb("thrc", [128, 1], F32, st)
            p.op("dve", lambda e: e.memset(thrc[:], -1e29), [], [thrc])
            rrs = [p.sb("rr", [128, 2], F32, st) for _ in range(4)]
            ols = [p.sb("ol", [128, 128], BF16, st) for _ in range(2)]
            olTs = [p.sb("olT", [128, 128], BF16, st) for _ in range(2)]
            osts = [p.sb("ost", [128, 2, 512], BF16, st) for _ in range(2)]
            cnt = [0]
            for c2 in range(NT // 2):
                qts = [2 * c2, 2 * c2 + 1]
                q0 = qts[0]
                cols = slice(q0 * 128, q0 * 128 + 256)
                iq, ql, ost = iqs[c2 % 2], qls[c2 % 2], osts[c2 % 2]
                self.load(iq, iq[:], self.FM[R_IQ:R_IQ + 512, cols].rearrange("(a p) t -> p a t", p=128))
                self.load(ql, ql[:], self.FM[R_QLAT:R_QLAT + 1024, cols].rearrange("(h p) t -> p h t", p=128))
                for j, qt in enumerate(qts):
                    L = (qt + 1) * 128
                    for c0 in range(0, L, 512):
                        c1 = min(L, c0 + 512)
                        w = c1 - c0
                        for jh in range(8):
                            hs = slice((jh % 2) * 64, (jh % 2) * 64 + 64)
                            ps = self.nextA()
                            self.mm(ps, ps[:, 0:w], iq[hs, jh // 2, j * 128:(j + 1) * 128], ik2[hs, c0:c1], [iq, ik2], True, True)
                            cnt[0] += 1
                            rl = rls[cnt[0] % 2]
                            p.op("act", lambda e: e.activation(out=rl[:, 0:w], in_=ps[:, 0:w], func=AF.Relu), [ps], [rl])
                            if jh == 0:
                                p.op("dve", lambda e: e.tensor_scalar(out=isc[:, c0:c1], in0=rl[:, 0:w], scalar1=iw[:, qt, 0:1], scalar2=None,
                                                                      op0=ALU.mult), [rl, iw], [isc])
                            else:
                                p.op("dve", lambda e: e.scalar_tensor_tensor(out=isc[:, c0:c1], in0=rl[:, 0:w], scalar=iw[:, qt, jh:jh + 1],
                                                                             in1=isc[:, c0:c1], op0=ALU.mult, op1=ALU.add), [rl, iw, isc], [isc])
                    if qt >= 2:
                        p.op("dve", lambda e: e.tensor_reduce(out=sml[:, 0:1], in_=isc[:, 0:L], axis=AX.X, op=ALU.max), [isc], [sml])
                        p.op("dve", lambda e: e.tensor_reduce(out=sml[:, 1:2], in_=isc[:, 0:L], axis=AX.X, op=ALU.min), [isc], [sml])
                    p.op("dve", lambda e: e.tensor_tensor(out=isc[:, qt * 128:L], in0=isc[:, qt * 128:L], in1=self.tri[:], op=ALU.add),
                         [isc, self.tri], [isc])
                    if qt >= 2:
                        p.op("dve", lambda e: e.tensor_copy(out=sml[:, 2:3], in_=sml[:, 1:2]), [sml], [sml])
                        p.op("dve", lambda e: e.tensor_tensor(out=sml[:, 3:4], in0=sml[:, 0:1], in1=sml[:, 1:2], op=ALU.subtract), [sml], [sml])
                        p.op("dve", lambda e: e.tensor_scalar(out=sml[:, 3:4], in0=sml[:, 3:4], scalar1=1.0001, scalar2=1e-6,
                                                              op0=ALU.mult, op1=ALU.add), [sml], [sml])
                        for k in range(NIT):
                            f = 2.0 ** -(k + 1)
                            p.op("dve", lambda e: e.tensor_scalar(out=sml[:, 4:5], in0=sml[:, 3:4], scalar1=f, scalar2=None, op0=ALU.mult), [sml], [sml])
                            p.op("dve", lambda e: e.tensor_tensor(out=sml[:, 5:6], in0=sml[:, 2:3], in1=sml[:, 4:5], op=ALU.add), [sml], [sml])
                            p.op("dve", lambda e: e.tensor_scalar(out=nm[:, 0:L], in0=isc[:, 0:L], scalar1=sml[:, 5:6], scalar2=None,
                                                                  op0=ALU.is_ge, op1=ALU.add, accum_out=sml[:, 6:7]), [isc, sml], [nm, sml])
                            p.op("dve", lambda e: e.scalar_tensor_tensor(out=sml[:, 7:8], in0=sml[:, 6:7], scalar=255.5, in1=sml[:, 4:5],
                                                                         op0=ALU.is_ge, op1=ALU.mult), [sml], [sml])
                            p.op("dve", lambda e: e.tensor_tensor(out=sml[:, 2:3], in0=sml[:, 2:3], in1=sml[:, 7:8], op=ALU.add), [sml], [sml])
                        thr = sml[:, 2:3]
                        tb = sml
                    else:
                        thr = thrc[:, 0:1]
                        tb = thrc
                    p.op("dve", lambda e: e.tensor_scalar(out=nm[:, 0:L], in0=isc[:, 0:L], scalar1=thr, scalar2=NEG, op0=ALU.is_lt, op1=ALU.mult),
                         [isc, tb], [nm])
                    for k0 in range(0, qt + 1, 8):
                        k1 = min(qt + 1, k0 + 8)
                        pb = self.nextB()
                        pbv = pb.t[:].bitcast(BF16)
                        for kt in range(k0, k1):
                            self.tr(pb, pbv[:, (kt - k0) * 128:(kt - k0 + 1) * 128], nm[:, kt * 128:(kt + 1) * 128], [nm])
                        self.evac(nmT[:, k0:k1, j * 128:(j + 1) * 128],
                                  pbv[:, 0:(k1 - k0) * 128].rearrange("p (k t) -> p k t", k=k1 - k0), nmT, pb)
                for h in range(8):
                    col = 8 + h

                    def special(qt, kt, col=col):
                        ib = self.ident_bf
                        if kt == qt:
                            return [(ib[:], self.Tt[:, col, 0, :], [ib, self.Tt])]
                        if kt == qt - 1:
                            return [(ib[:], self.Tt[:, col, 1, :], [ib, self.Tt])]
                        return []

                    def cextra(kt, qa, qb, q0=q0):
                        return [(self.ident_bf[:], nmT[:, kt, (qa - q0) * 128:(qb - q0) * 128], [self.ident_bf, nmT])]

                    def done(qt, so, h=h, q0=q0, ost=ost):
                        j = qt - q0
                        cnt[0] += 1
                        r, ol, olT = rrs[cnt[0] % 4], ols[cnt[0] % 2], olTs[cnt[0] % 2]
                        p.op("dve", lambda e: e.reciprocal(out=r[:, 0:1], in_=so[:, 128:129]), [so], [r])
                        p.op("dve", lambda e: e.tensor_scalar(out=ol[:], in0=so[:, 0:128], scalar1=r[:, 0:1], scalar2=None, op0=ALU.mult),
                             [so, r], [ol])
                        pb = self.nextB()
                        pbv = pb.t[:].bitcast(BF16)
                        self.tr(pb, pbv[:, 0:128], ol[:], [ol])
                        self.evac(olT[:], pbv[:, 0:128], olT, pb)
                        pb2 = self.nextB()
                        self.mm(pb2, pb2[:, 0:64], olT[:], wuv[:, h * 64:(h + 1) * 64], [olT, wuv], True, True)
                        self.evac(ost[:, j, h * 64:(h + 1) * 64], pb2[:, 0:64], ost, pb2)

                    self.attn(qts, lambda qt: 0, lambda qt: qt,
                              lambda a, b, h=h, ql=ql, q0=q0: (ql[:, h, (a - q0) * 128:(b - q0) * 128], [ql]),
                              lambda kt: (ckvT[:, kt * 128:(kt + 1) * 128], [ckvT]),
                              lambda kt: (ckva[:, kt, :], [ckva]),
                              128, special, cextra, done, pts)
                for j, qt in enumerate(qts):
                    self.store(ost, self.OS[qt * 128:(qt + 1) * 128, 512:1024], ost[:, j, :])
            p.end_stage()


_CACHE = {}


def _get_prog(S, layers, shapes):
    key = (S, tuple(layers))
    if key not in _CACHE:
        kb = KB(S, layers, shapes)
        kb.build()
        _CACHE[key] = kb
    return _CACHE[key]


def run_kernel(inputs, S, layers, nb):
    shapes = {n: np.asarray(inputs[n]).shape for n in W_NAMES}
    kb = _get_prog(S, layers, shapes)
    base = {n: np.ascontiguousarray(np.asarray(inputs[n], dtype=np.float32)) for n in W_NAMES}
    base.update(kb.consts)
    x = np.asarray(inputs["x"], dtype=np.float32)
    c = np.asarray(inputs["c"], dtype=np.float32)
    ncore = 2 * nb
    in_maps = []
    for k in range(ncore):
        b = k // 2
        m = dict(base)
        m["x"] = np.ascontiguousarray(x[b])
        m["c"] = np.ascontiguousarray(c[b:b + 1])
        in_maps.append(m)
    res = run_bass_kernel_spmd(kb.nc, in_maps, core_ids=list(range(ncore)))
    return np.stack([np.asarray(res.results[2 * b]["y"]) for b in range(nb)], axis=0).astype(np.float32)


def kernel(**inputs):
    return run_kernel(inputs, 8192, [0, 1, 2, 3], 4)
```

```python
import os
import numpy as np
import ml_dtypes
from contextlib import ExitStack
import concourse.bass as bass
import concourse.mybir as mybir
from concourse.bass_utils import run_bass_kernel_spmd

F32 = mybir.dt.float32
BF16 = mybir.dt.bfloat16
AF = mybir.ActivationFunctionType
ALU = mybir.AluOpType
AX = mybir.AxisListType
NPBF = ml_dtypes.bfloat16


class Buf:
    __slots__ = ("name", "t", "w", "rs", "sem", "cnt", "skind")

    def __init__(self, name, t):
        self.name = name
        self.t = t
        self.w = None
        self.rs = []
        self.sem = None
        self.cnt = 0
        self.skind = None

    def __getitem__(self, idx):
        return self.t[idx]

    def view(self, name, ap):
        return Buf(name, ap)


class Prog:
    ENGS = ("pe", "act", "dve", "pool", "sp")

    def __init__(self, nc, stack):
        self.nc = nc
        self.stack = stack
        self.eng = {"pe": nc.tensor, "act": nc.scalar, "dve": nc.vector,
                    "pool": nc.gpsimd, "sp": nc.sync}
        self.esem = {e: stack.enter_context(nc.semaphore("es_" + e)) for e in self.ENGS}
        self.tick = {e: 0 for e in self.ENGS}
        self.waited = {e: {} for e in self.ENGS}
        self.free_sems = {"hw": [], "sw": []}
        self.all = []
        self.cur = None
        self.nbuf = 0
        self.n_inst = 0
        self.bar_t = stack.enter_context(nc.sbuf_tensor("bar_t", [128, 8], F32))

    def sb(self, name, shape, dt, stack=None):
        self.nbuf += 1
        t = (stack or self.stack).enter_context(
            self.nc.sbuf_tensor("%s_%d" % (name, self.nbuf), list(shape), dt))
        nbytes = int(np.prod(shape[1:])) * (4 if dt == F32 else 2)
        pad = (-nbytes) % 64
        if pad:
            (stack or self.stack).enter_context(
                self.nc.sbuf_tensor("pad_%d" % self.nbuf, [128, pad // 2], BF16))
        b = Buf(name, t)
        self.all.append(b)
        if stack is not None and self.cur is not None:
            self.cur.append(b)
        return b

    def ps(self, name, shape, dt=F32, stack=None):
        self.nbuf += 1
        t = (stack or self.stack).enter_context(
            self.nc.psum_tensor("%s_%d" % (name, self.nbuf), list(shape), dt))
        b = Buf(name, t)
        self.all.append(b)
        return b

    def mk(self, name, ap):
        b = Buf(name, ap)
        self.all.append(b)
        return b

    def begin_stage(self):
        self.cur = []

    def end_stage(self):
        self.barrier()
        for b in self.cur:
            if b.sem is not None:
                self.free_sems[b.skind].append((b.sem, b.cnt))
                b.sem = None
        ids = set(id(b) for b in self.cur)
        self.all = [b for b in self.all if id(b) not in ids]
        self.cur = None

    def _dsem(self, b, kind):
        assert b.skind in (None, kind), "buffer %s used by both HW and SW DMA queues" % b.name
        b.skind = kind
        if b.sem is None:
            if self.free_sems[kind]:
                b.sem, b.cnt = self.free_sems[kind].pop()
                if os.environ.get("KDBG"):
                    print("REUSE", b.sem, b.cnt, "->", b.name)
            else:
                b.sem = self.stack.enter_context(self.nc.semaphore("ds_%s_%d" % (b.name, self.nbuf)))
                self.nbuf += 1
                b.cnt = 0
        return b.sem


    def _wait(self, e, dep):
        if dep is None:
            return
        if dep[0] == "c":
            _, pe_, tk = dep
            key = ("c", pe_)
            sem = self.esem[pe_]
            val = tk
        else:
            _, sem, val = dep
            key = ("d", id(sem))
        w = self.waited[e]
        if w.get(key, -1) >= val:
            return
        w[key] = val
        self.eng[e].wait_ge(sem, val)
        self.n_inst += 1

    def _deps(self, e, reads, writes):
        deps = []
        for b in reads:
            if b.w is not None:
                if not (b.w[0] == "c" and b.w[1] == e and e == "pe"):
                    deps.append(b.w)
        for b in writes:
            if b.w is not None and not (b.w[0] == "c" and b.w[1] == e and e == "pe"):
                deps.append(b.w)
            for r in b.rs:
                if not (r[0] == "c" and r[1] == e and e == "pe"):
                    deps.append(r)
        for d in self._compact(deps):
            self._wait(e, d)

    def op(self, e, fn, reads=(), writes=()):
        self._deps(e, reads, writes)
        ins = fn(self.eng[e])
        self.tick[e] += 1
        ins.then_inc(self.esem[e], 1)
        self.n_inst += 1
        me = ("c", e, self.tick[e])
        for b in reads:
            b.rs.append(me)
            if len(b.rs) > 24:
                b.rs = self._compact(b.rs)
        for b in writes:
            b.w = me
            b.rs = []
        return ins

    def _compact(self, rs):
        best = {}
        out = []
        for r in rs:
            if r[0] == "c":
                if r[1] not in best or best[r[1]][2] < r[2]:
                    best[r[1]] = r
            else:
                k = ("d", id(r[1]))
                if k not in best or best[k][2] < r[2]:
                    best[k] = r
        return list(best.values())

    def dma(self, q, out, in_, sbuf, is_load, extra_reads=(), **kw):
        if is_load:
            self._deps(q, extra_reads, (sbuf,))
        else:
            self._deps(q, (sbuf,) + tuple(extra_reads), ())
        sem = self._dsem(sbuf, "sw" if q == "pool" else "hw")
        ins = self.eng[q].dma_start(out=out, in_=in_, **kw)
        sbuf.cnt += 16
        ins.then_inc(sem, 16)
        self.n_inst += 1
        me = ("d", sem, sbuf.cnt)
        if is_load:
            sbuf.w = me
            sbuf.rs = []
        else:
            sbuf.rs.append(me)
        return ins

    def barrier(self):
        for e in self.ENGS:
            if e != "pool" and self.tick[e] > 0:
                self._wait("pool", ("c", e, self.tick[e]))
        for b in self.all:
            if b.sem is not None and b.cnt > 0:
                self._wait("pool", ("d", b.sem, b.cnt))
        if self.tick["pool"] > 0:
            self._wait("pool", ("c", "pool", self.tick["pool"]))
        ins = self.nc.gpsimd.memset(self.bar_t[:], 0.0)
        self.tick["pool"] += 1
        ins.then_inc(self.esem["pool"], 1)
        for e in self.ENGS:
            if e != "pool":
                self._wait(e, ("c", "pool", self.tick["pool"]))
        for b in self.all:
            b.w = None
            b.rs = []


import math

D = 1024
DFF = 4096
NEG = -30000.0
ALPHA = 8.0 ** 0.25
EPS = 1e-5
FM_ROWS = 2816
R_QN, R_KVC, R_KS, R_KW, R_IQ, R_IK, R_CKV, R_QLAT = 0, 512, 768, 896, 1024, 1536, 1664, 1792


def t5_bucket_np(dist):
    n = np.maximum(dist, 0)
    nf = np.maximum(n, 1).astype(np.float32)
    large = 16 + (np.log(nf / np.float32(16)) / np.float32(math.log(8.0)) * np.float32(16)).astype(np.int32)
    return np.where(n < 16, n, np.minimum(large, 31))


def make_consts(S):
    NT = S // 128
    NB = S // 64
    NC = S // 16 - 1
    NCB = (NC + 127) // 128
    c = {}
    c["ident_bf"] = np.eye(128, dtype=np.float32).astype(NPBF)
    c["ident_f"] = np.eye(128, dtype=np.float32)
    L = 383
    dist = np.arange(L) - 127
    bk = t5_bucket_np(dist)
    OH = np.zeros((33, L), np.float32)
    for j in range(L):
        if dist[j] >= 0:
            OH[bk[j], j] += 1.0
            OH[31, j] -= 1.0
        else:
            OH[32, j] = NEG
    c["ohf"] = OH
    c["ohr"] = np.ascontiguousarray(OH[:, ::-1])
    sl = np.arange(128)[:, None]
    ql = np.arange(128)[None, :]
    c["tw4"] = np.where(ql < sl, 0.0, NEG).astype(np.float32).astype(NPBF)
    c["tri"] = np.where(np.arange(128)[None, :] <= np.arange(128)[:, None], 0.0, -1e30).astype(np.float32)
    E = np.zeros((128, NT * 128), np.float32)
    for kt in range(NT):
        for s in range(128):
            E[2 * kt + s // 64, kt * 128 + s] = 1.0
    c["esel"] = E.astype(NPBF)
    cc = np.arange(256)[None, :]
    qq = np.arange(128)[:, None]
    cbl = (qq >= 64).astype(np.int64)
    c["sela"] = (cc <= 128 + cbl).astype(np.float32)
    c["selb"] = np.where(cc == 128 + cbl, 2e9, np.where(cc == 127 + cbl, 1e9,
                         np.where(cc > 128 + cbl, -1e9, 0.0))).astype(np.float32)
    cs = np.arange(NC) * 16
    ss = np.arange(NB) * 64
    ov = np.clip(np.minimum(cs[:, None] + 32, ss[None, :] + 64) - np.maximum(cs[:, None], ss[None, :]), 0, None)
    imp = np.zeros((NCB * 128, NB), np.float32)
    imp[:NC] = ov / 16.0
    c["imp"] = np.ascontiguousarray(imp.reshape(NCB, 128, NB).transpose(1, 0, 2)).reshape(128, NCB * NB)
    return c


W_NAMES = ["rel_bias", "ada_w", "ada_b", "ln_g", "ln_b", "ev_w_in", "ev_w_out", "nsa_pe_k", "nsa_pe_v",
           "nsa_w1_k", "nsa_w2_k", "nsa_w1_v", "nsa_w2_v", "dsa_kv_norm", "dsa_w_uk", "dsa_w_uv",
           "od_w_in", "od_w_out", "diff_lam", "diff_subln", "mlp_w1", "mlp_w2"]


class KB:
    def __init__(self, S, layers, shapes, dbg=()):
        self.S = S
        self.NT = S // 128
        self.NB = S // 64
        self.NC = S // 16 - 1
        self.NCB = (self.NC + 127) // 128
        self.layers = layers
        self.dbg = dbg
        nc = bass.Bass("TRN2", target_bir_lowering=False)
        self.nc = nc
        d = {}
        d["x"] = nc.dram_tensor("x", [S, D], F32, kind="ExternalInput").ap()
        d["c"] = nc.dram_tensor("c", [1, D], F32, kind="ExternalInput").ap()
        for n in W_NAMES:
            d[n] = nc.dram_tensor(n, list(shapes[n]), F32, kind="ExternalInput").ap()
        cs = make_consts(S)
        for n, v in cs.items():
            d[n] = nc.dram_tensor(n, list(v.shape), BF16 if v.dtype == NPBF else F32, kind="ExternalInput").ap()
        self.consts = cs
        d["y"] = nc.dram_tensor("y", [S, D], F32, kind="ExternalOutput").ap()
        self.d = d
        I = lambda n, sh, dt: nc.dram_tensor(n, sh, dt, kind="Internal")
        self.FMt = I("FM", [FM_ROWS, S], BF16)
        self.FM = self.FMt.ap()
        self.TM = I("TM", [S, 1024], BF16).ap()
        self.OS = I("OS", [S, 1024], BF16).ap()
        self.XS = [I("XS0", [S, D], F32).ap(), I("XS1", [S, D], F32).ap()]
        self.X1 = I("X1", [S, D], F32).ap()
        self.H2T = I("H2T", [D, S], BF16).ap()
        self.ADA = I("ADA", [128, 6 * D], F32).ap()
        self.GATE = I("GATE", [S, 24], F32).ap()
        self.IW = I("IW", [S, 8], F32).ap()
        NCB = self.NCB
        self.KC = [I("KC%d" % g, [64, NCB * 128], BF16).ap() for g in range(2)]
        self.VC = [I("VC%d" % g, [NCB * 128, 64], BF16).ap() for g in range(2)]
        self.VRt = I("VR", [16, 383], F32)
        self.VFt = I("VF", [16, 383], F32)
        self.dbg_out = {}
        for n, sh, dt in dbg:
            self.dbg_out[n] = nc.dram_tensor(n, sh, dt, kind="ExternalOutput").ap()

    def build(self):
        with ExitStack() as st:
            p = Prog(self.nc, st)
            self.p = p
            self.psA = [p.ps("psA", [128, 512]) for _ in range(2)]
            self.psOt = [p.ps("psO", [128, 512]) for _ in range(4)]
            self.psO = self.psOt
            self.psB = [p.ps("psB", [128, 512]) for _ in range(2)]
            self.iA = 0
            self.iB = 0
            self.iO = 0
            self.iE = 0
            self.setup()
            xin = self.d["x"]
            for li, l in enumerate(self.layers):
                xout = self.d["y"] if li == len(self.layers) - 1 else self.XS[li % 2]
                sk = os.environ.get("KSKIP", "").split(",")
                self.stage_ada(l)
                self.stage_pre(l, xin)
                if l % 2 == 0:
                    if "cmp" not in sk:
                        self.stage_cmp(l)
                    if "nsa" not in sk:
                        self.stage_nsa(l)
                    if "dsa" not in sk:
                        self.stage_dsa(l)
                else:
                    self.stage_diff(l)
                self.stage_post1(l, xin)
                self.stage_post2(l, xout)
                xin = xout
            p.barrier()
        return self.nc

    def nextA(self):
        self.iA += 1
        return self.psA[self.iA % 2]

    def nextB(self):
        self.iB += 1
        return self.psB[self.iB % 2]

    def mm(self, ob, out, lhsT, rhs, reads, start, stop):
        self.p.op("pe", lambda e: e.matmul(out, lhsT=lhsT, rhs=rhs, start=start, stop=stop), reads, [ob])

    def tr(self, ob, out, in_, reads, f32=False):
        idn = self.ident_f if f32 else self.ident_bf
        self.p.op("pe", lambda e: e.transpose(out=out, in_=in_, identity=idn[:]), list(reads) + [idn], [ob])

    def evac(self, out, in_, ob, ib, scale=None, eng=None):
        self.iE += 1
        e = eng or ("act" if self.iE % 2 else "dve")
        if e == "act":
            if scale is None:
                self.p.op("act", lambda a: a.copy(out=out, in_=in_), [ib], [ob])
            else:
                self.p.op("act", lambda a: a.mul(out=out, in_=in_, mul=float(scale)), [ib], [ob])
        else:
            if scale is None:
                self.p.op(e, lambda a: a.tensor_copy(out=out, in_=in_), [ib], [ob])
            else:
                self.p.op(e, lambda a: a.tensor_scalar(out=out, in0=in_, scalar1=float(scale), scalar2=None,
                                                       op0=ALU.mult), [ib], [ob])

    def load(self, buf, out, in_, q="sp", **kw):
        self.p.dma(q, out, in_, buf, True, **kw)

    def store(self, buf, out, in_, q="sp", **kw):
        self.p.dma(q, out, in_, buf, False, **kw)

    def rstd_from(self, dst, src_ap, srcbuf, scale, stackbufs):
        p = self.p
        p.op("dve", lambda e: e.tensor_scalar(out=dst[:, 0:1], in0=src_ap, scalar1=float(scale), scalar2=EPS,
                                              op0=ALU.mult, op1=ALU.add), [srcbuf], [dst])
        p.op("act", lambda e: e.sqrt(out=dst[:, 0:1], in_=dst[:, 0:1]), [dst], [dst])
        p.op("dve", lambda e: e.reciprocal(out=dst[:, 0:1], in_=dst[:, 0:1]), [dst], [dst])

    def setup(self):
        p, d = self.p, self.d
        g = lambda n, sh, dt: p.sb(n, sh, dt)
        self.ident_bf = g("identbf", [128, 128], BF16)
        self.ident_f = g("identf", [128, 128], F32)
        self.tw4 = g("tw4", [128, 128], BF16)
        self.tri = g("tri", [128, 128], F32)
        self.sela = g("sela", [128, 256], F32)
        self.selb = g("selb", [128, 256], F32)
        self.imp = g("imp", [128, self.NCB * self.NB], F32)
        for b, n in ((self.ident_bf, "ident_bf"), (self.ident_f, "ident_f"), (self.tw4, "tw4"), (self.tri, "tri"),
                     (self.sela, "sela"), (self.selb, "selb"), (self.imp, "imp")):
            self.load(b, b[:], d[n][:, :])
        self.ones_bf = g("onesbf", [128, 128], BF16)
        p.op("dve", lambda e: e.memset(self.ones_bf[:], 1.0), [], [self.ones_bf])
        self.Tt = g("Tt", [128, 16, 2, 128], BF16)
        self.CBt = g("CBt", [128, 8, 16], BF16)
        self.cact = g("cact", [128, 8, 128], BF16)
        self.gate_all = g("gateall", [128, self.NT, 24], F32)
        self.iw_all = g("iwall", [128, self.NT, 8], F32)
        p.begin_stage()
        with ExitStack() as st:
            rel33 = p.sb("rel33", [33, 16], F32, st)
            ohf = p.sb("ohf", [33, 383], F32, st)
            ohr = p.sb("ohr", [33, 383], F32, st)
            vs = p.sb("vs", [16, 383], F32, st)
            vs2 = p.sb("vs2", [16, 383], F32, st)
            p.op("dve", lambda e: e.memset(rel33[:], 1.0), [], [rel33])
            self.load(rel33, rel33[0:32, :], d["rel_bias"][:, :])
            self.load(ohf, ohf[:], d["ohf"][:, :])
            self.load(ohr, ohr[:], d["ohr"][:, :])
            for oh, v, dst in ((ohr, vs, self.VRt), (ohf, vs2, self.VFt)):
                ps = self.nextA()
                self.mm(ps, ps[0:16, 0:383], rel33[:, :], oh[:, :], [rel33, oh], True, True)
                self.evac(v[:], ps[0:16, 0:383], v, ps, eng="dve")
                self.store(v, dst.ap()[:, :], v[:])
            ct = p.sb("ct", [128, 8], F32, st)
            self.load(ct, ct[:], d["c"].rearrange("o (k p) -> p (o k)", p=128), allow_slow_non_contiguous=True)
            p.op("act", lambda e: e.activation(out=ct[:], in_=ct[:], func=AF.Silu), [ct], [ct])
            for kc in range(8):
                p.op("dve", lambda e: e.tensor_scalar(out=self.cact[:, kc, :], in0=self.ones_bf[:],
                                                      scalar1=ct[:, kc:kc + 1], scalar2=None, op0=ALU.mult),
                     [self.ones_bf, ct], [self.cact])
            p.barrier()
            tst = [p.sb("tst", [128, 128], F32, st) for _ in range(2)]
            for h in range(16):
                for dl in range(2):
                    t = tst[(h * 2 + dl) % 2]
                    src = bass.AP(self.VRt, h * 383 + 255 - 128 * dl, [[1, 128], [-1, 128]])
                    self.load(t, t[:], src, allow_slow_non_contiguous=True)
                    p.op("dve", lambda e: e.tensor_copy(out=self.Tt[:, h, dl, :], in_=t[:]), [t], [self.Tt])
            cst = [p.sb("cst", [128, 15], F32, st) for _ in range(2)]
            for h in range(8):
                t = cst[h % 2]
                src = bass.AP(self.VFt, h * 383 + 224, [[1, 128], [-16, 15]])
                self.load(t, t[:], src, allow_slow_non_contiguous=True)
                p.op("dve", lambda e: e.tensor_copy(out=self.CBt[:, h, 0:15], in_=t[:]), [t], [self.CBt])
            p.end_stage()

    def stage_ada(self, l):
        p, d = self.p, self.d
        p.begin_stage()
        with ExitStack() as st:
            wch = [p.sb("adaw", [128, 8, 512], BF16, st) for _ in range(2)]
            bch = [p.sb("adab", [1, 512], BF16, st) for _ in range(2)]
            ob = [p.sb("adao", [128, 512], F32, st) for _ in range(2)]
            for ch in range(12):
                w, b, o = wch[ch % 2], bch[ch % 2], ob[ch % 2]
                self.load(w, w[:], d["ada_w"][l, :, ch * 512:(ch + 1) * 512].rearrange("(k p) n -> p k n", p=128), q="pool")
                self.load(b, b[:], d["ada_b"][l:l + 1, ch * 512:(ch + 1) * 512], q="pool")
                ps = self.nextA()
                for kc in range(8):
                    self.mm(ps, ps[:, :], self.cact[:, kc, :], w[:, kc, :], [self.cact, w], kc == 0, False)
                self.mm(ps, ps[:, :], self.ones_bf[0:1, :], b[0:1, :], [self.ones_bf, b], False, True)
                add1 = 1.0 if (ch // 2) in (1, 2, 4, 5) else 0.0
                p.op("dve", lambda e: e.tensor_scalar(out=o[:], in0=ps[:, :], scalar1=add1, scalar2=None, op0=ALU.add),
                     [ps], [o])
                self.store(o, self.ADA[:, ch * 512:(ch + 1) * 512], o[:])
            p.end_stage()

    def modtile(self, st, name, col0):
        t = self.p.sb(name, [128, D], F32, st)
        self.load(t, t[:], self.ADA[:, col0:col0 + D])
        return t

    def bctile(self, st, name, src_row_ap, n=D, q="sp"):
        t = self.p.sb(name, [128, n], F32, st)
        self.load(t, t[:], src_row_ap.partition_broadcast(128), q=q)
        return t

    def stage_pre(self, l, xin):
        p, d, S = self.p, self.d, self.S
        even = l % 2 == 0
        i = l // 2
        NW = 2528 if even else 3072
        w_in = d["ev_w_in"] if even else d["od_w_in"]
        p.begin_stage()
        with ExitStack() as st:
            W = p.sb("W", [128, 8, NW], BF16, st)
            for kc in range(8):
                self.load(W, W[:, kc, :], w_in[i, kc * 128:(kc + 1) * 128, :], q="pool")
            A1 = self.modtile(st, "A1", 1 * D)
            B1 = self.modtile(st, "B1", 0)
            xts = [p.sb("xt", [128, D], F32, st) for _ in range(2)]
            hbs = [p.sb("hb", [128, D], BF16, st) for _ in range(2)]
            hT4s = [p.sb("hT4", [128, 8, 512], BF16, st) for _ in range(2)]
            stg = [p.sb("stg", [128, 512], BF16, st) for _ in range(4)]
            istg = 0
            if even:
                wuk = p.sb("wuk", [128, 512], BF16, st)
                self.load(wuk, wuk[:], d["dsa_w_uk"][i].rearrange("r h d -> r (h d)"), q="pool")
                wukT = p.sb("wukT", [128, 4, 128], BF16, st)
                for pr in range(4):
                    pb = self.nextB()
                    pbv = pb.t[:].bitcast(BF16)
                    self.tr(pb, pbv[:, 0:128], wuk[:, pr * 128:(pr + 1) * 128], [wuk])
                    self.evac(wukT[:, pr, :], pbv[:, 0:128], wukT, pb)
                kvn = self.bctile(st, "kvn", d["dsa_kv_norm"][i:i + 1, :], 128)
                dqs = [p.sb("dq", [128, 512], BF16, st) for _ in range(2)]
                tmst = [p.sb("tmst", [128, 384], BF16, st) for _ in range(2)]
                gst = [p.sb("gst", [128, 24], F32, st) for _ in range(2)]
                iwst = [p.sb("iwst", [128, 8], F32, st) for _ in range(2)]
                ckst = [p.sb("ckst", [128, 512], BF16, st) for _ in range(2)]
                sst = [p.sb("sst", [128, 2], F32, st) for _ in range(2)]
                junk = p.sb("junk", [128, 128], F32, st)
                traws = [p.sb("traw", [128, 512], F32, st) for _ in range(2)]
                if os.environ.get("KDBG"):
                    print("SBUF remaining after pre-even alloc:", self.nc.sbuf_bytes_remaining)
                fm = []
                for ch in range(4):
                    fm.append((ch * 128, 128, 0.125, R_QN + ch * 128))
                fm += [(512, 128, None, R_KVC), (640, 128, None, R_KVC + 128), (768, 128, None, R_KS),
                       (1024, 128, None, R_KW)]
                for ch in range(4):
                    fm.append((1944 + ch * 128, 128, None, R_IQ + ch * 128))
                fm.append((2456, 64, None, R_IK))
            else:
                vst = [p.sb("vst", [128, 1024], BF16, st) for _ in range(2)]
                fm = []
                for ch in range(8):
                    fm.append((ch * 128, 128, 0.125, ch * 128))
                for ch in range(8):
                    fm.append((1024 + ch * 128, 128, None, 1024 + ch * 128))
            for g4 in range(S // 512):
                hT4 = hT4s[g4 % 2]
                cols = slice(g4 * 512, (g4 + 1) * 512)
                for j in range(4):
                    tt = g4 * 4 + j
                    xt, hb = xts[tt % 2], hbs[tt % 2]
                    self.load(xt, xt[:], xin[tt * 128:(tt + 1) * 128, :])
                    p.op("pool", lambda e: e.tensor_tensor(out=xt[:], in0=xt[:], in1=A1[:], op=ALU.mult), [xt, A1], [xt])
                    p.op("dve", lambda e: e.tensor_tensor(out=hb[:], in0=xt[:], in1=B1[:], op=ALU.add), [xt, B1], [hb])
                    pb = self.nextB()
                    pbv = pb.t[:].bitcast(BF16)
                    for kc in range(8):
                        self.tr(pb, pbv[:, kc * 128:(kc + 1) * 128], hb[:, kc * 128:(kc + 1) * 128], [hb])
                    self.evac(hT4[:, :, j * 128:(j + 1) * 128], pbv.rearrange("p (k t) -> p k t", k=8), hT4, pb)
                for (c0, n, scale, r0) in fm:
                    ps = self.nextA()
                    for kc in range(8):
                        self.mm(ps, ps[0:n, :], W[:, kc, c0:c0 + n], hT4[:, kc, :], [W, hT4], kc == 0, kc == 7)
                    sg = stg[istg % 4]
                    istg += 1
                    self.evac(sg[0:n, :], ps[0:n, :], sg, ps, scale)
                    self.store(sg, self.FM[r0:r0 + n, cols], sg[0:n, :])
                psk = os.environ.get("KPRE", "").split(",")
                if even:
                    for pr in range(4 if "ql" not in psk else 0):
                        ps = self.nextA()
                        c0 = 1304 + pr * 128
                        for kc in range(8):
                            self.mm(ps, ps[:, :], W[:, kc, c0:c0 + 128], hT4[:, kc, :], [W, hT4], kc == 0, kc == 7)
                        dq = dqs[pr % 2]
                        self.evac(dq[:], ps[:, :], dq, ps)
                        for hh in range(2):
                            h = 2 * pr + hh
                            ps2 = self.nextA()
                            self.mm(ps2, ps2[:, :], wukT[hh * 64:(hh + 1) * 64, pr, :], dq[hh * 64:(hh + 1) * 64, :],
                                    [wukT, dq], True, True)
                            sg = stg[istg % 4]
                            istg += 1
                            self.evac(sg[:], ps2[:, :], sg, ps2, 0.125)
                            self.store(sg, self.FM[R_QLAT + h * 128:R_QLAT + (h + 1) * 128, cols], sg[:])
                    ck = ckst[g4 % 2]
                    for j in range(4 if "tm" not in psk else 0):
                        tt = g4 * 4 + j
                        rows = slice(tt * 128, (tt + 1) * 128)
                        ps = self.nextA()
                        ps_iw = self.nextA()
                        for kc in range(8):
                            self.mm(ps_iw, ps_iw[:, 0:136], hT4[:, kc, j * 128:(j + 1) * 128], W[:, kc, 2392:2528],
                                    [hT4, W], kc == 0, kc == 7)
                        for (c0, n, o0) in ((896, 128, 0), (1152, 152, 128), (1816, 128, 384)):
                            for kc in range(8):
                                self.mm(ps, ps[:, o0:o0 + n], hT4[:, kc, j * 128:(j + 1) * 128], W[:, kc, c0:c0 + n],
                                        [hT4, W], kc == 0, kc == 7)
                        tm, gs, iw, ss = tmst[tt % 2], gst[tt % 2], iwst[tt % 2], sst[tt % 2]
                        traw = traws[tt % 2]
                        p.op("dve", lambda e: e.tensor_copy(out=traw[:], in_=ps[:, :]), [ps], [traw])
                        self.evac(tm[:, 0:256], traw[:, 0:256], tm, traw)
                        p.op("dve", lambda e: e.tensor_tensor(out=junk[:], in0=traw[:, 384:512], in1=traw[:, 384:512], op=ALU.mult), [traw], [junk])
                        p.op("dve", lambda e: e.reduce_sum(out=ss[:, 0:1], in_=junk[:], axis=AX.X), [junk], [ss])
                        self.rstd2(ss[:, 0:1], ss, ss[:, 0:1], ss, 1.0 / 128)
                        p.op("dve", lambda e: e.scalar_tensor_tensor(out=tm[:, 256:384], in0=traw[:, 384:512],
                                                                     scalar=ss[:, 0:1], in1=kvn[:], op0=ALU.mult,
                                                                     op1=ALU.mult), [traw, ss, kvn], [tm])
                        p.op("act", lambda e: e.activation(out=self.gate_all[:, tt, :], in_=traw[:, 256:280], func=AF.Sigmoid), [traw], [self.gate_all])
                        p.op("dve", lambda e: e.tensor_copy(out=self.iw_all[:, tt, :], in_=ps_iw[:, 128:136]), [ps_iw], [self.iw_all])
                        self.store(tm, self.TM[rows, 0:384], tm[:])
                        pb = self.nextB()
                        pbv = pb.t[:].bitcast(BF16)
                        self.tr(pb, pbv[:, 0:128], tm[:, 256:384], [tm])
                        self.evac(ck[:, j * 128:(j + 1) * 128], pbv[:, 0:128], ck, pb)
                    self.store(ck, self.FM[R_CKV:R_CKV + 128, cols], ck[:])
                else:
                    for j in range(4):
                        tt = g4 * 4 + j
                        vs_ = vst[tt % 2]
                        for hf in range(2):
                            ps = self.nextA()
                            c0 = 2048 + hf * 512
                            for kc in range(8):
                                self.mm(ps, ps[:, :], hT4[:, kc, j * 128:(j + 1) * 128], W[:, kc, c0:c0 + 512],
                                        [hT4, W], kc == 0, kc == 7)
                            self.evac(vs_[:, hf * 512:(hf + 1) * 512], ps[:, :], vs_, ps)
                        self.store(vs_, self.TM[tt * 128:(tt + 1) * 128, :], vs_[:])
            p.end_stage()

    def attn(self, qts, lo, hi, q_ap, k_ap, v_ap, dv, special, chunk_extra, done, pts):
        p = self.p
        slots = {}
        for qt in qts:
            self.iO += 1
            slots[qt] = self.psO[self.iO % 4]
        kmin = min(lo(qt) for qt in qts)
        kmax = max(hi(qt) for qt in qts)
        q0 = qts[0]
        for kt in range(kmin, kmax + 1):
            act = [qt for qt in qts if lo(qt) <= kt <= hi(qt)]
            if not act:
                continue
            a, b = act[0] - q0, act[-1] - q0 + 1
            ps = self.nextA()
            extras = []
            if chunk_extra is not None:
                for (l_, r_, bufs) in chunk_extra(kt, act[0], act[-1] + 1):
                    extras.append((ps[:, a * 128:b * 128], l_, r_, bufs))
            for qt in act:
                for (l_, r_, bufs) in special(qt, kt):
                    j = qt - q0
                    extras.append((ps[:, j * 128:(j + 1) * 128], l_, r_, bufs))
            qa, qb_ = q_ap(act[0], act[-1] + 1)
            ka, kb_ = k_ap(kt)
            self.mm(ps, ps[:, a * 128:b * 128], ka, qa, kb_ + qb_, True, len(extras) == 0)
            for n_, (o_, l_, r_, bufs) in enumerate(extras):
                self.mm(ps, o_, l_, r_, bufs, False, n_ == len(extras) - 1)
            self.iPT = getattr(self, "iPT", 0) + 1
            pt = pts[self.iPT % len(pts)]
            p.op("act", lambda e: e.activation(out=pt[:, a * 128:b * 128], in_=ps[:, a * 128:b * 128], func=AF.Exp),
                 [ps], [pt])
            va, vb_ = v_ap(kt)
            for qt in act:
                j = qt - q0
                so = slots[qt]
                self.mm(so, so[:, 0:dv + 1], pt[:, j * 128:(j + 1) * 128], va, [pt] + vb_, kt == lo(qt), kt == hi(qt))
                if kt == hi(qt):
                    done(qt, so)

    def rstd2(self, dst_ap, dstbuf, src_ap, srcbuf, scale):
        p = self.p
        p.op("dve", lambda e: e.tensor_scalar(out=dst_ap, in0=src_ap, scalar1=float(scale), scalar2=EPS,
                                              op0=ALU.mult, op1=ALU.add), [srcbuf], [dstbuf])
        p.op("act", lambda e: e.sqrt(out=dst_ap, in_=dst_ap), [dstbuf], [dstbuf])
        p.op("dve", lambda e: e.reciprocal(out=dst_ap, in_=dst_ap), [dstbuf], [dstbuf])

    def load_split(self, buf, out3, in3, n, parts=4, q="sp"):
        step = (n + parts - 1) // parts
        for a in range(0, n, step):
            b = min(n, a + step)
            self.load(buf, out3[:, a:b], in3[:, a:b], q=q)

    def stage_diff(self, l):
        p, d, S, NT = self.p, self.d, self.S, self.NT
        i = l // 2
        lam_init = 0.8 - 0.6 * math.exp(-0.3 * l)
        p.begin_stage()
        with ExitStack() as st:
            lamt = self.bctile(st, "lamt", d["diff_lam"][i:i + 1].rearrange("o a b -> o (a b)"), 256)
            pr = p.sb("lpr", [128, 128], F32, st)
            sm = p.sb("lsm", [128, 4], F32, st)
            p.op("dve", lambda e: e.tensor_tensor(out=pr[:, 0:64], in0=lamt[:, 0:64], in1=lamt[:, 64:128], op=ALU.mult), [lamt], [pr])
            p.op("dve", lambda e: e.tensor_tensor(out=pr[:, 64:128], in0=lamt[:, 128:192], in1=lamt[:, 192:256], op=ALU.mult), [lamt], [pr])
            p.op("dve", lambda e: e.reduce_sum(out=sm[:, 0:1], in_=pr[:, 0:64], axis=AX.X), [pr], [sm])
            p.op("dve", lambda e: e.reduce_sum(out=sm[:, 1:2], in_=pr[:, 64:128], axis=AX.X), [pr], [sm])
            p.op("act", lambda e: e.activation(out=sm[:, 0:2], in_=sm[:, 0:2], func=AF.Exp), [sm], [sm])
            p.op("dve", lambda e: e.tensor_tensor(out=sm[:, 2:3], in0=sm[:, 1:2], in1=sm[:, 0:1], op=ALU.subtract), [sm], [sm])
            p.op("dve", lambda e: e.tensor_scalar(out=sm[:, 3:4], in0=sm[:, 2:3], scalar1=-lam_init, scalar2=None, op0=ALU.add), [sm], [sm])
            subw = self.bctile(st, "subw", d["diff_subln"][i:i + 1, :], 128)
            p.op("dve", lambda e: e.tensor_scalar(out=subw[:], in0=subw[:], scalar1=1.0 - lam_init, scalar2=None, op0=ALU.mult), [subw], [subw])
            Kh = [p.sb("Kh", [128, S], BF16, st) for _ in range(2)]
            Qh = [p.sb("Qh", [128, S], BF16, st) for _ in range(2)]
            Va = [p.sb("Va", [128, NT, 129], BF16, st) for _ in range(2)]
            for v in Va:
                p.op("pool", lambda e: e.memset(v[:, :, 128:129], 1.0), [], [v])
            pts = [p.sb("pt", [128, 512], BF16, st) for _ in range(3)]
            o0s = [p.sb("o0", [128, 4, 128], F32, st) for _ in range(2)]
            ods = [p.sb("od", [128, 128], F32, st) for _ in range(2)]
            osts = [p.sb("ost", [128, 128], BF16, st) for _ in range(3)]
            rrs = [p.sb("rr", [128, 4], F32, st) for _ in range(4)]
            junk = p.sb("junk", [128, 128], F32, st)
            cnt = [0]
            for h in range(8):
                K_, Q_, V_ = Kh[h % 2], Qh[h % 2], Va[h % 2]
                self.load(K_, K_[:], self.FM[1024 + h * 128:1024 + (h + 1) * 128, :])
                self.load(Q_, Q_[:], self.FM[h * 128:(h + 1) * 128, :])
                self.load_split(V_, V_[:, :, 0:128], self.TM[:, h * 128:(h + 1) * 128].rearrange("(t p) e -> p t e", p=128), NT)
                for c in range(NT // 4):
                    qts = list(range(c * 4, c * 4 + 4))
                    o0c = o0s[c % 2]
                    for m in range(2):
                        col = h * 2 + m

                        def special(qt, kt, col=col):
                            if kt == qt:
                                return [(self.ident_bf[:], self.Tt[:, col, 0, :], [self.ident_bf, self.Tt])]
                            if kt == qt - 1:
                                return [(self.ident_bf[:], self.Tt[:, col, 1, :], [self.ident_bf, self.Tt])]
                            return []

                        def done(qt, so, m=m, q0=qts[0], o0c=o0c, h=h):
                            j = qt - q0
                            cnt[0] += 1
                            r = rrs[cnt[0] % 4]
                            p.op("dve", lambda e: e.reciprocal(out=r[:, 0:1], in_=so[:, 128:129]), [so], [r])
                            if m == 0:
                                p.op("dve", lambda e: e.tensor_scalar(out=o0c[:, j, :], in0=so[:, 0:128], scalar1=r[:, 0:1],
                                                                      scalar2=None, op0=ALU.mult), [so, r], [o0c])
                                return
                            od, ot = ods[cnt[0] % 2], osts[cnt[0] % 3]
                            p.op("dve", lambda e: e.tensor_tensor(out=r[:, 1:2], in0=r[:, 0:1], in1=sm[:, 3:4], op=ALU.mult), [r, sm], [r])
                            p.op("dve", lambda e: e.scalar_tensor_tensor(out=od[:], in0=so[:, 0:128], scalar=r[:, 1:2],
                                                                         in1=o0c[:, j, :], op0=ALU.mult, op1=ALU.add),
                                 [so, r, o0c], [od])
                            p.op("act", lambda e: e.activation(out=junk[:], in_=od[:], func=AF.Square, accum_out=r[:, 2:3]),
                                 [od], [junk, r])
                            self.rstd2(r[:, 2:3], r, r[:, 2:3], r, 1.0 / 128)
                            p.op("dve", lambda e: e.scalar_tensor_tensor(out=ot[:], in0=od[:], scalar=r[:, 2:3], in1=subw[:],
                                                                         op0=ALU.mult, op1=ALU.mult), [od, r, subw], [ot])
                            self.store(ot, self.OS[qt * 128:(qt + 1) * 128, h * 128:(h + 1) * 128], ot[:])

                        self.attn(qts, lambda qt: 0, lambda qt: qt,
                                  lambda a, b, m=m, Q_=Q_: (Q_[m * 64:(m + 1) * 64, a * 128:b * 128], [Q_]),
                                  lambda kt, m=m, K_=K_: (K_[m * 64:(m + 1) * 64, kt * 128:(kt + 1) * 128], [K_]),
                                  lambda kt, V_=V_: (V_[:, kt, :], [V_]),
                                  128, special, None, done, pts)
            p.end_stage()

    def layernorm(self, ut, xo, LG, LB, stats, mv, rs):
        p = self.p
        p.op("dve", lambda e: e.bn_stats(out=stats[:, 0, :], in_=ut[:, 0:512]), [ut], [stats])
        p.op("dve", lambda e: e.bn_stats(out=stats[:, 1, :], in_=ut[:, 512:1024]), [ut], [stats])
        p.op("dve", lambda e: e.bn_aggr(out=mv[:], in_=stats[:].rearrange("p a b -> p (a b)")), [stats], [mv])
        self.rstd2(rs[:, 0:1], rs, mv[:, 1:2], mv, 1.0)
        p.op("dve", lambda e: e.tensor_scalar(out=ut[:], in0=ut[:], scalar1=mv[:, 0:1], scalar2=rs[:, 0:1],
                                              op0=ALU.subtract, op1=ALU.mult), [ut, mv, rs], [ut])
        p.op("pool", lambda e: e.tensor_tensor(out=ut[:], in0=ut[:], in1=LG[:], op=ALU.mult), [ut, LG], [ut])
        p.op("dve", lambda e: e.tensor_tensor(out=xo[:], in0=ut[:], in1=LB[:], op=ALU.add), [ut, LB], [xo])

    def stage_post1(self, l, xin):
        p, d, S, NT = self.p, self.d, self.S, self.NT
        i = l // 2
        w_out = d["ev_w_out"] if l % 2 == 0 else d["od_w_out"]
        p.begin_stage()
        with ExitStack() as st:
            Wo = p.sb("Wo", [128, 8, D], BF16, st)
            self.load(Wo, Wo[:], w_out[i].rearrange("(k p) n -> p k n", p=128), q="pool")
            G1 = self.modtile(st, "G1", 2 * D)
            SH2 = self.modtile(st, "SH2", 3 * D)
            SC2 = self.modtile(st, "SC2", 4 * D)
            LG = self.bctile(st, "LG", d["ln_g"][l, 0:1, :])
            LB = self.bctile(st, "LB", d["ln_b"][l, 0:1, :])
            ots = [p.sb("ot", [128, D], BF16, st) for _ in range(2)]
            oTs = [p.sb("oT", [128, 8, 128], BF16, st) for _ in range(2)]
            xts = [p.sb("xt", [128, D], F32, st) for _ in range(2)]
            uts = [p.sb("ut", [128, D], F32, st) for _ in range(2)]
            x1s = [p.sb("x1", [128, D], F32, st) for _ in range(2)]
            hbs = [p.sb("hb", [128, D], BF16, st) for _ in range(2)]
            hTs = [p.sb("hT", [128, 8, 128], BF16, st) for _ in range(2)]
            sts = [p.sb("stats", [128, 2, 6], F32, st) for _ in range(2)]
            mvs = [p.sb("mv", [128, 2], F32, st) for _ in range(2)]
            rss = [p.sb("rs", [128, 1], F32, st) for _ in range(2)]
            for tt in range(NT):
                k = tt % 2
                rows = slice(tt * 128, (tt + 1) * 128)
                ot, oT, xt, ut, x1, hb, hT = ots[k], oTs[k], xts[k], uts[k], x1s[k], hbs[k], hTs[k]
                self.load(ot, ot[:], self.OS[rows, :])
                self.load(xt, xt[:], xin[rows, :])
                pb = self.nextB()
                pbv = pb.t[:].bitcast(BF16)
                for kc in range(8):
                    self.tr(pb, pbv[:, kc * 128:(kc + 1) * 128], ot[:, kc * 128:(kc + 1) * 128], [ot])
                self.evac(oT[:], pbv.rearrange("p (k t) -> p k t", k=8), oT, pb)
                for hf in range(2):
                    ps = self.nextA()
                    for kc in range(8):
                        self.mm(ps, ps[:, :], oT[:, kc, :], Wo[:, kc, hf * 512:(hf + 1) * 512], [oT, Wo], kc == 0, kc == 7)
                    p.op("dve", lambda e: e.tensor_tensor(out=ut[:, hf * 512:(hf + 1) * 512], in0=ps[:, :],
                                                          in1=G1[:, hf * 512:(hf + 1) * 512], op=ALU.mult), [ps, G1], [ut])
                p.op("dve", lambda e: e.scalar_tensor_tensor(out=ut[:], in0=xt[:], scalar=ALPHA, in1=ut[:], op0=ALU.mult,
                                                              op1=ALU.add), [xt, ut], [ut])
                self.layernorm(ut, x1, LG, LB, sts[k], mvs[k], rss[k])
                self.store(x1, self.X1[rows, :], x1[:])
                p.op("pool", lambda e: e.tensor_tensor(out=ut[:], in0=x1[:], in1=SC2[:], op=ALU.mult), [x1, SC2], [ut])
                p.op("dve", lambda e: e.tensor_tensor(out=hb[:], in0=ut[:], in1=SH2[:], op=ALU.add), [ut, SH2], [hb])
                pb = self.nextB()
                pbv = pb.t[:].bitcast(BF16)
                for kc in range(8):
                    self.tr(pb, pbv[:, kc * 128:(kc + 1) * 128], hb[:, kc * 128:(kc + 1) * 128], [hb])
                self.evac(hT[:], pbv.rearrange("p (k t) -> p k t", k=8), hT, pb)
                self.store(hT, self.H2T[:, rows].rearrange("(k p) t -> p k t", p=128), hT[:])
            p.end_stage()

    def stage_post2(self, l, xout):
        p, d, S, NT = self.p, self.d, self.S, self.NT
        p.begin_stage()
        with ExitStack() as st:
            W1 = p.sb("W1", [128, 8, DFF], BF16, st)
            for kc in range(8):
                self.load(W1, W1[:, kc, :], d["mlp_w1"][l, kc * 128:(kc + 1) * 128, :], q="pool")
            W2 = p.sb("W2", [128, 32, D], BF16, st)
            w2v = d["mlp_w2"][l].rearrange("(f p) n -> p f n", p=128)
            for a in range(0, 32, 8):
                self.load(W2, W2[:, a:a + 8, :], w2v[:, a:a + 8, :], q="pool")
            G2 = self.modtile(st, "G2", 5 * D)
            LG = self.bctile(st, "LG", d["ln_g"][l, 1:2, :])
            LB = self.bctile(st, "LB", d["ln_b"][l, 1:2, :])
            h2s = [p.sb("h2", [128, 8, 256], BF16, st) for _ in range(2)]
            aT = p.sb("aT", [128, 32, 256], BF16, st)
            rls = [p.sb("rl", [128, 256], F32, st) for _ in range(2)]
            xts = [p.sb("xt", [128, D], F32, st) for _ in range(2)]
            uts = [p.sb("ut", [128, D], F32, st) for _ in range(2)]
            sts = [p.sb("stats", [128, 2, 6], F32, st) for _ in range(2)]
            mvs = [p.sb("mv", [128, 2], F32, st) for _ in range(2)]
            rss = [p.sb("rs", [128, 1], F32, st) for _ in range(2)]
            for g2 in range(S // 256):
                h2 = h2s[g2 % 2]
                self.load(h2, h2[:], self.H2T[:, g2 * 256:(g2 + 1) * 256].rearrange("(k p) t -> p k t", p=128))
                for fc in range(32):
                    ps = self.nextA()
                    for kc in range(8):
                        self.mm(ps, ps[:, 0:256], W1[:, kc, fc * 128:(fc + 1) * 128], h2[:, kc, :], [W1, h2], kc == 0, kc == 7)
                    rl = rls[fc % 2]
                    p.op("act", lambda e: e.activation(out=rl[:], in_=ps[:, 0:256], func=AF.Relu), [ps], [rl])
                    eng = "dve" if fc % 2 else "pool"
                    p.op(eng, lambda e: e.tensor_tensor(out=aT[:, fc, :], in0=rl[:], in1=rl[:], op=ALU.mult), [rl], [aT])
                for j in range(2):
                    tt = g2 * 2 + j
                    k = tt % 2
                    rows = slice(tt * 128, (tt + 1) * 128)
                    xt, ut = xts[k], uts[k]
                    self.load(xt, xt[:], self.X1[rows, :])
                    for hf in range(2):
                        ps = self.psOt[(tt * 2 + hf) % 4]
                        for fc in range(32):
                            self.mm(ps, ps[:, :], aT[:, fc, j * 128:(j + 1) * 128], W2[:, fc, hf * 512:(hf + 1) * 512],
                                    [aT, W2], fc == 0, fc == 31)
                        p.op("dve", lambda e: e.tensor_tensor(out=ut[:, hf * 512:(hf + 1) * 512], in0=ps[:, :],
                                                              in1=G2[:, hf * 512:(hf + 1) * 512], op=ALU.mult), [ps, G2], [ut])
                    p.op("dve", lambda e: e.scalar_tensor_tensor(out=ut[:], in0=xt[:], scalar=ALPHA, in1=ut[:], op0=ALU.mult,
                                                                  op1=ALU.add), [xt, ut], [ut])
                    self.layernorm(ut, xt, LG, LB, sts[k], mvs[k], rss[k])
                    self.store(xt, xout[rows, :], xt[:])
            p.end_stage()

    def stage_cmp(self, l):
        p, d, S, NT, NC, NCB = self.p, self.d, self.S, self.NT, self.NC, self.NCB
        i = l // 2
        p.begin_stage()
        with ExitStack() as st:
            krs = [p.sb("kr", [128, S], BF16, st) for _ in range(2)]
            hidT = p.sb("hidT", [128, 2, NCB * 128], BF16, st)
            xh = p.sb("xh", [128, 512], F32, st)
            t1 = p.sb("t1", [128, 512], F32, st)
            kcst = p.sb("kcst", [64, NCB * 128], BF16, st)
            vcst = p.sb("vcst", [128, NCB, 64], BF16, st)
            p.op("dve", lambda e: e.memset(kcst[:], 0.0), [], [kcst])
            p.op("dve", lambda e: e.memset(vcst[:], 0.0), [], [vcst])
            for kv in range(2):
                nm_ = "k" if kv == 0 else "v"
                w1 = p.sb("w1", [128, 16, 256], BF16, st)
                self.load(w1, w1[:], d["nsa_w1_" + nm_][i].rearrange("(c p) h -> p c h", p=128), q="pool")
                w2 = p.sb("w2", [128, 2, 64], BF16, st)
                self.load(w2, w2[:], d["nsa_w2_" + nm_][i].rearrange("(c p) e -> p c e", p=128), q="pool")
                pef = p.sb("pef", [128, 16], F32, st)
                for two in range(2):
                    self.load(pef, pef[two * 64:(two + 1) * 64, :],
                              d["nsa_pe_" + nm_][i].rearrange("(c two) e -> two e c", two=2)[two],
                              allow_slow_non_contiguous=True)
                peb = p.sb("peb", [128, 16, 32], BF16, st)
                for c in range(16):
                    p.op("dve", lambda e: e.tensor_scalar(out=peb[:, c, :], in0=self.ones_bf[:, 0:32], scalar1=pef[:, c:c + 1],
                                                          scalar2=None, op0=ALU.mult), [self.ones_bf, pef], [peb])
                biasT = p.sb("biasT", [128, 2], F32, st)
                pb = self.nextB()
                for hc in range(2):
                    for c in range(16):
                        self.mm(pb, pb[:, hc * 32:(hc + 1) * 32], w1[:, c, hc * 128:(hc + 1) * 128], peb[:, c, :], [w1, peb], c == 0, c == 15)
                for hc in range(2):
                    self.evac(biasT[:, hc:hc + 1], pb[:, hc * 32:hc * 32 + 1], biasT, pb, eng="dve")
                for g in range(2):
                    kr = krs[g]
                    r0 = R_KVC + kv * 128 + g * 64
                    self.load(kr, kr[0:64, :], self.FM[r0:r0 + 64, :])
                    self.load(kr, kr[64:128, 0:S - 1], self.FM[r0:r0 + 64, 1:S])
                    for hc in range(2):
                        ps = self.nextA()
                        for c in range(16):
                            self.mm(ps, ps[:, 0:NC], w1[:, c, hc * 128:(hc + 1) * 128],
                                    kr[:, 2 * c:2 * c + 16 * (NC - 1) + 1:16], [w1, kr], c == 0, c == 15)
                        p.op("act", lambda e: e.activation(out=xh[:, 0:NC], in_=ps[:, 0:NC], func=AF.Identity,
                                                           bias=biasT[:, hc:hc + 1]), [ps, biasT], [xh])
                        p.op("dve", lambda e: e.tensor_tensor(out=t1[:, 0:NC], in0=xh[:, 0:NC], in1=xh[:, 0:NC], op=ALU.mult), [xh], [t1])
                        p.op("dve", lambda e: e.tensor_scalar(out=t1[:, 0:NC], in0=t1[:, 0:NC], scalar1=0.044715, scalar2=1.0,
                                                              op0=ALU.mult, op1=ALU.add), [t1], [t1])
                        p.op("dve", lambda e: e.tensor_tensor(out=t1[:, 0:NC], in0=t1[:, 0:NC], in1=xh[:, 0:NC], op=ALU.mult), [t1, xh], [t1])
                        p.op("act", lambda e: e.activation(out=t1[:, 0:NC], in_=t1[:, 0:NC], func=AF.Tanh,
                                                           scale=0.7978845608028654), [t1], [t1])
                        p.op("dve", lambda e: e.tensor_scalar(out=t1[:, 0:NC], in0=t1[:, 0:NC], scalar1=0.5, scalar2=0.5,
                                                              op0=ALU.mult, op1=ALU.add), [t1], [t1])
                        p.op("dve", lambda e: e.tensor_tensor(out=hidT[:, hc, 0:NC], in0=t1[:, 0:NC], in1=xh[:, 0:NC], op=ALU.mult),
                             [t1, xh], [hidT])
                    if kv == 0:
                        ps = self.nextA()
                        for hc in range(2):
                            self.mm(ps, ps[0:64, 0:NC], w2[:, hc, :], hidT[:, hc, 0:NC], [w2, hidT], hc == 0, hc == 1)
                        self.evac(kcst[:, 0:NC], ps[0:64, 0:NC], kcst, ps)
                        self.store(kcst, self.KC[g], kcst[:])
                    else:
                        for nb in range(NCB):
                            n0 = nb * 128
                            nn = min(128, NC - n0)
                            ps = self.nextA()
                            for hc in range(2):
                                self.mm(ps, ps[0:nn, 0:64], hidT[:, hc, n0:n0 + nn], w2[:, hc, :], [hidT, w2], hc == 0, hc == 1)
                            self.evac(vcst[0:nn, nb, :], ps[0:nn, 0:64], vcst, ps)
                        self.store(vcst, self.VC[g].rearrange("(b p) e -> p b e", p=128), vcst[:])
            p.end_stage()

    def stage_nsa(self, l):
        p, d, S, NT, NB, NC, NCB = self.p, self.d, self.S, self.NT, self.NB, self.NC, self.NCB
        p.begin_stage()
        with ExitStack() as st:
            esel = p.sb("esel", [128, NT * 128], BF16, st)
            self.load(esel, esel[:], d["esel"][:, :])
            gate = self.gate_all
            kc2 = p.sb("kc2", [128, NCB * 128], BF16, st)
            vc = p.sb("vc", [128, NCB, 64], BF16, st)
            ks2 = p.sb("ks2", [128, S], BF16, st)
            kw2 = p.sb("kw2", [128, S], BF16, st)
            vsa = p.sb("vsa", [128, NT, 65], BF16, st)
            vwa = p.sb("vwa", [128, NT, 65], BF16, st)
            p.op("pool", lambda e: e.memset(vsa[:, :, 64:65], 1.0), [], [vsa])
            p.op("pool", lambda e: e.memset(vwa[:, :, 64:65], 1.0), [], [vwa])
            Qp = [p.sb("Qp", [128, S], BF16, st) for _ in range(2)]
            pts = [p.sb("pt", [128, 512], BF16, st) for _ in range(3)]
            accs = [p.sb("acc", [128, 4, 4, 64], F32, st) for _ in range(2)]
            nmTs = [p.sb("nmT", [128, 512], BF16, st) for _ in range(2)]
            Pbs = [p.sb("Pb", [128, NCB * 128], BF16, st) for _ in range(2)]
            Pg = p.sb("Pg", [128, NCB * 128], F32, st)
            PTs = [p.sb("PT", [128, NCB, 128], BF16, st) for _ in range(2)]
            PgT = p.sb("PgT", [128, NCB, 128], F32, st)
            sc2 = p.sb("sc2", [128, NB], F32, st)
            sc3 = p.sb("sc3", [128, NB], F32, st)
            m1 = p.sb("m1", [128, 8], F32, st)
            m2 = p.sb("m2", [128, 8], F32, st)
            nm = p.sb("nm", [128, 128], BF16, st)
            p.op("dve", lambda e: e.memset(nm[:], 0.0), [], [nm])
            rrs = [p.sb("rr", [128, 4], F32, st) for _ in range(4)]
            osts = [p.sb("ost", [128, 256], BF16, st) for _ in range(2)]
            cnt = [0]
            for g in range(2):
                for b in Pbs + [Pg]:
                    p.op("dve", lambda e: e.memset(b[:], 0.0), [], [b])
                for half in range(2):
                    hs = slice(half * 64, (half + 1) * 64)
                    self.load(kc2, kc2[hs, :], self.KC[g])
                    self.load(ks2, ks2[hs, :], self.FM[R_KS + g * 64:R_KS + (g + 1) * 64, :])
                    self.load(kw2, kw2[hs, :], self.FM[R_KW + g * 64:R_KW + (g + 1) * 64, :])
                self.load(vc, vc[:], self.VC[g].rearrange("(b p) e -> p b e", p=128))
                self.load_split(vsa, vsa[:, :, 0:64], self.TM[:, g * 64:(g + 1) * 64].rearrange("(t p) e -> p t e", p=128), NT)
                self.load_split(vwa, vwa[:, :, 0:64], self.TM[:, 128 + g * 64:128 + (g + 1) * 64].rearrange("(t p) e -> p t e", p=128), NT)
                for pp in range(2):
                    r0 = R_QN + (g * 4 + 2 * pp) * 64
                    self.load(Qp[pp], Qp[pp][:], self.FM[r0:r0 + 128, :])
                for c in range(NT // 4):
                    qts = list(range(c * 4, c * 4 + 4))
                    q0 = qts[0]
                    acc, nmT = accs[c % 2], nmTs[c % 2]
                    for j, qt in enumerate(qts):
                        lim = min(NC, 8 * qt + 7)
                        nblk = (lim + 127) // 128
                        n_lo = max(0, 8 * qt - 8)
                        n_hi = min(lim, 8 * qt + 7)
                        mm_lo = n_lo - (8 * qt - 8)
                        mm_hi = n_hi - (8 * qt - 8)
                        for hh in range(4):
                            h = g * 4 + hh
                            qs = slice((hh % 2) * 64, (hh % 2) * 64 + 64)
                            Q_ = Qp[hh // 2]
                            ps = self.nextA()
                            self.mm(ps, ps[:, 0:lim], Q_[qs, qt * 128:(qt + 1) * 128], kc2[qs, 0:lim], [Q_, kc2], True, False)
                            self.mm(ps, ps[:, n_lo:n_hi], self.ident_bf[:], self.CBt[:, h, mm_lo:mm_hi], [self.ident_bf, self.CBt], False, True)
                            cnt[0] += 1
                            r = rrs[cnt[0] % 4]
                            Pb = Pbs[cnt[0] % 2]
                            PT = PTs[cnt[0] % 2]
                            p.op("act", lambda e: e.activation(out=Pb[:, 0:lim], in_=ps[:, 0:lim], func=AF.Exp, accum_out=r[:, 0:1]),
                                 [ps], [Pb, r])
                            p.op("dve", lambda e: e.tensor_scalar(out=r[:, 0:1], in0=r[:, 0:1], scalar1=1e-30, scalar2=None, op0=ALU.max), [r], [r])
                            p.op("dve", lambda e: e.reciprocal(out=r[:, 0:1], in_=r[:, 0:1]), [r], [r])
                            if hh == 0:
                                p.op("dve", lambda e: e.tensor_scalar(out=Pg[:, 0:lim], in0=Pb[:, 0:lim], scalar1=r[:, 0:1], scalar2=None,
                                                                      op0=ALU.mult), [Pb, r], [Pg])
                            else:
                                p.op("dve", lambda e: e.scalar_tensor_tensor(out=Pg[:, 0:lim], in0=Pb[:, 0:lim], scalar=r[:, 0:1],
                                                                             in1=Pg[:, 0:lim], op0=ALU.mult, op1=ALU.add), [Pb, r, Pg], [Pg])
                            pb = self.nextB()
                            pbv = pb.t[:].bitcast(BF16)
                            for nb in range(nblk):
                                self.tr(pb, pbv[:, nb * 128:(nb + 1) * 128], Pb[:, nb * 128:(nb + 1) * 128], [Pb])
                            self.evac(PT[:, 0:nblk, :], pbv[:, 0:nblk * 128].rearrange("p (b t) -> p b t", b=nblk), PT, pb)
                            self.iO += 1
                            so = self.psO[self.iO % 4]
                            for nb in range(nblk):
                                self.mm(so, so[:, 0:64], PT[:, nb, :], vc[:, nb, :], [PT, vc], nb == 0, nb == nblk - 1)
                            p.op("dve", lambda e: e.tensor_tensor(out=r[:, 1:2], in0=r[:, 0:1], in1=gate[:, qt, h * 3:h * 3 + 1], op=ALU.mult),
                                 [r, gate], [r])
                            p.op("dve", lambda e: e.tensor_scalar(out=acc[:, hh, j, :], in0=so[:, 0:64], scalar1=r[:, 1:2], scalar2=None,
                                                                  op0=ALU.mult), [so, r], [acc])
                        pb = self.nextB()
                        for nb in range(nblk):
                            self.tr(pb, pb[:, nb * 128:(nb + 1) * 128], Pg[:, nb * 128:(nb + 1) * 128], [Pg], f32=True)
                        self.evac(PgT[:, 0:nblk, :], pb[:, 0:nblk * 128].rearrange("p (b t) -> p b t", b=nblk), PgT, pb)
                        ps = self.nextA()
                        for nb in range(nblk):
                            self.mm(ps, ps[:, 0:NB], PgT[:, nb, :], self.imp[:, nb * NB:(nb + 1) * NB], [PgT, self.imp], nb == 0, nb == nblk - 1)
                        o_ = 128 - 2 * qt
                        p.op("dve", lambda e: e.tensor_tensor(out=sc2[:], in0=ps[:, 0:NB], in1=self.sela[:, o_:o_ + NB], op=ALU.mult),
                             [ps, self.sela], [sc2])
                        p.op("dve", lambda e: e.tensor_tensor(out=sc2[:], in0=sc2[:], in1=self.selb[:, o_:o_ + NB], op=ALU.add),
                             [sc2, self.selb], [sc2])
                        p.op("dve", lambda e: e.memset(sc2[:, 0:1], 3e9), [], [sc2])
                        p.op("dve", lambda e: e.max(out=m1[:], in_=sc2[:]), [sc2], [m1])
                        p.op("dve", lambda e: e.match_replace(out=sc3[:], in_to_replace=m1[:], in_values=sc2[:], imm_value=-3e9),
                             [m1, sc2], [sc3])
                        p.op("dve", lambda e: e.max(out=m2[:], in_=sc3[:]), [sc3], [m2])
                        p.op("dve", lambda e: e.tensor_scalar(out=nm[:, 0:NB], in0=sc2[:], scalar1=m2[:, 7:8], scalar2=NEG,
                                                              op0=ALU.is_lt, op1=ALU.mult), [sc2, m2], [nm])
                        pb = self.nextB()
                        pbv = pb.t[:].bitcast(BF16)
                        self.tr(pb, pbv[:, 0:128], nm[:, :], [nm])
                        self.evac(nmT[:, j * 128:(j + 1) * 128], pbv[:, 0:128], nmT, pb)
                    for br in range(2):
                        for hh in range(4):
                            h = g * 4 + hh
                            qs = slice((hh % 2) * 64, (hh % 2) * 64 + 64)
                            Q_ = Qp[hh // 2]
                            K_ = ks2 if br == 0 else kw2
                            V_ = vsa if br == 0 else vwa

                            def special(qt, kt, h=h, br=br):
                                ib = self.ident_bf
                                if kt == qt:
                                    return [(ib[:], self.Tt[:, h, 0, :], [ib, self.Tt])]
                                if kt == qt - 1:
                                    return [(ib[:], self.Tt[:, h, 1, :], [ib, self.Tt])]
                                if br == 1 and kt == qt - 4:
                                    return [(ib[:], self.tw4[:], [ib, self.tw4])]
                                return []

                            def cextra(kt, qa, qb, nmT=nmT, q0=q0):
                                return [(esel[:, kt * 128:(kt + 1) * 128], nmT[:, (qa - q0) * 128:(qb - q0) * 128], [esel, nmT])]

                            def done(qt, so, h=h, hh=hh, br=br, acc=acc, q0=q0):
                                j = qt - q0
                                cnt[0] += 1
                                r = rrs[cnt[0] % 4]
                                p.op("dve", lambda e: e.reciprocal(out=r[:, 0:1], in_=so[:, 64:65]), [so], [r])
                                p.op("dve", lambda e: e.tensor_tensor(out=r[:, 1:2], in0=r[:, 0:1], in1=gate[:, qt, h * 3 + 1 + br:h * 3 + 2 + br],
                                                                      op=ALU.mult), [r, gate], [r])
                                p.op("dve", lambda e: e.scalar_tensor_tensor(out=acc[:, hh, j, :], in0=so[:, 0:64], scalar=r[:, 1:2],
                                                                             in1=acc[:, hh, j, :], op0=ALU.mult, op1=ALU.add),
                                     [so, r, acc], [acc])

                            lo = (lambda qt: 0) if br == 0 else (lambda qt: max(0, qt - 4))
                            self.attn(qts, lo, lambda qt: qt,
                                      lambda a, b, Q_=Q_, qs=qs: (Q_[qs, a * 128:b * 128], [Q_]),
                                      lambda kt, K_=K_, qs=qs: (K_[qs, kt * 128:(kt + 1) * 128], [K_]),
                                      lambda kt, V_=V_: (V_[:, kt, :], [V_]),
                                      64, special, cextra if br == 0 else None, done, pts)
                    for j, qt in enumerate(qts):
                        ot = osts[qt % 2]
                        p.op("pool", lambda e: e.tensor_copy(out=ot[:].rearrange("p (h e) -> p h e", h=4), in_=acc[:, :, j, :]), [acc], [ot])
                        self.store(ot, self.OS[qt * 128:(qt + 1) * 128, g * 256:(g + 1) * 256], ot[:])
            p.end_stage()

    def stage_dsa(self, l):
        p, d, S, NT = self.p, self.d, self.S, self.NT
        i = l // 2
        NIT = 13
        p.begin_stage()
        with ExitStack() as st:
            ckvT = p.sb("ckvT", [128, S], BF16, st)
            self.load(ckvT, ckvT[:], self.FM[R_CKV:R_CKV + 128, :])
            ckva = p.sb("ckva", [128, NT, 129], BF16, st)
            p.op("pool", lambda e: e.memset(ckva[:, :, 128:129], 1.0), [], [ckva])
            self.load_split(ckva, ckva[:, :, 0:128], self.TM[:, 256:384].rearrange("(t p) e -> p t e", p=128), NT)
            ik2 = p.sb("ik2", [128, S], BF16, st)
            for half in range(2):
                self.load(ik2, ik2[half * 64:(half + 1) * 64, :], self.FM[R_IK:R_IK + 64, :])
            wuv = p.sb("wuv", [128, 512], BF16, st)
            self.load(wuv, wuv[:], d["dsa_w_uv"][i].rearrange("r h e -> r (h e)"), q="pool")
            iw = self.iw_all
            isc = p.sb("isc", [128, S], F32, st)
            nm = p.sb("nm", [128, S], BF16, st)
            nmT = p.sb("nmT", [128, NT, 256], BF16, st)
            iqs = [p.sb("iq", [128, 4, 256], BF16, st) for _ in range(2)]
            qls = [p.sb("ql", [128, 8, 256], BF16, st) for _ in range(2)]
            rls = [p.sb("rl", [128, 512], F32, st) for _ in range(2)]
            pts = [p.sb("pt", [128, 256], BF16, st) for _ in range(3)]
            sml = p.sb("sml", [128, 8], F32, st)
            thrc = p.sb("thrc", [128, 1], F32, st)
            p.op("dve", lambda e: e.memset(thrc[:], -1e29), [], [thrc])
            rrs = [p.sb("rr", [128, 2], F32, st) for _ in range(4)]
            ols = [p.sb("ol", [128, 128], BF16, st) for _ in range(2)]
            olTs = [p.sb("olT", [128, 128], BF16, st) for _ in range(2)]
            osts = [p.sb("ost", [128, 2, 512], BF16, st) for _ in range(2)]
            cnt = [0]
            for c2 in range(NT // 2):
                qts = [2 * c2, 2 * c2 + 1]
                q0 = qts[0]
                cols = slice(q0 * 128, q0 * 128 + 256)
                iq, ql, ost = iqs[c2 % 2], qls[c2 % 2], osts[c2 % 2]
                self.load(iq, iq[:], self.FM[R_IQ:R_IQ + 512, cols].rearrange("(a p) t -> p a t", p=128))
                self.load(ql, ql[:], self.FM[R_QLAT:R_QLAT + 1024, cols].rearrange("(h p) t -> p h t", p=128))
                for j, qt in enumerate(qts):
                    L = (qt + 1) * 128
                    for c0 in range(0, L, 512):
                        c1 = min(L, c0 + 512)
                        w = c1 - c0
                        for jh in range(8):
                            hs = slice((jh % 2) * 64, (jh % 2) * 64 + 64)
                            ps = self.nextA()
                            self.mm(ps, ps[:, 0:w], iq[hs, jh // 2, j * 128:(j + 1) * 128], ik2[hs, c0:c1], [iq, ik2], True, True)
                            cnt[0] += 1
                            rl = rls[cnt[0] % 2]
                            p.op("act", lambda e: e.activation(out=rl[:, 0:w], in_=ps[:, 0:w], func=AF.Relu), [ps], [rl])
                            if jh == 0:
                                p.op("dve", lambda e: e.tensor_scalar(out=isc[:, c0:c1], in0=rl[:, 0:w], scalar1=iw[:, qt, 0:1], scalar2=None,
                                                                      op0=ALU.mult), [rl, iw], [isc])
                            else:
                                p.op("dve", lambda e: e.scalar_tensor_tensor(out=isc[:, c0:c1], in0=rl[:, 0:w], scalar=iw[:, qt, jh:jh + 1],
                                                                             in1=isc[:, c0:c1], op0=ALU.mult, op1=ALU.add), [rl, iw, isc], [isc])
                    if qt >= 2:
                        p.op("dve", lambda e: e.tensor_reduce(out=sml[:, 0:1], in_=isc[:, 0:L], axis=AX.X, op=ALU.max), [isc], [sml])
                        p.op("dve", lambda e: e.tensor_reduce(out=sml[:, 1:2], in_=isc[:, 0:L], axis=AX.X, op=ALU.min), [isc], [sml])
                    p.op("dve", lambda e: e.tensor_tensor(out=isc[:, qt * 128:L], in0=isc[:, qt * 128:L], in1=self.tri[:], op=ALU.add),
                         [isc, self.tri], [isc])
                    if qt >= 2:
                        p.op("dve", lambda e: e.tensor_copy(out=sml[:, 2:3], in_=sml[:, 1:2]), [sml], [sml])
                        p.op("dve", lambda e: e.tensor_tensor(out=sml[:, 3:4], in0=sml[:, 0:1], in1=sml[:, 1:2], op=ALU.subtract), [sml], [sml])
                        p.op("dve", lambda e: e.tensor_scalar(out=sml[:, 3:4], in0=sml[:, 3:4], scalar1=1.0001, scalar2=1e-6,
                                                              op0=ALU.mult, op1=ALU.add), [sml], [sml])
                        for k in range(NIT):
                            f = 2.0 ** -(k + 1)
                            p.op("dve", lambda e: e.tensor_scalar(out=sml[:, 4:5], in0=sml[:, 3:4], scalar1=f, scalar2=None, op0=ALU.mult), [sml], [sml])
                            p.op("dve", lambda e: e.tensor_tensor(out=sml[:, 5:6], in0=sml[:, 2:3], in1=sml[:, 4:5], op=ALU.add), [sml], [sml])
                            p.op("dve", lambda e: e.tensor_scalar(out=nm[:, 0:L], in0=isc[:, 0:L], scalar1=sml[:, 5:6], scalar2=None,
                                                                  op0=ALU.is_ge, op1=ALU.add, accum_out=sml[:, 6:7]), [isc, sml], [nm, sml])
                            p.op("dve", lambda e: e.scalar_tensor_tensor(out=sml[:, 7:8], in0=sml[:, 6:7], scalar=255.5, in1=sml[:, 4:5],
                                                                         op0=ALU.is_ge, op1=ALU.mult), [sml], [sml])
                            p.op("dve", lambda e: e.tensor_tensor(out=sml[:, 2:3], in0=sml[:, 2:3], in1=sml[:, 7:8], op=ALU.add), [sml], [sml])
                        thr = sml[:, 2:3]
                        tb = sml
                    else:
                        thr = thrc[:, 0:1]
                        tb = thrc
                    p.op("dve", lambda e: e.tensor_scalar(out=nm[:, 0:L], in0=isc[:, 0:L], scalar1=thr, scalar2=NEG, op0=ALU.is_lt, op1=ALU.mult),
                         [isc, tb], [nm])
                    for k0 in range(0, qt + 1, 8):
                        k1 = min(qt + 1, k0 + 8)
                        pb = self.nextB()
                        pbv = pb.t[:].bitcast(BF16)
                        for kt in range(k0, k1):
                            self.tr(pb, pbv[:, (kt - k0) * 128:(kt - k0 + 1) * 128], nm[:, kt * 128:(kt + 1) * 128], [nm])
                        self.evac(nmT[:, k0:k1, j * 128:(j + 1) * 128],
                                  pbv[:, 0:(k1 - k0) * 128].rearrange("p (k t) -> p k t", k=k1 - k0), nmT, pb)
                for h in range(8):
                    col = 8 + h

                    def special(qt, kt, col=col):
                        ib = self.ident_bf
                        if kt == qt:
                            return [(ib[:], self.Tt[:, col, 0, :], [ib, self.Tt])]
                        if kt == qt - 1:
                            return [(ib[:], self.Tt[:, col, 1, :], [ib, self.Tt])]
                        return []

                    def cextra(kt, qa, qb, q0=q0):
                        return [(self.ident_bf[:], nmT[:, kt, (qa - q0) * 128:(qb - q0) * 128], [self.ident_bf, nmT])]

                    def done(qt, so, h=h, q0=q0, ost=ost):
                        j = qt - q0
                        cnt[0] += 1
                        r, ol, olT = rrs[cnt[0] % 4], ols[cnt[0] % 2], olTs[cnt[0] % 2]
                        p.op("dve", lambda e: e.reciprocal(out=r[:, 0:1], in_=so[:, 128:129]), [so], [r])
                        p.op("dve", lambda e: e.tensor_scalar(out=ol[:], in0=so[:, 0:128], scalar1=r[:, 0:1], scalar2=None, op0=ALU.mult),
                             [so, r], [ol])
                        pb = self.nextB()
                        pbv = pb.t[:].bitcast(BF16)
                        self.tr(pb, pbv[:, 0:128], ol[:], [ol])
                        self.evac(olT[:], pbv[:, 0:128], olT, pb)
                        pb2 = self.nextB()
                        self.mm(pb2, pb2[:, 0:64], olT[:], wuv[:, h * 64:(h + 1) * 64], [olT, wuv], True, True)
                        self.evac(ost[:, j, h * 64:(h + 1) * 64], pb2[:, 0:64], ost, pb2)

                    self.attn(qts, lambda qt: 0, lambda qt: qt,
                              lambda a, b, h=h, ql=ql, q0=q0: (ql[:, h, (a - q0) * 128:(b - q0) * 128], [ql]),
                              lambda kt: (ckvT[:, kt * 128:(kt + 1) * 128], [ckvT]),
                              lambda kt: (ckva[:, kt, :], [ckva]),
                              128, special, cextra, done, pts)
                for j, qt in enumerate(qts):
                    self.store(ost, self.OS[qt * 128:(qt + 1) * 128, 512:1024], ost[:, j, :])
            p.end_stage()


_CACHE = {}


def _get_prog(S, layers, shapes):
    key = (S, tuple(layers))
    if key not in _CACHE:
        kb = KB(S, layers, shapes)
        kb.build()
        _CACHE[key] = kb
    return _CACHE[key]


def run_kernel(inputs, S, layers, nb):
    shapes = {n: np.asarray(inputs[n]).shape for n in W_NAMES}
    kb = _get_prog(S, layers, shapes)
    base = {n: np.ascontiguousarray(np.asarray(inputs[n], dtype=np.float32)) for n in W_NAMES}
    base.update(kb.consts)
    x = np.asarray(inputs["x"], dtype=np.float32)
    c = np.asarray(inputs["c"], dtype=np.float32)
    ncore = 2 * nb
    in_maps = []
    for k in range(ncore):
        b = k // 2
        m = dict(base)
        m["x"] = np.ascontiguousarray(x[b])
        m["c"] = np.ascontiguousarray(c[b:b + 1])
        in_maps.append(m)
    res = run_bass_kernel_spmd(kb.nc, in_maps, core_ids=list(range(ncore)))
    return np.stack([np.asarray(res.results[2 * b]["y"]) for b in range(nb)], axis=0).astype(np.float32)


def kernel(**inputs):
    return run_kernel(inputs, 8192, [0, 1, 2, 3], 4)
```

```python
import os
import numpy as np
import ml_dtypes
from contextlib import ExitStack
import concourse.bass as bass
import concourse.mybir as mybir
from concourse.bass_utils import run_bass_kernel_spmd

F32 = mybir.dt.float32
BF16 = mybir.dt.bfloat16
AF = mybir.ActivationFunctionType
ALU = mybir.AluOpType
AX = mybir.AxisListType
NPBF = ml_dtypes.bfloat16


class Buf:
    __slots__ = ("name", "t", "w", "rs", "sem", "cnt", "skind")

    def __init__(self, name, t):
        self.name = name
        self.t = t
        self.w = None
        self.rs = []
        self.sem = None
        self.cnt = 0
        self.skind = None

    def __getitem__(self, idx):
        return self.t[idx]

    def view(self, name, ap):
        return Buf(name, ap)


class Prog:
    ENGS = ("pe", "act", "dve", "pool", "sp")

    def __init__(self, nc, stack):
        self.nc = nc
        self.stack = stack
        self.eng = {"pe": nc.tensor, "act": nc.scalar, "dve": nc.vector,
                    "pool": nc.gpsimd, "sp": nc.sync}
        self.esem = {e: stack.enter_context(nc.semaphore("es_" + e)) for e in self.ENGS}
        self.tick = {e: 0 for e in self.ENGS}
        self.waited = {e: {} for e in self.ENGS}
        self.free_sems = {"hw": [], "sw": []}
        self.all = []
        self.cur = None
        self.nbuf = 0
        self.n_inst = 0
        self.bar_t = stack.enter_context(nc.sbuf_tensor("bar_t", [128, 8], F32))

    def sb(self, name, shape, dt, stack=None):
        self.nbuf += 1
        t = (stack or self.stack).enter_context(
            self.nc.sbuf_tensor("%s_%d" % (name, self.nbuf), list(shape), dt))
        nbytes = int(np.prod(shape[1:])) * (4 if dt == F32 else 2)
        pad = (-nbytes) % 64
        if pad:
            (stack or self.stack).enter_context(
                self.nc.sbuf_tensor("pad_%d" % self.nbuf, [128, pad // 2], BF16))
        b = Buf(name, t)
        self.all.append(b)
        if stack is not None and self.cur is not None:
            self.cur.append(b)
        return b

    def ps(self, name, shape, dt=F32, stack=None):
        self.nbuf += 1
        t = (stack or self.stack).enter_context(
            self.nc.psum_tensor("%s_%d" % (name, self.nbuf), list(shape), dt))
        b = Buf(name, t)
        self.all.append(b)
        return b

    def mk(self, name, ap):
        b = Buf(name, ap)
        self.all.append(b)
        return b

    def begin_stage(self):
        self.cur = []

    def end_stage(self):
        self.barrier()
        for b in self.cur:
            if b.sem is not None:
                self.free_sems[b.skind].append((b.sem, b.cnt))
                b.sem = None
        ids = set(id(b) for b in self.cur)
        self.all = [b for b in self.all if id(b) not in ids]
        self.cur = None

    def _dsem(self, b, kind):
        assert b.skind in (None, kind), "buffer %s used by both HW and SW DMA queues" % b.name
        b.skind = kind
        if b.sem is None:
            if self.free_sems[kind]:
                b.sem, b.cnt = self.free_sems[kind].pop()
                if os.environ.get("KDBG"):
                    print("REUSE", b.sem, b.cnt, "->", b.name)
            else:
                b.sem = self.stack.enter_context(self.nc.semaphore("ds_%s_%d" % (b.name, self.nbuf)))
                self.nbuf += 1
                b.cnt = 0
        return b.sem


    def _wait(self, e, dep):
        if dep is None:
            return
        if dep[0] == "c":
            _, pe_, tk = dep
            key = ("c", pe_)
            sem = self.esem[pe_]
            val = tk
        else:
            _, sem, val = dep
            key = ("d", id(sem))
        w = self.waited[e]
        if w.get(key, -1) >= val:
            return
        w[key] = val
        self.eng[e].wait_ge(sem, val)
        self.n_inst += 1

    def _deps(self, e, reads, writes):
        deps = []
        for b in reads:
            if b.w is not None:
                if not (b.w[0] == "c" and b.w[1] == e and e == "pe"):
                    deps.append(b.w)
        for b in writes:
            if b.w is not None and not (b.w[0] == "c" and b.w[1] == e and e == "pe"):
                deps.append(b.w)
            for r in b.rs:
                if not (r[0] == "c" and r[1] == e and e == "pe"):
                    deps.append(r)
        for d in self._compact(deps):
            self._wait(e, d)

    def op(self, e, fn, reads=(), writes=()):
        self._deps(e, reads, writes)
        ins = fn(self.eng[e])
        self.tick[e] += 1
        ins.then_inc(self.esem[e], 1)
        self.n_inst += 1
        me = ("c", e, self.tick[e])
        for b in reads:
            b.rs.append(me)
            if len(b.rs) > 24:
                b.rs = self._compact(b.rs)
        for b in writes:
            b.w = me
            b.rs = []
        return ins

    def _compact(self, rs):
        best = {}
        out = []
        for r in rs:
            if r[0] == "c":
                if r[1] not in best or best[r[1]][2] < r[2]:
                    best[r[1]] = r
            else:
                k = ("d", id(r[1]))
                if k not in best or best[k][2] < r[2]:
                    best[k] = r
        return list(best.values())

    def dma(self, q, out, in_, sbuf, is_load, extra_reads=(), **kw):
        if is_load:
            self._deps(q, extra_reads, (sbuf,))
        else:
            self._deps(q, (sbuf,) + tuple(extra_reads), ())
        sem = self._dsem(sbuf, "sw" if q == "pool" else "hw")
        ins = self.eng[q].dma_start(out=out, in_=in_, **kw)
        sbuf.cnt += 16
        ins.then_inc(sem, 16)
        self.n_inst += 1
        me = ("d", sem, sbuf.cnt)
        if is_load:
            sbuf.w = me
            sbuf.rs = []
        else:
            sbuf.rs.append(me)
        return ins

    def barrier(self):
        for e in self.ENGS:
            if e != "pool" and self.tick[e] > 0:
                self._wait("pool", ("c", e, self.tick[e]))
        for b in self.all:
            if b.sem is not None and b.cnt > 0:
                self._wait("pool", ("d", b.sem, b.cnt))
        if self.tick["pool"] > 0:
            self._wait("pool", ("c", "pool", self.tick["pool"]))
        ins = self.nc.gpsimd.memset(self.bar_t[:], 0.0)
        self.tick["pool"] += 1
        ins.then_inc(self.esem["pool"], 1)
        for e in self.ENGS:
            if e != "pool":
                self._wait(e, ("c", "pool", self.tick["pool"]))
        for b in self.all:
            b.w = None
            b.rs = []


import math

D = 1024
DFF = 4096
NEG = -30000.0
ALPHA = 8.0 ** 0.25
EPS = 1e-5
FM_ROWS = 2816
R_QN, R_KVC, R_KS, R_KW, R_IQ, R_IK, R_CKV, R_QLAT = 0, 512, 768, 896, 1024, 1536, 1664, 1792


def t5_bucket_np(dist):
    n = np.maximum(dist, 0)
    nf = np.maximum(n, 1).astype(np.float32)
    large = 16 + (np.log(nf / np.float32(16)) / np.float32(math.log(8.0)) * np.float32(16)).astype(np.int32)
    return np.where(n < 16, n, np.minimum(large, 31))


def make_consts(S):
    NT = S // 128
    NB = S // 64
    NC = S // 16 - 1
    NCB = (NC + 127) // 128
    c = {}
    c["ident_bf"] = np.eye(128, dtype=np.float32).astype(NPBF)
    c["ident_f"] = np.eye(128, dtype=np.float32)
    L = 383
    dist = np.arange(L) - 127
    bk = t5_bucket_np(dist)
    OH = np.zeros((33, L), np.float32)
    for j in range(L):
        if dist[j] >= 0:
            OH[bk[j], j] += 1.0
            OH[31, j] -= 1.0
        else:
            OH[32, j] = NEG
    c["ohf"] = OH
    c["ohr"] = np.ascontiguousarray(OH[:, ::-1])
    sl = np.arange(128)[:, None]
    ql = np.arange(128)[None, :]
    c["tw4"] = np.where(ql < sl, 0.0, NEG).astype(np.float32).astype(NPBF)
    c["tri"] = np.where(np.arange(128)[None, :] <= np.arange(128)[:, None], 0.0, -1e30).astype(np.float32)
    E = np.zeros((128, NT * 128), np.float32)
    for kt in range(NT):
        for s in range(128):
            E[2 * kt + s // 64, kt * 128 + s] = 1.0
    c["esel"] = E.astype(NPBF)
    cc = np.arange(256)[None, :]
    qq = np.arange(128)[:, None]
    cbl = (qq >= 64).astype(np.int64)
    c["sela"] = (cc <= 128 + cbl).astype(np.float32)
    c["selb"] = np.where(cc == 128 + cbl, 2e9, np.where(cc == 127 + cbl, 1e9,
                         np.where(cc > 128 + cbl, -1e9, 0.0))).astype(np.float32)
    cs = np.arange(NC) * 16
    ss = np.arange(NB) * 64
    ov = np.clip(np.minimum(cs[:, None] + 32, ss[None, :] + 64) - np.maximum(cs[:, None], ss[None, :]), 0, None)
    imp = np.zeros((NCB * 128, NB), np.float32)
    imp[:NC] = ov / 16.0
    c["imp"] = np.ascontiguousarray(imp.reshape(NCB, 128, NB).transpose(1, 0, 2)).reshape(128, NCB * NB)
    return c


W_NAMES = ["rel_bias", "ada_w", "ada_b", "ln_g", "ln_b", "ev_w_in", "ev_w_out", "nsa_pe_k", "nsa_pe_v",
           "nsa_w1_k", "nsa_w2_k", "nsa_w1_v", "nsa_w2_v", "dsa_kv_norm", "dsa_w_uk", "dsa_w_uv",
           "od_w_in", "od_w_out", "diff_lam", "diff_subln", "mlp_w1", "mlp_w2"]


class KB:
    def __init__(self, S, layers, shapes, dbg=()):
        self.S = S
        self.NT = S // 128
        self.NB = S // 64
        self.NC = S // 16 - 1
        self.NCB = (self.NC + 127) // 128
        self.layers = layers
        self.dbg = dbg
        nc = bass.Bass("TRN2", target_bir_lowering=False)
        self.nc = nc
        d = {}
        d["x"] = nc.dram_tensor("x", [S, D], F32, kind="ExternalInput").ap()
        d["c"] = nc.dram_tensor("c", [1, D], F32, kind="ExternalInput").ap()
        for n in W_NAMES:
            d[n] = nc.dram_tensor(n, list(shapes[n]), F32, kind="ExternalInput").ap()
        cs = make_consts(S)
        for n, v in cs.items():
            d[n] = nc.dram_tensor(n, list(v.shape), BF16 if v.dtype == NPBF else F32, kind="ExternalInput").ap()
        self.consts = cs
        d["y"] = nc.dram_tensor("y", [S, D], F32, kind="ExternalOutput").ap()
        self.d = d
        I = lambda n, sh, dt: nc.dram_tensor(n, sh, dt, kind="Internal")
        self.FMt = I("FM", [FM_ROWS, S], BF16)
        self.FM = self.FMt.ap()
        self.TM = I("TM", [S, 1024], BF16).ap()
        self.OS = I("OS", [S, 1024], BF16).ap()
        self.XS = [I("XS0", [S, D], F32).ap(), I("XS1", [S, D], F32).ap()]
        self.X1 = I("X1", [S, D], F32).ap()
        self.H2T = I("H2T", [D, S], BF16).ap()
        self.ADA = I("ADA", [128, 6 * D], F32).ap()
        self.GATE = I("GATE", [S, 24], F32).ap()
        self.IW = I("IW", [S, 8], F32).ap()
        NCB = self.NCB
        self.KC = [I("KC%d" % g, [64, NCB * 128], BF16).ap() for g in range(2)]
        self.VC = [I("VC%d" % g, [NCB * 128, 64], BF16).ap() for g in range(2)]
        self.VRt = I("VR", [16, 383], F32)
        self.VFt = I("VF", [16, 383], F32)
        self.dbg_out = {}
        for n, sh, dt in dbg:
            self.dbg_out[n] = nc.dram_tensor(n, sh, dt, kind="ExternalOutput").ap()

    def build(self):
        with ExitStack() as st:
            p = Prog(self.nc, st)
            self.p = p
            self.psA = [p.ps("psA", [128, 512]) for _ in range(2)]
            self.psOt = [p.ps("psO", [128, 512]) for _ in range(4)]
            self.psO = self.psOt
            self.psB = [p.ps("psB", [128, 512]) for _ in range(2)]
            self.iA = 0
            self.iB = 0
            self.iO = 0
            self.iE = 0
            self.setup()
            xin = self.d["x"]
            for li, l in enumerate(self.layers):
                xout = self.d["y"] if li == len(self.layers) - 1 else self.XS[li % 2]
                sk = os.environ.get("KSKIP", "").split(",")
                self.stage_ada(l)
                self.stage_pre(l, xin)
                if l % 2 == 0:
                    if "cmp" not in sk:
                        self.stage_cmp(l)
                    if "nsa" not in sk:
                        self.stage_nsa(l)
                    if "dsa" not in sk:
                        self.stage_dsa(l)
                else:
                    self.stage_diff(l)
                self.stage_post1(l, xin)
                self.stage_post2(l, xout)
                xin = xout
            p.barrier()
        return self.nc

    def nextA(self):
        self.iA += 1
        return self.psA[self.iA % 2]

    def nextB(self):
        self.iB += 1
        return self.psB[self.iB % 2]

    def mm(self, ob, out, lhsT, rhs, reads, start, stop):
        self.p.op("pe", lambda e: e.matmul(out, lhsT=lhsT, rhs=rhs, start=start, stop=stop), reads, [ob])

    def tr(self, ob, out, in_, reads, f32=False):
        idn = self.ident_f if f32 else self.ident_bf
        self.p.op("pe", lambda e: e.transpose(out=out, in_=in_, identity=idn[:]), list(reads) + [idn], [ob])

    def evac(self, out, in_, ob, ib, scale=None, eng=None):
        self.iE += 1
        e = eng or ("act" if self.iE % 2 else "dve")
        if e == "act":
            if scale is None:
                self.p.op("act", lambda a: a.copy(out=out, in_=in_), [ib], [ob])
            else:
                self.p.op("act", lambda a: a.mul(out=out, in_=in_, mul=float(scale)), [ib], [ob])
        else:
            if scale is None:
                self.p.op(e, lambda a: a.tensor_copy(out=out, in_=in_), [ib], [ob])
            else:
                self.p.op(e, lambda a: a.tensor_scalar(out=out, in0=in_, scalar1=float(scale), scalar2=None,
                                                       op0=ALU.mult), [ib], [ob])

    def load(self, buf, out, in_, q="sp", **kw):
        self.p.dma(q, out, in_, buf, True, **kw)

    def store(self, buf, out, in_, q="sp", **kw):
        self.p.dma(q, out, in_, buf, False, **kw)

    def rstd_from(self, dst, src_ap, srcbuf, scale, stackbufs):
        p = self.p
        p.op("dve", lambda e: e.tensor_scalar(out=dst[:, 0:1], in0=src_ap, scalar1=float(scale), scalar2=EPS,
                                              op0=ALU.mult, op1=ALU.add), [srcbuf], [dst])
        p.op("act", lambda e: e.sqrt(out=dst[:, 0:1], in_=dst[:, 0:1]), [dst], [dst])
        p.op("dve", lambda e: e.reciprocal(out=dst[:, 0:1], in_=dst[:, 0:1]), [dst], [dst])

    def setup(self):
        p, d = self.p, self.d
        g = lambda n, sh, dt: p.sb(n, sh, dt)
        self.ident_bf = g("identbf", [128, 128], BF16)
        self.ident_f = g("identf", [128, 128], F32)
        self.tw4 = g("tw4", [128, 128], BF16)
        self.tri = g("tri", [128, 128], F32)
        self.sela = g("sela", [128, 256], F32)
        self.selb = g("selb", [128, 256], F32)
        self.imp = g("imp", [128, self.NCB * self.NB], F32)
        for b, n in ((self.ident_bf, "ident_bf"), (self.ident_f, "ident_f"), (self.tw4, "tw4"), (self.tri, "tri"),
                     (self.sela, "sela"), (self.selb, "selb"), (self.imp, "imp")):
            self.load(b, b[:], d[n][:, :])
        self.ones_bf = g("onesbf", [128, 128], BF16)
        p.op("dve", lambda e: e.memset(self.ones_bf[:], 1.0), [], [self.ones_bf])
        self.Tt = g("Tt", [128, 16, 2, 128], BF16)
        self.CBt = g("CBt", [128, 8, 16], BF16)
        self.cact = g("cact", [128, 8, 128], BF16)
        self.gate_all = g("gateall", [128, self.NT, 24], F32)
        self.iw_all = g("iwall", [128, self.NT, 8], F32)
        p.begin_stage()
        with ExitStack() as st:
            rel33 = p.sb("rel33", [33, 16], F32, st)
            ohf = p.sb("ohf", [33, 383], F32, st)
            ohr = p.sb("ohr", [33, 383], F32, st)
            vs = p.sb("vs", [16, 383], F32, st)
            vs2 = p.sb("vs2", [16, 383], F32, st)
            p.op("dve", lambda e: e.memset(rel33[:], 1.0), [], [rel33])
            self.load(rel33, rel33[0:32, :], d["rel_bias"][:, :])
            self.load(ohf, ohf[:], d["ohf"][:, :])
            self.load(ohr, ohr[:], d["ohr"][:, :])
            for oh, v, dst in ((ohr, vs, self.VRt), (ohf, vs2, self.VFt)):
                ps = self.nextA()
                self.mm(ps, ps[0:16, 0:383], rel33[:, :], oh[:, :], [rel33, oh], True, True)
                self.evac(v[:], ps[0:16, 0:383], v, ps, eng="dve")
                self.store(v, dst.ap()[:, :], v[:])
            ct = p.sb("ct", [128, 8], F32, st)
            self.load(ct, ct[:], d["c"].rearrange("o (k p) -> p (o k)", p=128), allow_slow_non_contiguous=True)
            p.op("act", lambda e: e.activation(out=ct[:], in_=ct[:], func=AF.Silu), [ct], [ct])
            for kc in range(8):
                p.op("dve", lambda e: e.tensor_scalar(out=self.cact[:, kc, :], in0=self.ones_bf[:],
                                                      scalar1=ct[:, kc:kc + 1], scalar2=None, op0=ALU.mult),
                     [self.ones_bf, ct], [self.cact])
            p.barrier()
            tst = [p.sb("tst", [128, 128], F32, st) for _ in range(2)]
            for h in range(16):
                for dl in range(2):
                    t = tst[(h * 2 + dl) % 2]
                    src = bass.AP(self.VRt, h * 383 + 255 - 128 * dl, [[1, 128], [-1, 128]])
                    self.load(t, t[:], src, allow_slow_non_contiguous=True)
                    p.op("dve", lambda e: e.tensor_copy(out=self.Tt[:, h, dl, :], in_=t[:]), [t], [self.Tt])
            cst = [p.sb("cst", [128, 15], F32, st) for _ in range(2)]
            for h in range(8):
                t = cst[h % 2]
                src = bass.AP(self.VFt, h * 383 + 224, [[1, 128], [-16, 15]])
                self.load(t, t[:], src, allow_slow_non_contiguous=True)
                p.op("dve", lambda e: e.tensor_copy(out=self.CBt[:, h, 0:15], in_=t[:]), [t], [self.CBt])
            p.end_stage()

    def stage_ada(self, l):
        p, d = self.p, self.d
        p.begin_stage()
        with ExitStack() as st:
            wch = [p.sb("adaw", [128, 8, 512], BF16, st) for _ in range(2)]
            bch = [p.sb("adab", [1, 512], BF16, st) for _ in range(2)]
            ob = [p.sb("adao", [128, 512], F32, st) for _ in range(2)]
            for ch in range(12):
                w, b, o = wch[ch % 2], bch[ch % 2], ob[ch % 2]
                self.load(w, w[:], d["ada_w"][l, :, ch * 512:(ch + 1) * 512].rearrange("(k p) n -> p k n", p=128), q="pool")
                self.load(b, b[:], d["ada_b"][l:l + 1, ch * 512:(ch + 1) * 512], q="pool")
                ps = self.nextA()
                for kc in range(8):
                    self.mm(ps, ps[:, :], self.cact[:, kc, :], w[:, kc, :], [self.cact, w], kc == 0, False)
                self.mm(ps, ps[:, :], self.ones_bf[0:1, :], b[0:1, :], [self.ones_bf, b], False, True)
                add1 = 1.0 if (ch // 2) in (1, 2, 4, 5) else 0.0
                p.op("dve", lambda e: e.tensor_scalar(out=o[:], in0=ps[:, :], scalar1=add1, scalar2=None, op0=ALU.add),
                     [ps], [o])
                self.store(o, self.ADA[:, ch * 512:(ch + 1) * 512], o[:])
            p.end_stage()

    def modtile(self, st, name, col0):
        t = self.p.sb(name, [128, D], F32, st)
        self.load(t, t[:], self.ADA[:, col0:col0 + D])
        return t

    def bctile(self, st, name, src_row_ap, n=D, q="sp"):
        t = self.p.sb(name, [128, n], F32, st)
        self.load(t, t[:], src_row_ap.partition_broadcast(128), q=q)
        return t

    def stage_pre(self, l, xin):
        p, d, S = self.p, self.d, self.S
        even = l % 2 == 0
        i = l // 2
        NW = 2528 if even else 3072
        w_in = d["ev_w_in"] if even else d["od_w_in"]
        p.begin_stage()
        with ExitStack() as st:
            W = p.sb("W", [128, 8, NW], BF16, st)
            for kc in range(8):
                self.load(W, W[:, kc, :], w_in[i, kc * 128:(kc + 1) * 128, :], q="pool")
            A1 = self.modtile(st, "A1", 1 * D)
            B1 = self.modtile(st, "B1", 0)
            xts = [p.sb("xt", [128, D], F32, st) for _ in range(2)]
            hbs = [p.sb("hb", [128, D], BF16, st) for _ in range(2)]
            hT4s = [p.sb("hT4", [128, 8, 512], BF16, st) for _ in range(2)]
            stg = [p.sb("stg", [128, 512], BF16, st) for _ in range(4)]
            istg = 0
            if even:
                wuk = p.sb("wuk", [128, 512], BF16, st)
                self.load(wuk, wuk[:], d["dsa_w_uk"][i].rearrange("r h d -> r (h d)"), q="pool")
                wukT = p.sb("wukT", [128, 4, 128], BF16, st)
                for pr in range(4):
                    pb = self.nextB()
                    pbv = pb.t[:].bitcast(BF16)
                    self.tr(pb, pbv[:, 0:128], wuk[:, pr * 128:(pr + 1) * 128], [wuk])
                    self.evac(wukT[:, pr, :], pbv[:, 0:128], wukT, pb)
                kvn = self.bctile(st, "kvn", d["dsa_kv_norm"][i:i + 1, :], 128)
                dqs = [p.sb("dq", [128, 512], BF16, st) for _ in range(2)]
                tmst = [p.sb("tmst", [128, 384], BF16, st) for _ in range(2)]
                gst = [p.sb("gst", [128, 24], F32, st) for _ in range(2)]
                iwst = [p.sb("iwst", [128, 8], F32, st) for _ in range(2)]
                ckst = [p.sb("ckst", [128, 512], BF16, st) for _ in range(2)]
                sst = [p.sb("sst", [128, 2], F32, st) for _ in range(2)]
                junk = p.sb("junk", [128, 128], F32, st)
                traws = [p.sb("traw", [128, 512], F32, st) for _ in range(2)]
                if os.environ.get("KDBG"):
                    print("SBUF remaining after pre-even alloc:", self.nc.sbuf_bytes_remaining)
                fm = []
                for ch in range(4):
                    fm.append((ch * 128, 128, 0.125, R_QN + ch * 128))
                fm += [(512, 128, None, R_KVC), (640, 128, None, R_KVC + 128), (768, 128, None, R_KS),
                       (1024, 128, None, R_KW)]
                for ch in range(4):
                    fm.append((1944 + ch * 128, 128, None, R_IQ + ch * 128))
                fm.append((2456, 64, None, R_IK))
            else:
                vst = [p.sb("vst", [128, 1024], BF16, st) for _ in range(2)]
                fm = []
                for ch in range(8):
                    fm.append((ch * 128, 128, 0.125, ch * 128))
                for ch in range(8):
                    fm.append((1024 + ch * 128, 128, None, 1024 + ch * 128))
            for g4 in range(S // 512):
                hT4 = hT4s[g4 % 2]
                cols = slice(g4 * 512, (g4 + 1) * 512)
                for j in range(4):
                    tt = g4 * 4 + j
                    xt, hb = xts[tt % 2], hbs[tt % 2]
                    self.load(xt, xt[:], xin[tt * 128:(tt + 1) * 128, :])
                    p.op("pool", lambda e: e.tensor_tensor(out=xt[:], in0=xt[:], in1=A1[:], op=ALU.mult), [xt, A1], [xt])
                    p.op("dve", lambda e: e.tensor_tensor(out=hb[:], in0=xt[:], in1=B1[:], op=ALU.add), [xt, B1], [hb])
                    pb = self.nextB()
                    pbv = pb.t[:].bitcast(BF16)
                    for kc in range(8):
                        self.tr(pb, pbv[:, kc * 128:(kc + 1) * 128], hb[:, kc * 128:(kc + 1) * 128], [hb])
                    self.evac(hT4[:, :, j * 128:(j + 1) * 128], pbv.rearrange("p (k t) -> p k t", k=8), hT4, pb)
                for (c0, n, scale, r0) in fm:
                    ps = self.nextA()
                    for kc in range(8):
                        self.mm(ps, ps[0:n, :], W[:, kc, c0:c0 + n], hT4[:, kc, :], [W, hT4], kc == 0, kc == 7)
                    sg = stg[istg % 4]
                    istg += 1
                    self.evac(sg[0:n, :], ps[0:n, :], sg, ps, scale)
                    self.store(sg, self.FM[r0:r0 + n, cols], sg[0:n, :])
                psk = os.environ.get("KPRE", "").split(",")
                if even:
                    for pr in range(4 if "ql" not in psk else 0):
                        ps = self.nextA()
                        c0 = 1304 + pr * 128
                        for kc in range(8):
                            self.mm(ps, ps[:, :], W[:, kc, c0:c0 + 128], hT4[:, kc, :], [W, hT4], kc == 0, kc == 7)
                        dq = dqs[pr % 2]
                        self.evac(dq[:], ps[:, :], dq, ps)
                        for hh in range(2):
                            h = 2 * pr + hh
                            ps2 = self.nextA()
                            self.mm(ps2, ps2[:, :], wukT[hh * 64:(hh + 1) * 64, pr, :], dq[hh * 64:(hh + 1) * 64, :],
                                    [wukT, dq], True, True)
                            sg = stg[istg % 4]
                            istg += 1
                            self.evac(sg[:], ps2[:, :], sg, ps2, 0.125)
                            self.store(sg, self.FM[R_QLAT + h * 128:R_QLAT + (h + 1) * 128, cols], sg[:])
                    ck = ckst[g4 % 2]
                    for j in range(4 if "tm" not in psk else 0):
                        tt = g4 * 4 + j
                        rows = slice(tt * 128, (tt + 1) * 128)
                        ps = self.nextA()
                        ps_iw = self.nextA()
                        for kc in range(8):
                            self.mm(ps_iw, ps_iw[:, 0:136], hT4[:, kc, j * 128:(j + 1) * 128], W[:, kc, 2392:2528],
                                    [hT4, W], kc == 0, kc == 7)
                        for (c0, n, o0) in ((896, 128, 0), (1152, 152, 128), (1816, 128, 384)):
                            for kc in range(8):
                                self.mm(ps, ps[:, o0:o0 + n], hT4[:, kc, j * 128:(j + 1) * 128], W[:, kc, c0:c0 + n],
                                        [hT4, W], kc == 0, kc == 7)
                        tm, gs, iw, ss = tmst[tt % 2], gst[tt % 2], iwst[tt % 2], sst[tt % 2]
                        traw = traws[tt % 2]
                        p.op("dve", lambda e: e.tensor_copy(out=traw[:], in_=ps[:, :]), [ps], [traw])
                        self.evac(tm[:, 0:256], traw[:, 0:256], tm, traw)
                        p.op("dve", lambda e: e.tensor_tensor(out=junk[:], in0=traw[:, 384:512], in1=traw[:, 384:512], op=ALU.mult), [traw], [junk])
                        p.op("dve", lambda e: e.reduce_sum(out=ss[:, 0:1], in_=junk[:], axis=AX.X), [junk], [ss])
                        self.rstd2(ss[:, 0:1], ss, ss[:, 0:1], ss, 1.0 / 128)
                        p.op("dve", lambda e: e.scalar_tensor_tensor(out=tm[:, 256:384], in0=traw[:, 384:512],
                                                                     scalar=ss[:, 0:1], in1=kvn[:], op0=ALU.mult,
                                                                     op1=ALU.mult), [traw, ss, kvn], [tm])
                        p.op("act", lambda e: e.activation(out=self.gate_all[:, tt, :], in_=traw[:, 256:280], func=AF.Sigmoid), [traw], [self.gate_all])
                        p.op("dve", lambda e: e.tensor_copy(out=self.iw_all[:, tt, :], in_=ps_iw[:, 128:136]), [ps_iw], [self.iw_all])
                        self.store(tm, self.TM[rows, 0:384], tm[:])
                        pb = self.nextB()
                        pbv = pb.t[:].bitcast(BF16)
                        self.tr(pb, pbv[:, 0:128], tm[:, 256:384], [tm])
                        self.evac(ck[:, j * 128:(j + 1) * 128], pbv[:, 0:128], ck, pb)
                    self.store(ck, self.FM[R_CKV:R_CKV + 128, cols], ck[:])
                else:
                    for j in range(4):
                        tt = g4 * 4 + j
                        vs_ = vst[tt % 2]
                        for hf in range(2):
                            ps = self.nextA()
                            c0 = 2048 + hf * 512
                            for kc in range(8):
                                self.mm(ps, ps[:, :], hT4[:, kc, j * 128:(j + 1) * 128], W[:, kc, c0:c0 + 512],
                                        [hT4, W], kc == 0, kc == 7)
                            self.evac(vs_[:, hf * 512:(hf + 1) * 512], ps[:, :], vs_, ps)
                        self.store(vs_, self.TM[tt * 128:(tt + 1) * 128, :], vs_[:])
            p.end_stage()

    def attn(self, qts, lo, hi, q_ap, k_ap, v_ap, dv, special, chunk_extra, done, pts):
        p = self.p
        slots = {}
        for qt in qts:
            self.iO += 1
            slots[qt] = self.psO[self.iO % 4]
        kmin = min(lo(qt) for qt in qts)
        kmax = max(hi(qt) for qt in qts)
        q0 = qts[0]
        for kt in range(kmin, kmax + 1):
            act = [qt for qt in qts if lo(qt) <= kt <= hi(qt)]
            if not act:
                continue
            a, b = act[0] - q0, act[-1] - q0 + 1
            ps = self.nextA()
            extras = []
            if chunk_extra is not None:
                for (l_, r_, bufs) in chunk_extra(kt, act[0], act[-1] + 1):
                    extras.append((ps[:, a * 128:b * 128], l_, r_, bufs))
            for qt in act:
                for (l_, r_, bufs) in special(qt, kt):
                    j = qt - q0
                    extras.append((ps[:, j * 128:(j + 1) * 128], l_, r_, bufs))
            qa, qb_ = q_ap(act[0], act[-1] + 1)
            ka, kb_ = k_ap(kt)
            self.mm(ps, ps[:, a * 128:b * 128], ka, qa, kb_ + qb_, True, len(extras) == 0)
            for n_, (o_, l_, r_, bufs) in enumerate(extras):
                self.mm(ps, o_, l_, r_, bufs, False, n_ == len(extras) - 1)
            self.iPT = getattr(self, "iPT", 0) + 1
            pt = pts[self.iPT % len(pts)]
            p.op("act", lambda e: e.activation(out=pt[:, a * 128:b * 128], in_=ps[:, a * 128:b * 128], func=AF.Exp),
                 [ps], [pt])
            va, vb_ = v_ap(kt)
            for qt in act:
                j = qt - q0
                so = slots[qt]
                self.mm(so, so[:, 0:dv + 1], pt[:, j * 128:(j + 1) * 128], va, [pt] + vb_, kt == lo(qt), kt == hi(qt))
                if kt == hi(qt):
                    done(qt, so)

    def rstd2(self, dst_ap, dstbuf, src_ap, srcbuf, scale):
        p = self.p
        p.op("dve", lambda e: e.tensor_scalar(out=dst_ap, in0=src_ap, scalar1=float(scale), scalar2=EPS,
                                              op0=ALU.mult, op1=ALU.add), [srcbuf], [dstbuf])
        p.op("act", lambda e: e.sqrt(out=dst_ap, in_=dst_ap), [dstbuf], [dstbuf])
        p.op("dve", lambda e: e.reciprocal(out=dst_ap, in_=dst_ap), [dstbuf], [dstbuf])

    def load_split(self, buf, out3, in3, n, parts=4, q="sp"):
        step = (n + parts - 1) // parts
        for a in range(0, n, step):
            b = min(n, a + step)
            self.load(buf, out3[:, a:b], in3[:, a:b], q=q)

    def stage_diff(self, l):
        p, d, S, NT = self.p, self.d, self.S, self.NT
        i = l // 2
        lam_init = 0.8 - 0.6 * math.exp(-0.3 * l)
        p.begin_stage()
        with ExitStack() as st:
            lamt = self.bctile(st, "lamt", d["diff_lam"][i:i + 1].rearrange("o a b -> o (a b)"), 256)
            pr = p.sb("lpr", [128, 128], F32, st)
            sm = p.sb("lsm", [128, 4], F32, st)
            p.op("dve", lambda e: e.tensor_tensor(out=pr[:, 0:64], in0=lamt[:, 0:64], in1=lamt[:, 64:128], op=ALU.mult), [lamt], [pr])
            p.op("dve", lambda e: e.tensor_tensor(out=pr[:, 64:128], in0=lamt[:, 128:192], in1=lamt[:, 192:256], op=ALU.mult), [lamt], [pr])
            p.op("dve", lambda e: e.reduce_sum(out=sm[:, 0:1], in_=pr[:, 0:64], axis=AX.X), [pr], [sm])
            p.op("dve", lambda e: e.reduce_sum(out=sm[:, 1:2], in_=pr[:, 64:128], axis=AX.X), [pr], [sm])
            p.op("act", lambda e: e.activation(out=sm[:, 0:2], in_=sm[:, 0:2], func=AF.Exp), [sm], [sm])
            p.op("dve", lambda e: e.tensor_tensor(out=sm[:, 2:3], in0=sm[:, 1:2], in1=sm[:, 0:1], op=ALU.subtract), [sm], [sm])
            p.op("dve", lambda e: e.tensor_scalar(out=sm[:, 3:4], in0=sm[:, 2:3], scalar1=-lam_init, scalar2=None, op0=ALU.add), [sm], [sm])
            subw = self.bctile(st, "subw", d["diff_subln"][i:i + 1, :], 128)
            p.op("dve", lambda e: e.tensor_scalar(out=subw[:], in0=subw[:], scalar1=1.0 - lam_init, scalar2=None, op0=ALU.mult), [subw], [subw])
            Kh = [p.sb("Kh", [128, S], BF16, st) for _ in range(2)]
            Qh = [p.sb("Qh", [128, S], BF16, st) for _ in range(2)]
            Va = [p.sb("Va", [128, NT, 129], BF16, st) for _ in range(2)]
            for v in Va:
                p.op("pool", lambda e: e.memset(v[:, :, 128:129], 1.0), [], [v])
            pts = [p.sb("pt", [128, 512], BF16, st) for _ in range(3)]
            o0s = [p.sb("o0", [128, 4, 128], F32, st) for _ in range(2)]
            ods = [p.sb("od", [128, 128], F32, st) for _ in range(2)]
            osts = [p.sb("ost", [128, 128], BF16, st) for _ in range(3)]
            rrs = [p.sb("rr", [128, 4], F32, st) for _ in range(4)]
            junk = p.sb("junk", [128, 128], F32, st)
            cnt = [0]
            for h in range(8):
                K_, Q_, V_ = Kh[h % 2], Qh[h % 2], Va[h % 2]
                self.load(K_, K_[:], self.FM[1024 + h * 128:1024 + (h + 1) * 128, :])
                self.load(Q_, Q_[:], self.FM[h * 128:(h + 1) * 128, :])
                self.load_split(V_, V_[:, :, 0:128], self.TM[:, h * 128:(h + 1) * 128].rearrange("(t p) e -> p t e", p=128), NT)
                for c in range(NT // 4):
                    qts = list(range(c * 4, c * 4 + 4))
                    o0c = o0s[c % 2]
                    for m in range(2):
                        col = h * 2 + m

                        def special(qt, kt, col=col):
                            if kt == qt:
                                return [(self.ident_bf[:], self.Tt[:, col, 0, :], [self.ident_bf, self.Tt])]
                            if kt == qt - 1:
                                return [(self.ident_bf[:], self.Tt[:, col, 1, :], [self.ident_bf, self.Tt])]
                            return []

                        def done(qt, so, m=m, q0=qts[0], o0c=o0c, h=h):
                            j = qt - q0
                            cnt[0] += 1
                            r = rrs[cnt[0] % 4]
                            p.op("dve", lambda e: e.reciprocal(out=r[:, 0:1], in_=so[:, 128:129]), [so], [r])
                            if m == 0:
                                p.op("dve", lambda e: e.tensor_scalar(out=o0c[:, j, :], in0=so[:, 0:128], scalar1=r[:, 0:1],
                                                                      scalar2=None, op0=ALU.mult), [so, r], [o0c])
                                return
                            od, ot = ods[cnt[0] % 2], osts[cnt[0] % 3]
                            p.op("dve", lambda e: e.tensor_tensor(out=r[:, 1:2], in0=r[:, 0:1], in1=sm[:, 3:4], op=ALU.mult), [r, sm], [r])
                            p.op("dve", lambda e: e.scalar_tensor_tensor(out=od[:], in0=so[:, 0:128], scalar=r[:, 1:2],
                                                                         in1=o0c[:, j, :], op0=ALU.mult, op1=ALU.add),
                                 [so, r, o0c], [od])
                            p.op("act", lambda e: e.activation(out=junk[:], in_=od[:], func=AF.Square, accum_out=r[:, 2:3]),
                                 [od], [junk, r])
                            self.rstd2(r[:, 2:3], r, r[:, 2:3], r, 1.0 / 128)
                            p.op("dve", lambda e: e.scalar_tensor_tensor(out=ot[:], in0=od[:], scalar=r[:, 2:3], in1=subw[:],
                                                                         op0=ALU.mult, op1=ALU.mult), [od, r, subw], [ot])
                            self.store(ot, self.OS[qt * 128:(qt + 1) * 128, h * 128:(h + 1) * 128], ot[:])

                        self.attn(qts, lambda qt: 0, lambda qt: qt,
                                  lambda a, b, m=m, Q_=Q_: (Q_[m * 64:(m + 1) * 64, a * 128:b * 128], [Q_]),
                                  lambda kt, m=m, K_=K_: (K_[m * 64:(m + 1) * 64, kt * 128:(kt + 1) * 128], [K_]),
                                  lambda kt, V_=V_: (V_[:, kt, :], [V_]),
                                  128, special, None, done, pts)
            p.end_stage()

    def layernorm(self, ut, xo, LG, LB, stats, mv, rs):
        p = self.p
        p.op("dve", lambda e: e.bn_stats(out=stats[:, 0, :], in_=ut[:, 0:512]), [ut], [stats])
        p.op("dve", lambda e: e.bn_stats(out=stats[:, 1, :], in_=ut[:, 512:1024]), [ut], [stats])
        p.op("dve", lambda e: e.bn_aggr(out=mv[:], in_=stats[:].rearrange("p a b -> p (a b)")), [stats], [mv])
        self.rstd2(rs[:, 0:1], rs, mv[:, 1:2], mv, 1.0)
        p.op("dve", lambda e: e.tensor_scalar(out=ut[:], in0=ut[:], scalar1=mv[:, 0:1], scalar2=rs[:, 0:1],
                                              op0=ALU.subtract, op1=ALU.mult), [ut, mv, rs], [ut])
        p.op("pool", lambda e: e.tensor_tensor(out=ut[:], in0=ut[:], in1=LG[:], op=ALU.mult), [ut, LG], [ut])
        p.op("dve", lambda e: e.tensor_tensor(out=xo[:], in0=ut[:], in1=LB[:], op=ALU.add), [ut, LB], [xo])

    def stage_post1(self, l, xin):
        p, d, S, NT = self.p, self.d, self.S, self.NT
        i = l // 2
        w_out = d["ev_w_out"] if l % 2 == 0 else d["od_w_out"]
        p.begin_stage()
        with ExitStack() as st:
            Wo = p.sb("Wo", [128, 8, D], BF16, st)
            self.load(Wo, Wo[:], w_out[i].rearrange("(k p) n -> p k n", p=128), q="pool")
            G1 = self.modtile(st, "G1", 2 * D)
            SH2 = self.modtile(st, "SH2", 3 * D)
            SC2 = self.modtile(st, "SC2", 4 * D)
            LG = self.bctile(st, "LG", d["ln_g"][l, 0:1, :])
            LB = self.bctile(st, "LB", d["ln_b"][l, 0:1, :])
            ots = [p.sb("ot", [128, D], BF16, st) for _ in range(2)]
            oTs = [p.sb("oT", [128, 8, 128], BF16, st) for _ in range(2)]
            xts = [p.sb("xt", [128, D], F32, st) for _ in range(2)]
            uts = [p.sb("ut", [128, D], F32, st) for _ in range(2)]
            x1s = [p.sb("x1", [128, D], F32, st) for _ in range(2)]
            hbs = [p.sb("hb", [128, D], BF16, st) for _ in range(2)]
            hTs = [p.sb("hT", [128, 8, 128], BF16, st) for _ in range(2)]
            sts = [p.sb("stats", [128, 2, 6], F32, st) for _ in range(2)]
            mvs = [p.sb("mv", [128, 2], F32, st) for _ in range(2)]
            rss = [p.sb("rs", [128, 1], F32, st) for _ in range(2)]
            for tt in range(NT):
                k = tt % 2
                rows = slice(tt * 128, (tt + 1) * 128)
                ot, oT, xt, ut, x1, hb, hT = ots[k], oTs[k], xts[k], uts[k], x1s[k], hbs[k], hTs[k]
                self.load(ot, ot[:], self.OS[rows, :])
                self.load(xt, xt[:], xin[rows, :])
                pb = self.nextB()
                pbv = pb.t[:].bitcast(BF16)
                for kc in range(8):
                    self.tr(pb, pbv[:, kc * 128:(kc + 1) * 128], ot[:, kc * 128:(kc + 1) * 128], [ot])
                self.evac(oT[:], pbv.rearrange("p (k t) -> p k t", k=8), oT, pb)
                for hf in range(2):
                    ps = self.nextA()
                    for kc in range(8):
                        self.mm(ps, ps[:, :], oT[:, kc, :], Wo[:, kc, hf * 512:(hf + 1) * 512], [oT, Wo], kc == 0, kc == 7)
                    p.op("dve", lambda e: e.tensor_tensor(out=ut[:, hf * 512:(hf + 1) * 512], in0=ps[:, :],
                                                          in1=G1[:, hf * 512:(hf + 1) * 512], op=ALU.mult), [ps, G1], [ut])
                p.op("dve", lambda e: e.scalar_tensor_tensor(out=ut[:], in0=xt[:], scalar=ALPHA, in1=ut[:], op0=ALU.mult,
                                                              op1=ALU.add), [xt, ut], [ut])
                self.layernorm(ut, x1, LG, LB, sts[k], mvs[k], rss[k])
                self.store(x1, self.X1[rows, :], x1[:])
                p.op("pool", lambda e: e.tensor_tensor(out=ut[:], in0=x1[:], in1=SC2[:], op=ALU.mult), [x1, SC2], [ut])
                p.op("dve", lambda e: e.tensor_tensor(out=hb[:], in0=ut[:], in1=SH2[:], op=ALU.add), [ut, SH2], [hb])
                pb = self.nextB()
                pbv = pb.t[:].bitcast(BF16)
                for kc in range(8):
                    self.tr(pb, pbv[:, kc * 128:(kc + 1) * 128], hb[:, kc * 128:(kc + 1) * 128], [hb])
                self.evac(hT[:], pbv.rearrange("p (k t) -> p k t", k=8), hT, pb)
                self.store(hT, self.H2T[:, rows].rearrange("(k p) t -> p k t", p=128), hT[:])
            p.end_stage()

    def stage_post2(self, l, xout):
        p, d, S, NT = self.p, self.d, self.S, self.NT
        p.begin_stage()
        with ExitStack() as st:
            W1 = p.sb("W1", [128, 8, DFF], BF16, st)
            for kc in range(8):
                self.load(W1, W1[:, kc, :], d["mlp_w1"][l, kc * 128:(kc + 1) * 128, :], q="pool")
            W2 = p.sb("W2", [128, 32, D], BF16, st)
            w2v = d["mlp_w2"][l].rearrange("(f p) n -> p f n", p=128)
            for a in range(0, 32, 8):
                self.load(W2, W2[:, a:a + 8, :], w2v[:, a:a + 8, :], q="pool")
            G2 = self.modtile(st, "G2", 5 * D)
            LG = self.bctile(st, "LG", d["ln_g"][l, 1:2, :])
            LB = self.bctile(st, "LB", d["ln_b"][l, 1:2, :])
            h2s = [p.sb("h2", [128, 8, 256], BF16, st) for _ in range(2)]
            aT = p.sb("aT", [128, 32, 256], BF16, st)
            rls = [p.sb("rl", [128, 256], F32, st) for _ in range(2)]
            xts = [p.sb("xt", [128, D], F32, st) for _ in range(2)]
            uts = [p.sb("ut", [128, D], F32, st) for _ in range(2)]
            sts = [p.sb("stats", [128, 2, 6], F32, st) for _ in range(2)]
            mvs = [p.sb("mv", [128, 2], F32, st) for _ in range(2)]
            rss = [p.sb("rs", [128, 1], F32, st) for _ in range(2)]
            for g2 in range(S // 256):
                h2 = h2s[g2 % 2]
                self.load(h2, h2[:], self.H2T[:, g2 * 256:(g2 + 1) * 256].rearrange("(k p) t -> p k t", p=128))
                for fc in range(32):
                    ps = self.nextA()
                    for kc in range(8):
                        self.mm(ps, ps[:, 0:256], W1[:, kc, fc * 128:(fc + 1) * 128], h2[:, kc, :], [W1, h2], kc == 0, kc == 7)
                    rl = rls[fc % 2]
                    p.op("act", lambda e: e.activation(out=rl[:], in_=ps[:, 0:256], func=AF.Relu), [ps], [rl])
                    eng = "dve" if fc % 2 else "pool"
                    p.op(eng, lambda e: e.tensor_tensor(out=aT[:, fc, :], in0=rl[:], in1=rl[:], op=ALU.mult), [rl], [aT])
                for j in range(2):
                    tt = g2 * 2 + j
                    k = tt % 2
                    rows = slice(tt * 128, (tt + 1) * 128)
                    xt, ut = xts[k], uts[k]
                    self.load(xt, xt[:], self.X1[rows, :])
                    for hf in range(2):
                        ps = self.psOt[(tt * 2 + hf) % 4]
                        for fc in range(32):
                            self.mm(ps, ps[:, :], aT[:, fc, j * 128:(j + 1) * 128], W2[:, fc, hf * 512:(hf + 1) * 512],
                                    [aT, W2], fc == 0, fc == 31)
                        p.op("dve", lambda e: e.tensor_tensor(out=ut[:, hf * 512:(hf + 1) * 512], in0=ps[:, :],
                                                              in1=G2[:, hf * 512:(hf + 1) * 512], op=ALU.mult), [ps, G2], [ut])
                    p.op("dve", lambda e: e.scalar_tensor_tensor(out=ut[:], in0=xt[:], scalar=ALPHA, in1=ut[:], op0=ALU.mult,
                                                                  op1=ALU.add), [xt, ut], [ut])
                    self.layernorm(ut, xt, LG, LB, sts[k], mvs[k], rss[k])
                    self.store(xt, xout[rows, :], xt[:])
            p.end_stage()

    def stage_cmp(self, l):
        p, d, S, NT, NC, NCB = self.p, self.d, self.S, self.NT, self.NC, self.NCB
        i = l // 2
        p.begin_stage()
        with ExitStack() as st:
            krs = [p.sb("kr", [128, S], BF16, st) for _ in range(2)]
            hidT = p.sb("hidT", [128, 2, NCB * 128], BF16, st)
            xh = p.sb("xh", [128, 512], F32, st)
            t1 = p.sb("t1", [128, 512], F32, st)
            kcst = p.sb("kcst", [64, NCB * 128], BF16, st)
            vcst = p.sb("vcst", [128, NCB, 64], BF16, st)
            p.op("dve", lambda e: e.memset(kcst[:], 0.0), [], [kcst])
            p.op("dve", lambda e: e.memset(vcst[:], 0.0), [], [vcst])
            for kv in range(2):
                nm_ = "k" if kv == 0 else "v"
                w1 = p.sb("w1", [128, 16, 256], BF16, st)
                self.load(w1, w1[:], d["nsa_w1_" + nm_][i].rearrange("(c p) h -> p c h", p=128), q="pool")
                w2 = p.sb("w2", [128, 2, 64], BF16, st)
                self.load(w2, w2[:], d["nsa_w2_" + nm_][i].rearrange("(c p) e -> p c e", p=128), q="pool")
                pef = p.sb("pef", [128, 16], F32, st)
                for two in range(2):
                    self.load(pef, pef[two * 64:(two + 1) * 64, :],
                              d["nsa_pe_" + nm_][i].rearrange("(c two) e -> two e c", two=2)[two],
                              allow_slow_non_contiguous=True)
                peb = p.sb("peb", [128, 16, 32], BF16, st)
                for c in range(16):
                    p.op("dve", lambda e: e.tensor_scalar(out=peb[:, c, :], in0=self.ones_bf[:, 0:32], scalar1=pef[:, c:c + 1],
                                                          scalar2=None, op0=ALU.mult), [self.ones_bf, pef], [peb])
                biasT = p.sb("biasT", [128, 2], F32, st)
                pb = self.nextB()
                for hc in range(2):
                    for c in range(16):
                        self.mm(pb, pb[:, hc * 32:(hc + 1) * 32], w1[:, c, hc * 128:(hc + 1) * 128], peb[:, c, :], [w1, peb], c == 0, c == 15)
                for hc in range(2):
                    self.evac(biasT[:, hc:hc + 1], pb[:, hc * 32:hc * 32 + 1], biasT, pb, eng="dve")
                for g in range(2):
                    kr = krs[g]
                    r0 = R_KVC + kv * 128 + g * 64
                    self.load(kr, kr[0:64, :], self.FM[r0:r0 + 64, :])
                    self.load(kr, kr[64:128, 0:S - 1], self.FM[r0:r0 + 64, 1:S])
                    for hc in range(2):
                        ps = self.nextA()
                        for c in range(16):
                            self.mm(ps, ps[:, 0:NC], w1[:, c, hc * 128:(hc + 1) * 128],
                                    kr[:, 2 * c:2 * c + 16 * (NC - 1) + 1:16], [w1, kr], c == 0, c == 15)
                        p.op("act", lambda e: e.activation(out=xh[:, 0:NC], in_=ps[:, 0:NC], func=AF.Identity,
                                                           bias=biasT[:, hc:hc + 1]), [ps, biasT], [xh])
                        p.op("dve", lambda e: e.tensor_tensor(out=t1[:, 0:NC], in0=xh[:, 0:NC], in1=xh[:, 0:NC], op=ALU.mult), [xh], [t1])
                        p.op("dve", lambda e: e.tensor_scalar(out=t1[:, 0:NC], in0=t1[:, 0:NC], scalar1=0.044715, scalar2=1.0,
                                                              op0=ALU.mult, op1=ALU.add), [t1], [t1])
                        p.op("dve", lambda e: e.tensor_tensor(out=t1[:, 0:NC], in0=t1[:, 0:NC], in1=xh[:, 0:NC], op=ALU.mult), [t1, xh], [t1])
                        p.op("act", lambda e: e.activation(out=t1[:, 0:NC], in_=t1[:, 0:NC], func=AF.Tanh,
                                                           scale=0.7978845608028654), [t1], [t1])
                        p.op("dve", lambda e: e.tensor_scalar(out=t1[:, 0:NC], in0=t1[:, 0:NC], scalar1=0.5, scalar2=0.5,
                                                              op0=ALU.mult, op1=ALU.add), [t1], [t1])
                        p.op("dve", lambda e: e.tensor_tensor(out=hidT[:, hc, 0:NC], in0=t1[:, 0:NC], in1=xh[:, 0:NC], op=ALU.mult),
                             [t1, xh], [hidT])
                    if kv == 0:
                        ps = self.nextA()
                        for hc in range(2):
                            self.mm(ps, ps[0:64, 0:NC], w2[:, hc, :], hidT[:, hc, 0:NC], [w2, hidT], hc == 0, hc == 1)
                        self.evac(kcst[:, 0:NC], ps[0:64, 0:NC], kcst, ps)
                        self.store(kcst, self.KC[g], kcst[:])
                    else:
                        for nb in range(NCB):
                            n0 = nb * 128
                            nn = min(128, NC - n0)
                            ps = self.nextA()
                            for hc in range(2):
                                self.mm(ps, ps[0:nn, 0:64], hidT[:, hc, n0:n0 + nn], w2[:, hc, :], [hidT, w2], hc == 0, hc == 1)
                            self.evac(vcst[0:nn, nb, :], ps[0:nn, 0:64], vcst, ps)
                        self.store(vcst, self.VC[g].rearrange("(b p) e -> p b e", p=128), vcst[:])
            p.end_stage()

    def stage_nsa(self, l):
        p, d, S, NT, NB, NC, NCB = self.p, self.d, self.S, self.NT, self.NB, self.NC, self.NCB
        p.begin_stage()
        with ExitStack() as st:
            esel = p.sb("esel", [128, NT * 128], BF16, st)
            self.load(esel, esel[:], d["esel"][:, :])
            gate = self.gate_all
            kc2 = p.sb("kc2", [128, NCB * 128], BF16, st)
            vc = p.sb("vc", [128, NCB, 64], BF16, st)
            ks2 = p.sb("ks2", [128, S], BF16, st)
            kw2 = p.sb("kw2", [128, S], BF16, st)
            vsa = p.sb("vsa", [128, NT, 65], BF16, st)
            vwa = p.sb("vwa", [128, NT, 65], BF16, st)
            p.op("pool", lambda e: e.memset(vsa[:, :, 64:65], 1.0), [], [vsa])
            p.op("pool", lambda e: e.memset(vwa[:, :, 64:65], 1.0), [], [vwa])
            Qp = [p.sb("Qp", [128, S], BF16, st) for _ in range(2)]
            pts = [p.sb("pt", [128, 512], BF16, st) for _ in range(3)]
            accs = [p.sb("acc", [128, 4, 4, 64], F32, st) for _ in range(2)]
            nmTs = [p.sb("nmT", [128, 512], BF16, st) for _ in range(2)]
            Pbs = [p.sb("Pb", [128, NCB * 128], BF16, st) for _ in range(2)]
            Pg = p.sb("Pg", [128, NCB * 128], F32, st)
            PTs = [p.sb("PT", [128, NCB, 128], BF16, st) for _ in range(2)]
            PgT = p.sb("PgT", [128, NCB, 128], F32, st)
            sc2 = p.sb("sc2", [128, NB], F32, st)
            sc3 = p.sb("sc3", [128, NB], F32, st)
            m1 = p.sb("m1", [128, 8], F32, st)
            m2 = p.sb("m2", [128, 8], F32, st)
            nm = p.sb("nm", [128, 128], BF16, st)
            p.op("dve", lambda e: e.memset(nm[:], 0.0), [], [nm])
            rrs = [p.sb("rr", [128, 4], F32, st) for _ in range(4)]
            osts = [p.sb("ost", [128, 256], BF16, st) for _ in range(2)]
            cnt = [0]
            for g in range(2):
                for b in Pbs + [Pg]:
                    p.op("dve", lambda e: e.memset(b[:], 0.0), [], [b])
                for half in range(2):
                    hs = slice(half * 64, (half + 1) * 64)
                    self.load(kc2, kc2[hs, :], self.KC[g])
                    self.load(ks2, ks2[hs, :], self.FM[R_KS + g * 64:R_KS + (g + 1) * 64, :])
                    self.load(kw2, kw2[hs, :], self.FM[R_KW + g * 64:R_KW + (g + 1) * 64, :])
                self.load(vc, vc[:], self.VC[g].rearrange("(b p) e -> p b e", p=128))
                self.load_split(vsa, vsa[:, :, 0:64], self.TM[:, g * 64:(g + 1) * 64].rearrange("(t p) e -> p t e", p=128), NT)
                self.load_split(vwa, vwa[:, :, 0:64], self.TM[:, 128 + g * 64:128 + (g + 1) * 64].rearrange("(t p) e -> p t e", p=128), NT)
                for pp in range(2):
                    r0 = R_QN + (g * 4 + 2 * pp) * 64
                    self.load(Qp[pp], Qp[pp][:], self.FM[r0:r0 + 128, :])
                for c in range(NT // 4):
                    qts = list(range(c * 4, c * 4 + 4))
                    q0 = qts[0]
                    acc, nmT = accs[c % 2], nmTs[c % 2]
                    for j, qt in enumerate(qts):
                        lim = min(NC, 8 * qt + 7)
                        nblk = (lim + 127) // 128
                        n_lo = max(0, 8 * qt - 8)
                        n_hi = min(lim, 8 * qt + 7)
                        mm_lo = n_lo - (8 * qt - 8)
                        mm_hi = n_hi - (8 * qt - 8)
                        for hh in range(4):
                            h = g * 4 + hh
                            qs = slice((hh % 2) * 64, (hh % 2) * 64 + 64)
                            Q_ = Qp[hh // 2]
                            ps = self.nextA()
                            self.mm(ps, ps[:, 0:lim], Q_[qs, qt * 128:(qt + 1) * 128], kc2[qs, 0:lim], [Q_, kc2], True, False)
                            self.mm(ps, ps[:, n_lo:n_hi], self.ident_bf[:], self.CBt[:, h, mm_lo:mm_hi], [self.ident_bf, self.CBt], False, True)
                            cnt[0] += 1
                            r = rrs[cnt[0] % 4]
                            Pb = Pbs[cnt[0] % 2]
                            PT = PTs[cnt[0] % 2]
                            p.op("act", lambda e: e.activation(out=Pb[:, 0:lim], in_=ps[:, 0:lim], func=AF.Exp, accum_out=r[:, 0:1]),
                                 [ps], [Pb, r])
                            p.op("dve", lambda e: e.tensor_scalar(out=r[:, 0:1], in0=r[:, 0:1], scalar1=1e-30, scalar2=None, op0=ALU.max), [r], [r])
                            p.op("dve", lambda e: e.reciprocal(out=r[:, 0:1], in_=r[:, 0:1]), [r], [r])
                            if hh == 0:
                                p.op("dve", lambda e: e.tensor_scalar(out=Pg[:, 0:lim], in0=Pb[:, 0:lim], scalar1=r[:, 0:1], scalar2=None,
                                                                      op0=ALU.mult), [Pb, r], [Pg])
                            else:
                                p.op("dve", lambda e: e.scalar_tensor_tensor(out=Pg[:, 0:lim], in0=Pb[:, 0:lim], scalar=r[:, 0:1],
                                                                             in1=Pg[:, 0:lim], op0=ALU.mult, op1=ALU.add), [Pb, r, Pg], [Pg])
                            pb = self.nextB()
                            pbv = pb.t[:].bitcast(BF16)
                            for nb in range(nblk):
                                self.tr(pb, pbv[:, nb * 128:(nb + 1) * 128], Pb[:, nb * 128:(nb + 1) * 128], [Pb])
                            self.evac(PT[:, 0:nblk, :], pbv[:, 0:nblk * 128].rearrange("p (b t) -> p b t", b=nblk), PT, pb)
                            self.iO += 1
                            so = self.psO[self.iO % 4]
                            for nb in range(nblk):
                                self.mm(so, so[:, 0:64], PT[:, nb, :], vc[:, nb, :], [PT, vc], nb == 0, nb == nblk - 1)
                            p.op("dve", lambda e: e.tensor_tensor(out=r[:, 1:2], in0=r[:, 0:1], in1=gate[:, qt, h * 3:h * 3 + 1], op=ALU.mult),
                                 [r, gate], [r])
                            p.op("dve", lambda e: e.tensor_scalar(out=acc[:, hh, j, :], in0=so[:, 0:64], scalar1=r[:, 1:2], scalar2=None,
                                                                  op0=ALU.mult), [so, r], [acc])
                        pb = self.nextB()
                        for nb in range(nblk):
                            self.tr(pb, pb[:, nb * 128:(nb + 1) * 128], Pg[:, nb * 128:(nb + 1) * 128], [Pg], f32=True)
                        self.evac(PgT[:, 0:nblk, :], pb[:, 0:nblk * 128].rearrange("p (b t) -> p b t", b=nblk), PgT, pb)
                        ps = self.nextA()
                        for nb in range(nblk):
                            self.mm(ps, ps[:, 0:NB], PgT[:, nb, :], self.imp[:, nb * NB:(nb + 1) * NB], [PgT, self.imp], nb == 0, nb == nblk - 1)
                        o_ = 128 - 2 * qt
                        p.op("dve", lambda e: e.tensor_tensor(out=sc2[:], in0=ps[:, 0:NB], in1=self.sela[:, o_:o_ + NB], op=ALU.mult),
                             [ps, self.sela], [sc2])
                        p.op("dve", lambda e: e.tensor_tensor(out=sc2[:], in0=sc2[:], in1=self.selb[:, o_:o_ + NB], op=ALU.add),
                             [sc2, self.selb], [sc2])
                        p.op("dve", lambda e: e.memset(sc2[:, 0:1], 3e9), [], [sc2])
                        p.op("dve", lambda e: e.max(out=m1[:], in_=sc2[:]), [sc2], [m1])
                        p.op("dve", lambda e: e.match_replace(out=sc3[:], in_to_replace=m1[:], in_values=sc2[:], imm_value=-3e9),
                             [m1, sc2], [sc3])
                        p.op("dve", lambda e: e.max(out=m2[:], in_=sc3[:]), [sc3], [m2])
                        p.op("dve", lambda e: e.tensor_scalar(out=nm[:, 0:NB], in0=sc2[:], scalar1=m2[:, 7:8], scalar2=NEG,
                                                              op0=ALU.is_lt, op1=ALU.mult), [sc2, m2], [nm])
                        pb = self.nextB()
                        pbv = pb.t[:].bitcast(BF16)
                        self.tr(pb, pbv[:, 0:128], nm[:, :], [nm])
                        self.evac(nmT[:, j * 128:(j + 1) * 128], pbv[:, 0:128], nmT, pb)
                    for br in range(2):
                        for hh in range(4):
                            h = g * 4 + hh
                            qs = slice((hh % 2) * 64, (hh % 2) * 64 + 64)
                            Q_ = Qp[hh // 2]
                            K_ = ks2 if br == 0 else kw2
                            V_ = vsa if br == 0 else vwa

                            def special(qt, kt, h=h, br=br):
                                ib = self.ident_bf
                                if kt == qt:
                                    return [(ib[:], self.Tt[:, h, 0, :], [ib, self.Tt])]
                                if kt == qt - 1:
                                    return [(ib[:], self.Tt[:, h, 1, :], [ib, self.Tt])]
                                if br == 1 and kt == qt - 4:
                                    return [(ib[:], self.tw4[:], [ib, self.tw4])]
                                return []

                            def cextra(kt, qa, qb, nmT=nmT, q0=q0):
                                return [(esel[:, kt * 128:(kt + 1) * 128], nmT[:, (qa - q0) * 128:(qb - q0) * 128], [esel, nmT])]

                            def done(qt, so, h=h, hh=hh, br=br, acc=acc, q0=q0):
                                j = qt - q0
                                cnt[0] += 1
                                r = rrs[cnt[0] % 4]
                                p.op("dve", lambda e: e.reciprocal(out=r[:, 0:1], in_=so[:, 64:65]), [so], [r])
                                p.op("dve", lambda e: e.tensor_tensor(out=r[:, 1:2], in0=r[:, 0:1], in1=gate[:, qt, h * 3 + 1 + br:h * 3 + 2 + br],
                                                                      op=ALU.mult), [r, gate], [r])
                                p.op("dve", lambda e: e.scalar_tensor_tensor(out=acc[:, hh, j, :], in0=so[:, 0:64], scalar=r[:, 1:2],
                                                                             in1=acc[:, hh, j, :], op0=ALU.mult, op1=ALU.add),
                                     [so, r, acc], [acc])

                            lo = (lambda qt: 0) if br == 0 else (lambda qt: max(0, qt - 4))
                            self.attn(qts, lo, lambda qt: qt,
                                      lambda a, b, Q_=Q_, qs=qs: (Q_[qs, a * 128:b * 128], [Q_]),
                                      lambda kt, K_=K_, qs=qs: (K_[qs, kt * 128:(kt + 1) * 128], [K_]),
                                      lambda kt, V_=V_: (V_[:, kt, :], [V_]),
                                      64, special, cextra if br == 0 else None, done, pts)
                    for j, qt in enumerate(qts):
                        ot = osts[qt % 2]
                        p.op("pool", lambda e: e.tensor_copy(out=ot[:].rearrange("p (h e) -> p h e", h=4), in_=acc[:, :, j, :]), [acc], [ot])
                        self.store(ot, self.OS[qt * 128:(qt + 1) * 128, g * 256:(g + 1) * 256], ot[:])
            p.end_stage()

    def stage_dsa(self, l):
        p, d, S, NT = self.p, self.d, self.S, self.NT
        i = l // 2
        NIT = 10
        p.begin_stage()
        with ExitStack() as st:
            ckvT = p.sb("ckvT", [128, S], BF16, st)
            self.load(ckvT, ckvT[:], self.FM[R_CKV:R_CKV + 128, :])
            ckva = p.sb("ckva", [128, NT, 129], BF16, st)
            p.op("pool", lambda e: e.memset(ckva[:, :, 128:129], 1.0), [], [ckva])
            self.load_split(ckva, ckva[:, :, 0:128], self.TM[:, 256:384].rearrange("(t p) e -> p t e", p=128), NT)
            ik2 = p.sb("ik2", [128, S], BF16, st)
            for half in range(2):
                self.load(ik2, ik2[half * 64:(half + 1) * 64, :], self.FM[R_IK:R_IK + 64, :])
            wuv = p.sb("wuv", [128, 512], BF16, st)
            self.load(wuv, wuv[:], d["dsa_w_uv"][i].rearrange("r h e -> r (h e)"), q="pool")
            iw = self.iw_all
            isc = p.sb("isc", [128, S], F32, st)
            nm = p.sb("nm", [128, S], BF16, st)
            nmT = p.sb("nmT", [128, NT, 256], BF16, st)
            iqs = [p.sb("iq", [128, 4, 256], BF16, st) for _ in range(2)]
            qls = [p.sb("ql", [128, 8, 256], BF16, st) for _ in range(2)]
            rls = [p.sb("rl", [128, 512], F32, st) for _ in range(2)]
            pts = [p.sb("pt", [128, 256], BF16, st) for _ in range(3)]
            sml = p.sb("sml", [128, 8], F32, st)
            thrc = p.sb("thrc", [128, 1], F32, st)
            p.op("dve", lambda e: e.memset(thrc[:], -1e29), [], [thrc])
            rrs = [p.sb("rr", [128, 2], F32, st) for _ in range(4)]
            ols = [p.sb("ol", [128, 128], BF16, st) for _ in range(2)]
            olTs = [p.sb("olT", [128, 128], BF16, st) for _ in range(2)]
            osts = [p.sb("ost", [128, 2, 512], BF16, st) for _ in range(2)]
            cnt = [0]
            for c2 in range(NT // 2):
                qts = [2 * c2, 2 * c2 + 1]
                q0 = qts[0]
                cols = slice(q0 * 128, q0 * 128 + 256)
                iq, ql, ost = iqs[c2 % 2], qls[c2 % 2], osts[c2 % 2]
                self.load(iq, iq[:], self.FM[R_IQ:R_IQ + 512, cols].rearrange("(a p) t -> p a t", p=128))
                self.load(ql, ql[:], self.FM[R_QLAT:R_QLAT + 1024, cols].rearrange("(h p) t -> p h t", p=128))
                for j, qt in enumerate(qts):
                    L = (qt + 1) * 128
                    for c0 in range(0, L, 512):
                        c1 = min(L, c0 + 512)
                        w = c1 - c0
                        for jh in range(8):
                            hs = slice((jh % 2) * 64, (jh % 2) * 64 + 64)
                            ps = self.nextA()
                            self.mm(ps, ps[:, 0:w], iq[hs, jh // 2, j * 128:(j + 1) * 128], ik2[hs, c0:c1], [iq, ik2], True, True)
                            cnt[0] += 1
                            rl = rls[cnt[0] % 2]
                            p.op("act", lambda e: e.activation(out=rl[:, 0:w], in_=ps[:, 0:w], func=AF.Relu), [ps], [rl])
                            if jh == 0:
                                p.op("dve", lambda e: e.tensor_scalar(out=isc[:, c0:c1], in0=rl[:, 0:w], scalar1=iw[:, qt, 0:1], scalar2=None,
                                                                      op0=ALU.mult), [rl, iw], [isc])
                            else:
                                p.op("dve", lambda e: e.scalar_tensor_tensor(out=isc[:, c0:c1], in0=rl[:, 0:w], scalar=iw[:, qt, jh:jh + 1],
                                                                             in1=isc[:, c0:c1], op0=ALU.mult, op1=ALU.add), [rl, iw, isc], [isc])
                    if qt >= 2:
                        p.op("dve", lambda e: e.tensor_reduce(out=sml[:, 0:1], in_=isc[:, 0:L], axis=AX.X, op=ALU.max), [isc], [sml])
                        p.op("dve", lambda e: e.tensor_reduce(out=sml[:, 1:2], in_=isc[:, 0:L], axis=AX.X, op=ALU.min), [isc], [sml])
                    p.op("dve", lambda e: e.tensor_tensor(out=isc[:, qt * 128:L], in0=isc[:, qt * 128:L], in1=self.tri[:], op=ALU.add),
                         [isc, self.tri], [isc])
                    if qt >= 2:
                        p.op("dve", lambda e: e.tensor_copy(out=sml[:, 2:3], in_=sml[:, 1:2]), [sml], [sml])
                        p.op("dve", lambda e: e.tensor_tensor(out=sml[:, 3:4], in0=sml[:, 0:1], in1=sml[:, 1:2], op=ALU.subtract), [sml], [sml])
                        p.op("dve", lambda e: e.tensor_scalar(out=sml[:, 3:4], in0=sml[:, 3:4], scalar1=1.0001, scalar2=1e-6,
                                                              op0=ALU.mult, op1=ALU.add), [sml], [sml])
                        for k in range(NIT):
                            f = 2.0 ** -(k + 1)
                            p.op("dve", lambda e: e.tensor_scalar(out=sml[:, 4:5], in0=sml[:, 3:4], scalar1=f, scalar2=None, op0=ALU.mult), [sml], [sml])
                            p.op("dve", lambda e: e.tensor_tensor(out=sml[:, 5:6], in0=sml[:, 2:3], in1=sml[:, 4:5], op=ALU.add), [sml], [sml])
                            p.op("dve", lambda e: e.tensor_scalar(out=nm[:, 0:L], in0=isc[:, 0:L], scalar1=sml[:, 5:6], scalar2=None,
                                                                  op0=ALU.is_ge, op1=ALU.add, accum_out=sml[:, 6:7]), [isc, sml], [nm, sml])
                            p.op("dve", lambda e: e.scalar_tensor_tensor(out=sml[:, 7:8], in0=sml[:, 6:7], scalar=255.5, in1=sml[:, 4:5],
                                                                         op0=ALU.is_ge, op1=ALU.mult), [sml], [sml])
                            p.op("dve", lambda e: e.tensor_tensor(out=sml[:, 2:3], in0=sml[:, 2:3], in1=sml[:, 7:8], op=ALU.add), [sml], [sml])
                        thr = sml[:, 2:3]
                        tb = sml
                    else:
                        thr = thrc[:, 0:1]
                        tb = thrc
                    p.op("dve", lambda e: e.tensor_scalar(out=nm[:, 0:L], in0=isc[:, 0:L], scalar1=thr, scalar2=NEG, op0=ALU.is_lt, op1=ALU.mult),
                         [isc, tb], [nm])
                    for k0 in range(0, qt + 1, 8):
                        k1 = min(qt + 1, k0 + 8)
                        pb = self.nextB()
                        pbv = pb.t[:].bitcast(BF16)
                        for kt in range(k0, k1):
                            self.tr(pb, pbv[:, (kt - k0) * 128:(kt - k0 + 1) * 128], nm[:, kt * 128:(kt + 1) * 128], [nm])
                        self.evac(nmT[:, k0:k1, j * 128:(j + 1) * 128],
                                  pbv[:, 0:(k1 - k0) * 128].rearrange("p (k t) -> p k t", k=k1 - k0), nmT, pb)
                for h in range(8):
                    col = 8 + h

                    def special(qt, kt, col=col):
                        ib = self.ident_bf
                        if kt == qt:
                            return [(ib[:], self.Tt[:, col, 0, :], [ib, self.Tt])]
                        if kt == qt - 1:
                            return [(ib[:], self.Tt[:, col, 1, :], [ib, self.Tt])]
                        return []

                    def cextra(kt, qa, qb, q0=q0):
                        return [(self.ident_bf[:], nmT[:, kt, (qa - q0) * 128:(qb - q0) * 128], [self.ident_bf, nmT])]

                    def done(qt, so, h=h, q0=q0, ost=ost):
                        j = qt - q0
                        cnt[0] += 1
                        r, ol, olT = rrs[cnt[0] % 4], ols[cnt[0] % 2], olTs[cnt[0] % 2]
                        p.op("dve", lambda e: e.reciprocal(out=r[:, 0:1], in_=so[:, 128:129]), [so], [r])
                        p.op("dve", lambda e: e.tensor_scalar(out=ol[:], in0=so[:, 0:128], scalar1=r[:, 0:1], scalar2=None, op0=ALU.mult),
                             [so, r], [ol])
                        pb = self.nextB()
                        pbv = pb.t[:].bitcast(BF16)
                        self.tr(pb, pbv[:, 0:128], ol[:], [ol])
                        self.evac(olT[:], pbv[:, 0:128], olT, pb)
                        pb2 = self.nextB()
                        self.mm(pb2, pb2[:, 0:64], olT[:], wuv[:, h * 64:(h + 1) * 64], [olT, wuv], True, True)
                        self.evac(ost[:, j, h * 64:(h + 1) * 64], pb2[:, 0:64], ost, pb2)

                    self.attn(qts, lambda qt: 0, lambda qt: qt,
                              lambda a, b, h=h, ql=ql, q0=q0: (ql[:, h, (a - q0) * 128:(b - q0) * 128], [ql]),
                              lambda kt: (ckvT[:, kt * 128:(kt + 1) * 128], [ckvT]),
                              lambda kt: (ckva[:, kt, :], [ckva]),
                              128, special, cextra, done, pts)
                for j, qt in enumerate(qts):
                    self.store(ost, self.OS[qt * 128:(qt + 1) * 128, 512:1024], ost[:, j, :])
            p.end_stage()


_CACHE = {}


def _get_prog(S, layers, shapes):
    key = (S, tuple(layers))
    if key not in _CACHE:
        kb = KB(S, layers, shapes)
        kb.build()
        _CACHE[key] = kb
    return _CACHE[key]


def run_kernel(inputs, S, layers, nb):
    shapes = {n: np.asarray(inputs[n]).shape for n in W_NAMES}
    kb = _get_prog(S, layers, shapes)
    base = {n: np.ascontiguousarray(np.asarray(inputs[n], dtype=np.float32)) for n in W_NAMES}
    base.update(kb.consts)
    x = np.asarray(inputs["x"], dtype=np.float32)
    c = np.asarray(inputs["c"], dtype=np.float32)
    ncore = 2 * nb
    in_maps = []
    for k in range(ncore):
        b = k // 2
        m = dict(base)
        m["x"] = np.ascontiguousarray(x[b])
        m["c"] = np.ascontiguousarray(c[b:b + 1])
        in_maps.append(m)
    res = run_bass_kernel_spmd(kb.nc, in_maps, core_ids=list(range(ncore)))
    return np.stack([np.asarray(res.results[2 * b]["y"]) for b in range(nb)], axis=0).astype(np.float32)


def kernel(**inputs):
    return run_kernel(inputs, 8192, [0, 1, 2, 3], 4)
```
